# Optimizing a Trainium2 kernel written in Bass

```python
import jax, jax.numpy as jnp
from jax import lax
import numpy as np

D_MODEL = 2048
BATCH = 4
SEQ = 4096
DEPTH = 4
DEC_BATCH = 4
DEC_SEQ = 8192
PAST_LEN = 128

HEAD_DIM = 128
DILATIONS = ((128, 1), (512, 4), (2048, 16))
A_HEADS_PER_GROUP = 4
A_HEADS = A_HEADS_PER_GROUP * len(DILATIONS)
B_Q_HEADS = D_MODEL // 256
B_KV_HEADS = 2
B_HALF_WINDOW = 128
B_BLOCK = 128
C_HEADS = D_MODEL // HEAD_DIM
GRID_W = 64
NA_ROWS = 8
NA_COLS = 16
NUM_BUCKETS = 32
MAX_DISTANCE = 1024
D_FF = -(-8 * D_MODEL // 768) * 256
RMS_EPS = 1e-6
A_IN = len(DILATIONS) * 3 * A_HEADS_PER_GROUP * HEAD_DIM
B_IN = (B_Q_HEADS + 2 * B_KV_HEADS) * HEAD_DIM
AB_IN = A_IN + B_IN
AB_OUT = (A_HEADS_PER_GROUP + B_Q_HEADS) * HEAD_DIM
C_IN = 3 * C_HEADS * HEAD_DIM
C_OUT = C_HEADS * HEAD_DIM
SCALE = HEAD_DIM ** -0.5
NEG_INF = -1e30

kernel_name = "hybrid_dilated_window_neighbourhood_encoder"


def rms_norm(x, g):
    xf = x.astype(jnp.float32)
    y = xf * lax.rsqrt(jnp.mean(xf * xf, axis=-1, keepdims=True) + RMS_EPS)
    return (y * g.astype(jnp.float32)).astype(x.dtype)


def t5_bucket(rel):
    nb = NUM_BUCKETS // 2
    max_exact = nb // 2
    ret = (rel > 0).astype(np.int32) * nb
    n = np.abs(rel)
    large = max_exact + (np.log(np.maximum(n, 1) / max_exact) / np.log(MAX_DISTANCE / max_exact)
                         * (nb - max_exact)).astype(np.int32)
    large = np.minimum(large, nb - 1)
    return (ret + np.where(n < max_exact, n, large)).astype(np.int32)


def t5_bias(table_cols, blk, d):
    rel = (np.arange(3 * blk)[None, :] - blk - np.arange(blk)[:, None]) * d
    return table_cols[t5_bucket(rel)].transpose(2, 0, 1).astype(jnp.float32)


def banded_attention(q, k, v, half_w, blk, bias, sink=None):
    n, L, H, dh = q.shape
    G = k.shape[2]
    rep = H // G
    nb = -(-L // blk)
    Lp = nb * blk
    qb = jnp.pad(q, ((0, 0), (0, Lp - L), (0, 0), (0, 0))).reshape(n, nb, blk, G, rep, dh)

    def windows(t):
        tp = jnp.pad(t, ((0, 0), (blk, Lp - L + blk), (0, 0), (0, 0))).reshape(n, nb + 2, blk, G, dh)
        return jnp.concatenate([tp[:, :-2], tp[:, 1:-1], tp[:, 2:]], axis=2)

    kb, vb = windows(k), windows(v)
    qpos = np.arange(Lp).reshape(nb, blk)
    kpos = np.arange(-blk, Lp + blk).reshape(nb + 2, blk)
    kpos = np.concatenate([kpos[:-2], kpos[1:-1], kpos[2:]], axis=1)
    rel = kpos[:, None, :] - qpos[:, :, None]
    mask = (np.abs(rel) <= half_w) & (kpos[:, None, :] >= 0) & (kpos[:, None, :] < L)

    s = jnp.einsum('nbqgrd,nbkgd->nbgrqk', qb, kb, preferred_element_type=jnp.float32) * SCALE
    s = s + bias.reshape(G, rep, blk, 3 * blk)[None, None]
    s = jnp.where(mask[None, :, None, None], s, NEG_INF)
    m = jnp.max(s, axis=-1)
    if sink is not None:
        sk = sink.astype(jnp.float32).reshape(G, rep, 1)
        m = jnp.maximum(m, sk)
    p = jnp.exp(s - m[..., None])
    den = jnp.sum(p, axis=-1)
    if sink is not None:
        den = den + jnp.exp(sk - m)
    o = jnp.einsum('nbgrqk,nbkgd->nbqgrd', p.astype(v.dtype), vb, preferred_element_type=jnp.float32)
    den_t = jnp.moveaxis(den, -1, 2)
    lse = jnp.moveaxis(m, -1, 2) + jnp.log(den_t)
    o = (o / den_t[..., None]).reshape(n, Lp, H, dh)[:, :L]
    lse = lse.reshape(n, Lp, H)[:, :L]
    return o, lse


def dilated_attention(q, k, v, d, half, bias):
    Bn, T, H, dh = q.shape

    def split(t):
        return t.reshape(Bn, T // d, d, H, dh).transpose(0, 2, 1, 3, 4).reshape(Bn * d, T // d, H, dh)

    o, lse = banded_attention(split(q), split(k), split(v), half, half, bias)
    o = o.reshape(Bn, d, T // d, H, dh).transpose(0, 2, 1, 3, 4).reshape(Bn, T, H, dh)
    lse = lse.reshape(Bn, d, T // d, H).transpose(0, 2, 1, 3).reshape(Bn, T, H)
    return o, lse


def mixer_ab(h, w_in, w_out, sink, t5_table):
    Bn, T, _ = h.shape
    proj = h @ w_in
    a = proj[..., :A_IN].reshape(Bn, T, len(DILATIONS), 3, A_HEADS_PER_GROUP, HEAD_DIM)
    b = proj[..., A_IN:]
    nq = B_Q_HEADS * HEAD_DIM
    nkv = B_KV_HEADS * HEAD_DIM
    bq = b[..., :nq].reshape(Bn, T, B_Q_HEADS, HEAD_DIM)
    bk = b[..., nq:nq + nkv].reshape(Bn, T, B_KV_HEADS, HEAD_DIM)
    bv = b[..., nq + nkv:].reshape(Bn, T, B_KV_HEADS, HEAD_DIM)
    outs, lses = [], []
    for g, (w, d) in enumerate(DILATIONS):
        half = w // (2 * d)
        cols = t5_table[:, g * A_HEADS_PER_GROUP:(g + 1) * A_HEADS_PER_GROUP]
        o, l = dilated_attention(a[:, :, g, 0], a[:, :, g, 1], a[:, :, g, 2], d, half, t5_bias(cols, half, d))
        outs.append(o)
        lses.append(l)
    wts = jax.nn.softmax(jnp.stack(lses), axis=0)
    o_a = jnp.sum(wts[..., None] * jnp.stack(outs), axis=0)
    o_b, _ = banded_attention(bq, bk, bv, B_HALF_WINDOW, B_BLOCK,
                              t5_bias(t5_table[:, A_HEADS:], B_BLOCK, 1), sink)
    mixed = jnp.concatenate([o_a.reshape(Bn, T, -1), o_b.reshape(Bn, T, -1)], axis=-1).astype(h.dtype)
    return mixed @ w_out


def neighbourhood_attention(q, k, v, rpb):
    Bn, T, H, dh = q.shape
    rows = T // GRID_W
    kh = min(NA_ROWS, rows)
    r = np.arange(rows)
    rs = np.clip(r - kh // 2, 0, rows - kh)
    key_rows = rs[:, None] + np.arange(kh)[None, :]
    c = np.arange(GRID_W)
    cs = np.clip(c - NA_COLS // 2, 0, GRID_W - NA_COLS)
    col_mask = (c[None, :] >= cs[:, None]) & (c[None, :] < cs[:, None] + NA_COLS)
    dr = key_rows - r[:, None] + NA_ROWS - 1
    dc = np.clip(c[None, :] - c[:, None] + NA_COLS - 1, 0, 2 * NA_COLS - 2)
    bias = rpb[:, dr[:, None, :, None], dc[None, :, None, :]].astype(jnp.float32)
    bias = bias.transpose(1, 0, 2, 3, 4)
    qg = q.reshape(Bn, rows, GRID_W, H, dh)
    kg = k.reshape(Bn, rows, GRID_W, H, dh)[:, key_rows]
    vg = v.reshape(Bn, rows, GRID_W, H, dh)[:, key_rows]
    s = jnp.einsum('brqhd,brikhd->brhqik', qg, kg, preferred_element_type=jnp.float32) * SCALE + bias[None]
    s = jnp.where(col_mask[:, None, :], s, NEG_INF)
    p = jax.nn.softmax(s.reshape(Bn, rows, H, GRID_W, kh * GRID_W), axis=-1).reshape(s.shape)
    o = jnp.einsum('brhqik,brikhd->brqhd', p.astype(v.dtype), vg, preferred_element_type=jnp.float32)
    return o.reshape(Bn, T, H, dh)


def mixer_c(h, w_in, w_out, rpb):
    Bn, T, _ = h.shape
    qkv = (h @ w_in).reshape(Bn, T, 3, C_HEADS, HEAD_DIM)
    o = neighbourhood_attention(qkv[:, :, 0], qkv[:, :, 1], qkv[:, :, 2], rpb)
    return o.reshape(Bn, T, C_OUT).astype(h.dtype) @ w_out


def swiglu(h, w_gate, w_up, w_down):
    return (jax.nn.silu(h @ w_gate) * (h @ w_up)) @ w_down


def trunk(x, w_in_ab, w_out_ab, sink_b, w_in_c, w_out_c, rpb_c, t5_table,
          norm_mix, norm_ffn, w_gate, w_up, w_down, norm_final):
    for layer in range(DEPTH):
        j = layer // 2
        h = rms_norm(x, norm_mix[layer])
        if layer % 2 == 0:
            x = x + mixer_ab(h, w_in_ab[j], w_out_ab[j], sink_b[j], t5_table)
        else:
            x = x + mixer_c(h, w_in_c[j], w_out_c[j], rpb_c[j])
        x = x + swiglu(rms_norm(x, norm_ffn[layer]), w_gate[layer], w_up[layer], w_down[layer])
    return rms_norm(x, norm_final)


def setup_inputs(seed: int = 0) -> dict:
    key = jax.random.key(seed)
    ks = jax.random.split(key, 17)
    n_even = (DEPTH + 1) // 2
    n_odd = DEPTH // 2
    nrm = jax.random.normal
    f32 = jnp.float32
    return {
        'x_prompt': nrm(ks[0], (BATCH, SEQ, D_MODEL), f32),
        'x_sample': nrm(ks[1], (DEC_BATCH, DEC_SEQ, D_MODEL), f32),
        'w_in_ab': nrm(ks[2], (n_even, D_MODEL, AB_IN), f32) * D_MODEL ** -0.5,
        'w_out_ab': nrm(ks[3], (n_even, AB_OUT, D_MODEL), f32) * AB_OUT ** -0.5,
        'sink_b': 0.5 * nrm(ks[4], (n_even, B_Q_HEADS), f32),
        'w_in_c': nrm(ks[5], (n_odd, D_MODEL, C_IN), f32) * D_MODEL ** -0.5,
        'w_out_c': nrm(ks[6], (n_odd, C_OUT, D_MODEL), f32) * C_OUT ** -0.5,
        'rpb_c': 0.2 * nrm(ks[7], (n_odd, C_HEADS, 2 * NA_ROWS - 1, 2 * NA_COLS - 1), f32),
        't5_table': 0.2 * nrm(ks[8], (NUM_BUCKETS, A_HEADS + B_Q_HEADS), f32),
        'norm_mix': 1.0 + 0.05 * nrm(ks[9], (DEPTH, D_MODEL), f32),
        'norm_ffn': 1.0 + 0.05 * nrm(ks[10], (DEPTH, D_MODEL), f32),
        'w_gate': nrm(ks[11], (DEPTH, D_MODEL, D_FF), f32) * D_MODEL ** -0.5,
        'w_up': nrm(ks[12], (DEPTH, D_MODEL, D_FF), f32) * D_MODEL ** -0.5,
        'w_down': nrm(ks[13], (DEPTH, D_FF, D_MODEL), f32) * D_FF ** -0.5,
        'norm_final': 1.0 + 0.05 * nrm(ks[14], (D_MODEL,), f32),
    }


def reference(x_prompt, x_sample, w_in_ab, w_out_ab, sink_b, w_in_c, w_out_c, rpb_c, t5_table,
              norm_mix, norm_ffn, w_gate, w_up, w_down, norm_final):
    y_prompt = trunk(x_prompt, w_in_ab, w_out_ab, sink_b, w_in_c, w_out_c, rpb_c, t5_table,
                     norm_mix, norm_ffn, w_gate, w_up, w_down, norm_final)
    y_sample = trunk(x_sample, w_in_ab, w_out_ab, sink_b, w_in_c, w_out_c, rpb_c, t5_table,
                     norm_mix, norm_ffn, w_gate, w_up, w_down, norm_final)
    return (y_prompt, y_sample)
```

```python
import contextlib
import numpy as np
import concourse.bass as bass
import concourse.mybir as mybir
from concourse.bass_utils import run_bass_kernel_spmd

F32 = mybir.dt.float32
BF16 = mybir.dt.bfloat16
AF = mybir.ActivationFunctionType
ALU = mybir.AluOpType
AX = mybir.AxisListType

D = 2048
KC = 16
DFF = 5632
SCALE = 128 ** -0.5
NEG = -1e30
DIL = (1, 4, 16)

COMPUTE = ("pe", "act", "dve", "pool")
QUEUES = {"sp": 8, "poolq": 8}
QUEUE_ENGINE = {"sp": "sp", "poolq": "pool"}


class Buf:
    __slots__ = ("name", "w", "rs")

    def __init__(self, name=""):
        self.name = name
        self.w = None
        self.rs = []


class Prog:
    def __init__(self):
        self.streams = {e: [] for e in ("pe", "act", "dve", "pool", "sp")}
        self.cnt = {e: 0 for e in COMPUTE}
        self.dcnt = {q: 0 for q in QUEUES}
        self.seen = {}

    def _waits_for(self, stream, toks):
        waits = {}
        for t in toks:
            if t is None:
                continue
            if t[0] == "c":
                _, eng, seq = t
                if eng == "pe" and stream == "pe":
                    continue
                key = ("c", eng)
                val = seq
            else:
                _, q, n = t
                k = QUEUES[q]
                key = ("d", q, n % k)
                val = 16 * (n // k + 1)
            if self.seen.get((stream, key), 0) >= val:
                continue
            if waits.get(key, 0) < val:
                waits[key] = val
        for key, val in waits.items():
            self.seen[(stream, key)] = val
        return list(waits.items())

    @staticmethod
    def _deps(reads, writes):
        toks = []
        for b in reads:
            toks.append(b.w)
        for b in writes:
            toks.append(b.w)
            toks.extend(b.rs)
        return toks

    @staticmethod
    def _mark(tok, reads, writes):
        for b in reads:
            b.rs.append(tok)
            if len(b.rs) > 64:
                last = {}
                for t in b.rs:
                    key = (t[0], t[1]) if t[0] == "c" else (t[0], t[1], t[2] % QUEUES[t[1]])
                    if key not in last or last[key][2] < t[2]:
                        last[key] = t
                b.rs = list(last.values())
        for b in writes:
            b.w = tok
            b.rs = []

    def op(self, eng, fn, reads=(), writes=()):
        waits = self._waits_for(eng, self._deps(reads, writes))
        self.cnt[eng] += 1
        tok = ("c", eng, self.cnt[eng])
        self.streams[eng].append((fn, waits, ("c", eng)))
        self._mark(tok, reads, writes)
        return tok

    def dma(self, q, fn, reads=(), writes=()):
        stream = QUEUE_ENGINE[q]
        n = self.dcnt[q]
        k = QUEUES[q]
        toks = self._deps(reads, writes)
        if n >= k:
            toks.append(("d", q, n - k))
        waits = self._waits_for(stream, toks)
        self.dcnt[q] += 1
        tok = ("d", q, n)
        self.streams[stream].append((fn, waits, ("d", q, n % k)))
        self._mark(tok, reads, writes)
        return tok

    def barrier(self, queues=("sp",)):
        toks = [("c", e, self.cnt[e]) for e in COMPUTE if self.cnt[e] > 0]
        for q in queues:
            n = self.dcnt[q]
            for i in range(max(0, n - QUEUES[q]), n):
                toks.append(("d", q, i))
        for stream in self.streams:
            waits = self._waits_for(stream, toks)
            if waits:
                self.streams[stream].append((None, waits, None))

    def emit(self, nc):
        with contextlib.ExitStack() as es:
            sems = {}
            for e in COMPUTE:
                sems[("c", e)] = es.enter_context(nc.semaphore("s_" + e))
            for q, k in QUEUES.items():
                for i in range(k):
                    sems[("d", q, i)] = es.enter_context(nc.semaphore("s_%s%d" % (q, i)))
            block = es.enter_context(nc.Block())
            streams = self.streams

            def run(engine, ops):
                for fn, waits, inc in ops:
                    for key, val in waits:
                        engine.wait_ge(sems[key], val)
                    if fn is None:
                        continue
                    ins = fn(engine)
                    ins.then_inc(sems[inc], 1 if inc[0] == "c" else 16)

            @block.tensor
            def _(e):
                run(e, streams["pe"])

            @block.scalar
            def _(e):
                run(e, streams["act"])

            @block.vector
            def _(e):
                run(e, streams["dve"])

            @block.gpsimd
            def _(e):
                run(e, streams["pool"])

            @block.sync
            def _(e):
                run(e, streams["sp"])


ARENA_WORDS = 53000


class Arena:
    def __init__(self, t):
        self.t = t
        self.off = 0

    def get(self, parts, free, dt):
        n = int(np.prod(free))
        words = n if dt == F32 else (n + 1) // 2
        ap = self.t[0:parts, self.off:self.off + words]
        self.off += words
        assert self.off <= ARENA_WORDS, ("arena overflow", self.off)
        if dt == BF16:
            ap = ap.bitcast(BF16)[:, 0:n]
        if len(free) == 2:
            ap = ap.rearrange("p (a b) -> p a b", a=free[0])
        elif len(free) == 3:
            ap = ap.rearrange("p (a b c) -> p a b c", a=free[0], b=free[1])
        return ap


def build(T, depth, stop=None):
    NB = T // 128
    HALF = NB // 2
    nc = bass.Bass("TRN2", target_bir_lowering=False)

    def din(name, shape, dt=F32):
        return nc.dram_tensor(name, list(shape), dt, kind="ExternalInput").ap()

    def dscr(name, shape, dt):
        return nc.dram_tensor(name, list(shape), dt, kind="Internal").ap()

    x_in = din("x", [T, D])
    W = {
        "w_in_ab": din("w_in_ab", [2, D, 6144]), "w_out_ab": din("w_out_ab", [2, 1536, D]),
        "w_in_c": din("w_in_c", [2, D, 6144]), "w_out_c": din("w_out_c", [2, D, D]),
        "w_gate": din("w_gate", [4, D, DFF]), "w_up": din("w_up", [4, D, DFF]),
        "w_down": din("w_down", [4, DFF, D]),
    }
    norm_mix = din("norm_mix", [4, D])
    norm_ffn = din("norm_ffn", [4, D])
    norm_final = din("norm_final", [1, D])
    sink_b = din("sink_b", [2, 8])
    biasab_in = din("biasab", [128, 20 * 384])
    biasc_in = din("biasc", [2, 128, 16 * 9 * 128])
    midab_in = din("midab", [1, 128])
    midc_in = din("midc", [2, 28 * 128])
    consts_in = din("consts", [128, 3 * 128])
    kyind_in = din("kyind", [2, 128])
    y_out = nc.dram_tensor("y", [T, D], F32, kind="ExternalOutput").ap()

    Wb = {k: dscr("b_" + k, v.shape, BF16) for k, v in W.items()}
    QKs = dscr("qks", [34, 128, T], BF16)
    Vs = dscr("vs", [T, 2048], BF16)
    XA = dscr("xa", [T, D], F32)
    XB = dscr("xb", [T, D], F32)

    P = Prog()
    es = contextlib.ExitStack()
    arena_t = es.enter_context(nc.sbuf_tensor("arena", [128, ARENA_WORDS], F32))
    ar = Arena(arena_t)
    A0 = es.enter_context(nc.psum_tensor("pa0", [128, 1024], F32))
    A1 = es.enter_context(nc.psum_tensor("pa1", [128, 1024], F32))
    B0 = es.enter_context(nc.psum_tensor("pb0", [128, 512], F32))
    B1 = es.enter_context(nc.psum_tensor("pb1", [128, 512], F32))
    TPa = es.enter_context(nc.psum_tensor("ptpa", [128, 1024], BF16))
    TPb = es.enter_context(nc.psum_tensor("ptpb", [128, 1024], BF16))
    bA0l, bA0h, bA1l, bA1h, bB0, bB1, bTP0, bTP1 = [Buf() for _ in range(8)]
    banks = [(A0[:, 0:512], bA0l), (A0[:, 512:1024], bA0h), (A1[:, 0:512], bA1l),
             (A1[:, 512:1024], bA1h), (B0[:, :], bB0), (B1[:, :], bB1)]
    bank_ctr = [0]

    def next_bank():
        b = banks[bank_ctr[0] % 6]
        bank_ctr[0] += 1
        return b

    TPv = [(TPa[:, :], bTP0), (TPb[:, :], bTP1)]

    cst = ar.get(128, [3, 128], BF16)
    ident, jrev, ones = cst[:, 0, :], cst[:, 1, :], cst[:, 2, :]
    b_cst = Buf()
    biasab = ar.get(128, [20, 384], BF16)
    b_biasab = Buf()
    midab = ar.get(1, [128], BF16)
    midc = ar.get(2, [28 * 128], BF16)
    kyind = ar.get(2, [128], BF16)
    b_mid = Buf()
    epsT = ar.get(128, [1], F32)
    b_eps = Buf()
    esb = ar.get(128, [1, 8], F32)
    b_esb = Buf()
    small = ar.get(128, [16], F32)
    b_small = [Buf() for _ in range(6)]
    PERSIST = ar.off

    P.dma("poolq", lambda e: e.dma_start(out=cst, in_=consts_in.rearrange("p (a b) -> p a b", a=3)), writes=[b_cst])
    P.dma("poolq", lambda e: e.dma_start(out=biasab, in_=biasab_in.rearrange("p (a b) -> p a b", a=20)), writes=[b_biasab])
    P.dma("poolq", lambda e: e.dma_start(out=midab, in_=midab_in), writes=[b_mid])
    P.dma("poolq", lambda e: e.dma_start(out=midc, in_=midc_in), writes=[b_mid])
    P.dma("poolq", lambda e: e.dma_start(out=kyind, in_=kyind_in), writes=[b_mid])
    P.op("dve", lambda e: e.memset(epsT, 1e-6), writes=[b_eps])

    wc = {}

    def precast(name, idx):
        src = W[name][idx]
        dst = Wb[name][idx]
        rows = src.shape[0]
        bl = []
        for r0 in range(0, rows, 256):
            r1 = min(rows, r0 + 256)
            b = Buf()
            P.dma("poolq", lambda e, r0=r0, r1=r1: e.dma_start(out=dst[r0:r1, :], in_=src[r0:r1, :]), writes=[b])
            bl.append(b)
        wc[(name, idx)] = bl

    for l in range(depth):
        j = l // 2
        if l % 2 == 0:
            precast("w_in_ab", j)
            precast("w_out_ab", j)
        else:
            precast("w_in_c", j)
            precast("w_out_c", j)
        precast("w_gate", l)
        precast("w_up", l)
        precast("w_down", l)

    norm_ctr = [0]

    def norm_tile(xin, xin_buf, gbc, b_g, nb, dstT, dst_buf, col0, out32=None):
        k = norm_ctr[0] % 2
        norm_ctr[0] += 1
        sq, b_sq, hb, b_hb = nb["sq"], nb["b_sq"], nb["hb"][k], nb["b_hb"][k]
        ss, sd, rs = small[:, k:k + 1], small[:, 2 + k:3 + k], small[:, 4 + k:5 + k]
        bss, bsd, brs = b_small[k], b_small[2 + k], b_small[4 + k]
        P.op("act", lambda e: e.activation(out=sq, in_=xin, func=AF.Square), reads=[xin_buf], writes=[b_sq])
        P.op("dve", lambda e: e.tensor_reduce(out=ss, in_=sq, axis=AX.X, op=ALU.add), reads=[b_sq], writes=[bss])
        P.op("act", lambda e: e.activation(out=sd, in_=ss, func=AF.Sqrt, bias=epsT, scale=1.0 / D),
             reads=[bss, b_eps], writes=[bsd])
        P.op("dve", lambda e: e.reciprocal(out=rs, in_=sd), reads=[bsd], writes=[brs])
        if out32 is not None:
            oap, ob = out32
            P.op("dve", lambda e: e.scalar_tensor_tensor(out=oap, in0=xin, scalar=rs, in1=gbc, op0=ALU.mult, op1=ALU.mult),
                 reads=[xin_buf, brs, b_g], writes=[ob])
            return
        P.op("dve", lambda e: e.scalar_tensor_tensor(out=hb, in0=xin, scalar=rs, in1=gbc, op0=ALU.mult, op1=ALU.mult),
             reads=[xin_buf, brs, b_g], writes=[b_hb])
        for q4 in range(2):
            tpv, btp = TPv[q4 % 2]
            for jj in range(8):
                kc = q4 * 8 + jj
                P.op("pe", lambda e, jj=jj, kc=kc, tpv=tpv: e.transpose(out=tpv[:, jj * 128:(jj + 1) * 128],
                                                                      in_=hb[:, kc * 128:(kc + 1) * 128], identity=ident),
                     reads=[b_hb, b_cst], writes=[btp])
            P.op("act", lambda e, q4=q4, tpv=tpv: e.activation(out=dstT[:, q4 * 8:q4 * 8 + 8, col0:col0 + 128],
                                                               in_=tpv.rearrange("p (k c) -> p k c", c=128), func=AF.Copy),
                 reads=[btp], writes=[dst_buf])

    def alloc_norm():
        nb = {"sq": ar.get(128, [D], F32), "b_sq": Buf(),
              "hb": [ar.get(128, [D], BF16) for _ in range(2)], "b_hb": [Buf(), Buf()]}
        gbc = ar.get(128, [1, D], F32)
        return nb, gbc, Buf()

    def phase_proj(l, xsrc):
        ar.off = PERSIST
        j = l // 2
        ab = (l % 2 == 0)
        wname = "w_in_ab" if ab else "w_in_c"
        Wsrc = Wb[wname][j].rearrange("(k p) n -> p k n", p=128)
        wdeps = wc[(wname, j)]
        nb, gbc, b_g = alloc_norm()
        hT = ar.get(128, [16, 2048], BF16)
        hTb = [Buf() for _ in range(16)]
        wbuf = [ar.get(128, [16, 512], BF16) for _ in range(2)]
        b_w = [Buf(), Buf()]
        stg = [ar.get(128, [2048], BF16) for _ in range(2)]
        b_stg = [Buf(), Buf()]
        vst = [ar.get(128, [512], BF16) for _ in range(2)]
        b_vst = [Buf(), Buf()]
        xt = [ar.get(128, [D], F32) for _ in range(2)]
        b_xt = [Buf(), Buf()]
        P.dma("sp", lambda e: e.dma_start(out=gbc, in_=norm_mix[l:l + 1, :].partition_broadcast(128)), writes=[b_g])
        groups = []
        if ab:
            for g in range(3):
                groups.append([("fm", DIL[g], g * 8 + c, True, c * 128) for c in range(4)])
                groups.append([("fm", DIL[g], g * 8 + 4 + c, False, c * 128) for c in range(4)])
                groups.append([("tm", DIL[g], 512 * g, 512, 0)])
            groups.append([("fm", 1, 24 + c, True, c * 128) for c in range(4)])
            groups.append([("fm", 1, 28 + c, True, c * 128) for c in range(4)])
            groups.append([("fm", 1, 32, False, 0), ("fm", 1, 33, False, 128), ("tm", 1, 1536, 256, 256)])
        else:
            for g in range(4):
                groups.append([("fm", 1, g * 4 + c, True, c * 128) for c in range(4)])
            for g in range(4):
                groups.append([("fm", 1, 16 + g * 4 + c, False, c * 128) for c in range(4)])
            for g in range(4):
                groups.append([("tm", 1, 512 * g, 512, 0)])
        ctr = {"stg": 0, "vst": 0}

        def load_w(gi):
            P.dma("sp", lambda e: e.dma_start(out=wbuf[gi % 2], in_=Wsrc[:, :, gi * 512:(gi + 1) * 512]),
                  reads=wdeps, writes=[b_w[gi % 2]])

        for tt in range(T // 2048):
            for s in range(16):
                k = s % 2
                r0 = tt * 2048 + s * 128
                P.dma("sp", lambda e, k=k, r0=r0: e.dma_start(out=xt[k], in_=xsrc[r0:r0 + 128, :]), writes=[b_xt[k]])
                norm_tile(xt[k], b_xt[k], gbc[:, 0, :], b_g, nb, hT, hTb[s], s * 128)
            load_w(0)
            for gi in range(12):
                if gi + 1 < 12:
                    load_w(gi + 1)
                wb_, bw_ = wbuf[gi % 2], b_w[gi % 2]
                for spec in groups[gi]:
                    if spec[0] == "fm":
                        _, d, row, scaled, c0 = spec
                        si = ctr["stg"] % 2
                        ctr["stg"] += 1
                        eng = "act" if si == 0 else "dve"
                        st_ = stg[si]
                        for tq in range(4):
                            bank, bb = next_bank()
                            for kc in range(16):
                                P.op("pe", lambda e, kc=kc, bank=bank, tq=tq, c0=c0, wb_=wb_: e.matmul(
                                    bank, lhsT=wb_[:, kc, c0:c0 + 128], rhs=hT[:, kc, tq * 512:(tq + 1) * 512],
                                    start=(kc == 0), stop=(kc == 15)),
                                    reads=[bw_] + hTb[tq * 4:tq * 4 + 4], writes=[bb])
                            m0, m1 = tq * 512 // d, (tq + 1) * 512 // d
                            if d == 1:
                                oap = st_[:, m0:m1]
                                iap = bank
                            else:
                                oap = st_.rearrange("p (r m) -> p r m", r=d)[:, :, m0:m1]
                                iap = bank.rearrange("p (m r) -> p r m", r=d)
                            sc = SCALE if scaled else 1.0
                            if eng == "act":
                                P.op("act", lambda e, oap=oap, iap=iap, sc=sc: e.activation(out=oap, in_=iap, func=AF.Copy, scale=sc),
                                     reads=[bb], writes=[b_stg[si]])
                            else:
                                P.op("dve", lambda e, oap=oap, iap=iap, sc=sc: e.tensor_scalar(out=oap, in0=iap, scalar1=sc, scalar2=None, op0=ALU.mult),
                                     reads=[bb], writes=[b_stg[si]])
                        M = T // d
                        ml = 2048 // d
                        if d == 1:
                            dst = QKs[row][:, tt * 2048:(tt + 1) * 2048]
                            src = st_
                        else:
                            dst = QKs[row].rearrange("p (r m) -> p r m", r=d)[:, :, tt * ml:(tt + 1) * ml]
                            src = st_.rearrange("p (r m) -> p r m", r=d)
                        P.dma("sp", lambda e, dst=dst, src=src: e.dma_start(out=dst, in_=src), reads=[b_stg[si]])
                    else:
                        _, d, vcol, N, c0 = spec
                        nbu = 16 // d
                        for r in range(d):
                            for mb in range(nbu):
                                vi = ctr["vst"] % 2
                                ctr["vst"] += 1
                                bank, bb = next_bank()
                                t0 = r + d * 128 * mb
                                for kc in range(16):
                                    if d == 1:
                                        lap = hT[:, kc, t0:t0 + 128]
                                    else:
                                        lap = hT[:, kc, t0:t0 + d * 127 + 1:d]
                                    P.op("pe", lambda e, kc=kc, bank=bank, lap=lap, c0=c0, N=N, wb_=wb_: e.matmul(
                                        bank[:, 0:N], lhsT=lap, rhs=wb_[:, kc, c0:c0 + N], start=(kc == 0), stop=(kc == 15)),
                                        reads=[bw_] + hTb[mb * d:mb * d + d], writes=[bb])
                                if vi == 0:
                                    P.op("act", lambda e, bank=bank, N=N, vi=vi: e.activation(out=vst[vi][:, 0:N], in_=bank[:, 0:N], func=AF.Copy),
                                         reads=[bb], writes=[b_vst[vi]])
                                else:
                                    P.op("dve", lambda e, bank=bank, N=N, vi=vi: e.tensor_copy(out=vst[vi][:, 0:N], in_=bank[:, 0:N]),
                                         reads=[bb], writes=[b_vst[vi]])
                                p0 = r * (T // d) + tt * (2048 // d) + mb * 128
                                P.dma("sp", lambda e, p0=p0, vcol=vcol, N=N, vi=vi: e.dma_start(
                                    out=Vs[p0:p0 + 128, vcol:vcol + N], in_=vst[vi][:, 0:N]), reads=[b_vst[vi]])
        P.barrier()

    class AttnBufs:
        pass

    def alloc_attn(nkc):
        a = AttnBufs()
        a.Qb = [ar.get(128, [2048], BF16) for _ in range(2)]
        a.Kb = [ar.get(128, [3072], BF16) for _ in range(2)]
        a.Vb = [ar.get(128, [24, 128], BF16) for _ in range(2)]
        a.bQ = [Buf(), Buf()]
        a.bK = [Buf(), Buf()]
        a.bV = [Buf(), Buf()]
        a.PT = [ar.get(128, [1024], BF16) for _ in range(2)]
        a.bPT = [Buf(), Buf()]
        a.dtmp = [ar.get(128, [128], F32) for _ in range(2)]
        a.bdt = [Buf(), Buf()]
        a.wo = [ar.get(128, [nkc, 512], BF16) for _ in range(2)]
        a.bwo = [Buf(), Buf()]
        a.xc = [ar.get(128, [512], F32) for _ in range(3)]
        a.bxc = [Buf() for _ in range(3)]
        a.xo = [ar.get(128, [512], F32) for _ in range(2)]
        a.bxo = [Buf(), Buf()]
        a.S = [(A0, [bA0l, bA0h]), (A1, [bA1l, bA1h])]
        a.ND = [(B0, bB0), (B1, bB1)]
        a.qb = 0
        a.pending = None
        return a

    def qblock(a, parts, evac):
        i = a.qb % 2
        a.qb += 1
        S, bS = a.S[i]
        PT, bPT = a.PT[i], a.bPT[i]
        sc = 0
        for p in parts:
            n = p["n"]
            p["sc"] = sc
            so = S[:, sc:sc + n]
            P.op("pe", lambda e, so=so, p=p: e.matmul(so, lhsT=p["K"], rhs=p["Q"], start=True, stop=False),
                 reads=[p["Kb"], p["Qb"]], writes=bS)
            P.op("pe", lambda e, so=so, p=p: e.matmul(so, lhsT=jrev, rhs=p["bias"], start=False, stop=(p["mask"] is None)),
                 reads=[b_cst, p["bbuf"]], writes=bS)
            if p["mask"] is not None:
                ml, mr, mbuf = p["mask"]
                P.op("pe", lambda e, so=so, ml=ml, mr=mr: e.matmul(so, lhsT=ml, rhs=mr, start=False, stop=True),
                     reads=[b_cst, mbuf], writes=bS)
            sc += n
        tot = sc
        P.op("act", lambda e, S=S, PT=PT, tot=tot: e.activation(out=PT[:, 0:tot], in_=S[:, 0:tot], func=AF.Exp),
             reads=bS, writes=[bPT])
        cur = (i, parts, evac)
        prev = a.pending
        a.pending = cur
        if prev is not None:
            finish(a, prev)

    def finish(a, item):
        i, parts, evac = item
        PT, bPT = a.PT[i], a.bPT[i]
        nd, bnd = a.ND[i]
        last = len(parts) - 1
        for pi, p in enumerate(parts):
            n, sc, oc0 = p["n"], p["sc"], p["oc0"]
            P.op("pe", lambda e, p=p, n=n, sc=sc, oc0=oc0, pi=pi, nd=nd, PT=PT: e.matmul(
                nd[:, oc0:oc0 + n], lhsT=p["V"], rhs=PT[:, sc:sc + n], start=(pi == 0), stop=(pi == last)),
                reads=[p["Vbuf"], bPT], writes=[bnd])
        for pi, p in enumerate(parts):
            n, sc, oc0 = p["n"], p["sc"], p["oc0"]
            P.op("pe", lambda e, n=n, sc=sc, oc0=oc0, pi=pi, nd=nd, PT=PT: e.matmul(
                nd[:, 128 + oc0:128 + oc0 + n], lhsT=ones, rhs=PT[:, sc:sc + n], start=(pi == 0), stop=(pi == last)),
                reads=[b_cst, bPT], writes=[bnd])
        evac(nd, bnd, i)

    def flush(a):
        if a.pending is not None:
            finish(a, a.pending)
            a.pending = None

    def out_proj(a, wname, j, nkc, mixedT, b_mx, xsrc, tok0, ntok):
        Wsrc = Wb[wname][j].rearrange("(k p) n -> p k n", p=128)
        wdeps = wc[(wname, j)]
        ci = 0

        def load_wo(n):
            P.dma("sp", lambda e, n=n: e.dma_start(out=a.wo[n % 2], in_=Wsrc[:, :, n * 512:(n + 1) * 512]),
                  reads=wdeps, writes=[a.bwo[n % 2]])
        load_wo(0)
        for n in range(4):
            if n + 1 < 4:
                load_wo(n + 1)
            wo, bwo = a.wo[n % 2], a.bwo[n % 2]
            for sub in range(ntok // 128):
                r0 = tok0 + sub * 128
                xi, oi = ci % 3, ci % 2
                ci += 1
                P.dma("sp", lambda e, r0=r0, n=n, xi=xi: e.dma_start(out=a.xc[xi], in_=xsrc[r0:r0 + 128, n * 512:(n + 1) * 512]),
                      writes=[a.bxc[xi]])
                bank, bb = next_bank()
                for kc in range(nkc):
                    P.op("pe", lambda e, kc=kc, bank=bank, sub=sub, wo=wo: e.matmul(
                        bank, lhsT=mixedT[:, kc, sub * 128:(sub + 1) * 128], rhs=wo[:, kc, :], start=(kc == 0), stop=(kc == nkc - 1)),
                        reads=[b_mx, bwo], writes=[bb])
                P.op("dve", lambda e, bank=bank, xi=xi, oi=oi: e.tensor_tensor(out=a.xo[oi], in0=bank, in1=a.xc[xi], op=ALU.add),
                     reads=[bb, a.bxc[xi]], writes=[a.bxo[oi]])
                P.dma("sp", lambda e, r0=r0, n=n, oi=oi: e.dma_start(out=XB[r0:r0 + 128, n * 512:(n + 1) * 512], in_=a.xo[oi]),
                      reads=[a.bxo[oi]])

    def phase_attn_ab(l, xsrc):
        ar.off = PERSIST
        j = l // 2
        a = alloc_attn(12)
        mixedT = ar.get(128, [12, 2048], BF16)
        b_mx = Buf()
        accn = ar.get(128, [2048], F32)
        accd = ar.get(128, [2048], F32)
        b_acc = Buf()
        P.dma("sp", lambda e: e.dma_start(out=esb, in_=sink_b[j:j + 1, :].partition_broadcast(128)), writes=[b_esb])
        P.op("act", lambda e: e.activation(out=esb, in_=esb, func=AF.Exp), reads=[b_esb], writes=[b_esb])
        qi = [0]
        kvi = [0]
        for u in range(T // 2048):
            for s in range(4):
                for g in range(3):
                    d = DIL[g]
                    nbu = 16 // d
                    Mb = NB // d
                    midb = Mb // 2
                    RB = min(d, 4)
                    lo = max(u * nbu - 1, 0)
                    hi = min((u + 1) * nbu + 1, Mb)
                    nk = hi - lo
                    h = g * 4 + s
                    for rb in range(d // RB):
                        q_i = qi[0] % 2
                        qi[0] += 1
                        k_i = kvi[0] % 2
                        kvi[0] += 1
                        Qv = a.Qb[q_i][:, 0:RB * nbu * 128].rearrange("p (r m) -> p r m", r=RB)
                        Kv = a.Kb[k_i][:, 0:RB * nk * 128].rearrange("p (r m) -> p r m", r=RB)
                        Vv = a.Vb[k_i][:, 0:RB * nk, :].rearrange("p (r b) c -> p r b c", r=RB)
                        qsrc = QKs[g * 8 + s].rearrange("p (r m) -> p r m", r=d)[:, rb * RB:(rb + 1) * RB, u * nbu * 128:(u + 1) * nbu * 128]
                        ksrc = QKs[g * 8 + 4 + s].rearrange("p (r m) -> p r m", r=d)[:, rb * RB:(rb + 1) * RB, lo * 128:hi * 128]
                        P.dma("sp", lambda e, Qv=Qv, qsrc=qsrc: e.dma_start(out=Qv, in_=qsrc), writes=[a.bQ[q_i]])
                        P.dma("sp", lambda e, Kv=Kv, ksrc=ksrc: e.dma_start(out=Kv, in_=ksrc), writes=[a.bK[k_i]])
                        for rr in range(RB):
                            r = rb * RB + rr
                            vsrc = Vs[(r * Mb + lo) * 128:(r * Mb + hi) * 128, 512 * g + s * 128:512 * g + s * 128 + 128].rearrange("(b p) c -> p b c", p=128)
                            P.dma("sp", lambda e, rr=rr, Vv=Vv, vsrc=vsrc: e.dma_start(out=Vv[:, rr, :, :], in_=vsrc), writes=[a.bV[k_i]])
                        for rr in range(RB):
                            r = rb * RB + rr
                            for mbl in range(nbu):
                                mb = u * nbu + mbl
                                parts = []

                                def mk(kblk, qc0, n, bc0, mask):
                                    return dict(K=Kv[:, rr, (kblk - lo) * 128:(kblk - lo + 1) * 128], Kb=a.bK[k_i],
                                                Q=Qv[:, rr, mbl * 128 + qc0:mbl * 128 + qc0 + n], Qb=a.bQ[q_i], n=n,
                                                bias=biasab[:, h, bc0:bc0 + n], bbuf=b_biasab,
                                                mask=((ones[0:1, :], midab[0:1, 0:n], b_mid) if mask else None),
                                                V=Vv[:, rr, kblk - lo, :], Vbuf=a.bV[k_i], oc0=qc0)
                                parts.append(mk(mb, 0, 128, 128, False))
                                if mb > 0:
                                    parts.append(mk(mb - 1, 0, 64, 0, mb == midb))
                                if mb < Mb - 1:
                                    parts.append(mk(mb + 1, 64, 64, 320, mb == midb - 1))
                                c0 = r + d * 128 * mbl
                                if d == 1:
                                    cn = accn[:, c0:c0 + 128]
                                    cd = accd[:, c0:c0 + 128]
                                else:
                                    cn = accn[:, c0:c0 + d * 127 + 1:d]
                                    cd = accd[:, c0:c0 + d * 127 + 1:d]

                                def evac(nd, bnd, i, g=g, cn=cn, cd=cd):
                                    if g == 0:
                                        P.op("dve", lambda e: e.tensor_copy(out=cn, in_=nd[:, 0:128]), reads=[bnd], writes=[b_acc])
                                        P.op("dve", lambda e: e.tensor_copy(out=cd, in_=nd[:, 128:256]), reads=[bnd], writes=[b_acc])
                                    else:
                                        P.op("dve", lambda e: e.tensor_tensor(out=cn, in0=cn, in1=nd[:, 0:128], op=ALU.add), reads=[bnd], writes=[b_acc])
                                        P.op("dve", lambda e: e.tensor_tensor(out=cd, in0=cd, in1=nd[:, 128:256], op=ALU.add), reads=[bnd], writes=[b_acc])
                                qblock(a, parts, evac)
                flush(a)
                P.op("dve", lambda e: e.reciprocal(out=accd, in_=accd), writes=[b_acc])
                P.op("dve", lambda e, s=s: e.tensor_tensor(out=mixedT[:, s, :], in0=accn, in1=accd, op=ALU.mult), reads=[b_acc], writes=[b_mx])
            lo = max(16 * u - 1, 0)
            hi = min(16 * u + 17, NB)
            nk = hi - lo
            for kvh in range(2):
                k_i = kvi[0] % 2
                kvi[0] += 1
                Kf = a.Kb[k_i]
                Vf = a.Vb[k_i]
                P.dma("sp", lambda e, Kf=Kf, kvh=kvh, lo=lo, hi=hi, nk=nk: e.dma_start(out=Kf[:, 0:nk * 128], in_=QKs[32 + kvh][:, lo * 128:hi * 128]),
                      writes=[a.bK[k_i]])
                P.dma("sp", lambda e, Vf=Vf, kvh=kvh, lo=lo, hi=hi, nk=nk: e.dma_start(
                    out=Vf[:, 0:nk, :], in_=Vs[lo * 128:hi * 128, 1536 + kvh * 128:1536 + kvh * 128 + 128].rearrange("(b p) c -> p b c", p=128)),
                    writes=[a.bV[k_i]])
                for hq in range(4):
                    h = kvh * 4 + hq
                    q_i = qi[0] % 2
                    qi[0] += 1
                    Qf = a.Qb[q_i]
                    P.dma("sp", lambda e, Qf=Qf, h=h, u=u: e.dma_start(out=Qf, in_=QKs[24 + h][:, u * 2048:(u + 1) * 2048]), writes=[a.bQ[q_i]])
                    for mbl in range(16):
                        mb = 16 * u + mbl

                        def mk(kblk, bc0, mask):
                            return dict(K=Kf[:, (kblk - lo) * 128:(kblk - lo + 1) * 128], Kb=a.bK[k_i],
                                        Q=Qf[:, mbl * 128:(mbl + 1) * 128], Qb=a.bQ[q_i], n=128,
                                        bias=biasab[:, 12 + h, bc0:bc0 + 128], bbuf=b_biasab,
                                        mask=((ones[0:1, :], midab[0:1, :], b_mid) if mask else None),
                                        V=Vf[:, kblk - lo, :], Vbuf=a.bV[k_i], oc0=0)
                        parts = [mk(mb, 128, False)]
                        if mb > 0:
                            parts.append(mk(mb - 1, 0, mb == HALF))
                        if mb < NB - 1:
                            parts.append(mk(mb + 1, 256, mb == HALF - 1))

                        def evac(nd, bnd, i, h=h, mbl=mbl):
                            dt_, bdt = a.dtmp[i], a.bdt[i]
                            P.op("dve", lambda e: e.tensor_scalar(out=dt_, in0=nd[:, 128:256], scalar1=esb[:, 0, h:h + 1], scalar2=None, op0=ALU.add),
                                 reads=[bnd, b_esb], writes=[bdt])
                            P.op("dve", lambda e: e.reciprocal(out=dt_, in_=dt_), writes=[bdt])
                            P.op("dve", lambda e: e.tensor_tensor(out=mixedT[:, 4 + h, mbl * 128:(mbl + 1) * 128], in0=nd[:, 0:128], in1=dt_, op=ALU.mult),
                                 reads=[bnd, bdt], writes=[b_mx])
                        qblock(a, parts, evac)
            flush(a)
            out_proj(a, "w_out_ab", j, 12, mixedT, b_mx, xsrc, u * 2048, 2048)
        P.barrier()

    def phase_attn_c(l, xsrc):
        ar.off = PERSIST
        j = l // 2
        a = alloc_attn(16)
        mixedT = ar.get(128, [16, 1024], BF16)
        b_mx = Buf()
        bc = [ar.get(128, [9, 128], BF16) for _ in range(2)]
        b_bc = [Buf(), Buf()]
        bsrc = biasc_in[j].rearrange("p (h t c) -> p h t c", h=16, t=9)
        it = 0
        for u in range(T // 1024):
            lo = max(8 * u - 3, 0)
            hi = min(8 * u + 11, NB)
            nk = hi - lo
            for h in range(16):
                i2 = it % 2
                it += 1
                Qf, Kf, Vf = a.Qb[i2], a.Kb[i2], a.Vb[i2]
                P.dma("poolq", lambda e, i2=i2, h=h: e.dma_start(out=bc[i2], in_=bsrc[:, h, :, :]), writes=[b_bc[i2]])
                P.dma("sp", lambda e, Qf=Qf, h=h, u=u: e.dma_start(out=Qf[:, 0:1024], in_=QKs[h][:, u * 1024:(u + 1) * 1024]), writes=[a.bQ[i2]])
                P.dma("sp", lambda e, Kf=Kf, h=h, lo=lo, hi=hi, nk=nk: e.dma_start(out=Kf[:, 0:nk * 128], in_=QKs[16 + h][:, lo * 128:hi * 128]), writes=[a.bK[i2]])
                P.dma("sp", lambda e, Vf=Vf, h=h, lo=lo, hi=hi, nk=nk: e.dma_start(
                    out=Vf[:, 0:nk, :], in_=Vs[lo * 128:hi * 128, h * 128:(h + 1) * 128].rearrange("(b p) c -> p b c", p=128)), writes=[a.bV[i2]])
                for bl in range(8):
                    b = 8 * u + bl
                    if HALF - 2 <= b <= HALF + 1:
                        dl = [(dd, dd + 3, (b - (HALF - 2)) * 7 + dd + 3) for dd in range(-3, 4)]
                    elif b == 0:
                        dl = [(dd, dd + 3, None) for dd in range(0, 4)]
                    elif b == 1:
                        dl = [(dd, dd + 3, None) for dd in range(-1, 3)]
                    elif b == NB - 2:
                        dl = [(dd, dd + 3, None) for dd in range(-2, 2)]
                    elif b == NB - 1:
                        dl = [(dd, dd + 3, None) for dd in range(-3, 1)]
                    else:
                        dl = [(-2, 7, None), (-1, 2, None), (0, 3, None), (1, 4, None), (2, 8, None)]
                    parts = []
                    for dd, tile, mi in dl:
                        kb_ = b + dd
                        if kb_ < 0 or kb_ >= NB:
                            continue
                        parts.append(dict(K=Kf[:, (kb_ - lo) * 128:(kb_ - lo + 1) * 128], Kb=a.bK[i2],
                                          Q=Qf[:, bl * 128:(bl + 1) * 128], Qb=a.bQ[i2], n=128,
                                          bias=bc[i2][:, tile, :], bbuf=b_bc[i2],
                                          mask=((kyind[0:2, :], midc[0:2, mi * 128:(mi + 1) * 128], b_mid) if mi is not None else None),
                                          V=Vf[:, kb_ - lo, :], Vbuf=a.bV[i2], oc0=0))

                    def evac(nd, bnd, i, h=h, bl=bl):
                        dt_, bdt = a.dtmp[i], a.bdt[i]
                        P.op("dve", lambda e: e.reciprocal(out=dt_, in_=nd[:, 128:256]), reads=[bnd], writes=[bdt])
                        P.op("dve", lambda e: e.tensor_tensor(out=mixedT[:, h, bl * 128:(bl + 1) * 128], in0=nd[:, 0:128], in1=dt_, op=ALU.mult),
                             reads=[bnd, bdt], writes=[b_mx])
                    qblock(a, parts, evac)
            flush(a)
            out_proj(a, "w_out_c", j, 16, mixedT, b_mx, xsrc, u * 1024, 1024)
        P.barrier()

    def phase_ffn(l, xdst, final):
        ar.off = PERSIST
        nb, gbc, b_g = alloc_norm()
        xq = ar.get(128, [8, D], F32)
        xqb = [Buf() for _ in range(8)]
        hT = ar.get(128, [16, 1024], BF16)
        hTb = [Buf() for _ in range(8)]
        wg = [ar.get(128, [16, 256], BF16) for _ in range(2)]
        wu = [ar.get(128, [16, 256], BF16) for _ in range(2)]
        wd = [ar.get(128, [2, D], BF16) for _ in range(2)]
        b_wg, b_wu, b_wd = [Buf(), Buf()], [Buf(), Buf()], [Buf(), Buf()]
        gT = [ar.get(128, [2, 1024], BF16) for _ in range(2)]
        b_gT = [Buf(), Buf()]
        sg = [ar.get(128, [512], F32) for _ in range(2)]
        b_sg = [Buf(), Buf()]
        Wg = Wb["w_gate"][l].rearrange("(k p) n -> p k n", p=128)
        Wu = Wb["w_up"][l].rearrange("(k p) n -> p k n", p=128)
        Wd = Wb["w_down"][l].rearrange("(c p) n -> p c n", p=128)
        dg, du, dd_ = wc[("w_gate", l)], wc[("w_up", l)], wc[("w_down", l)]
        sgi = [0]
        NFG = DFF // 256

        def load_f(fg):
            k = fg % 2
            P.dma("sp", lambda e: e.dma_start(out=wg[k], in_=Wg[:, :, fg * 256:(fg + 1) * 256]), reads=dg, writes=[b_wg[k]])
            P.dma("sp", lambda e: e.dma_start(out=wu[k], in_=Wu[:, :, fg * 256:(fg + 1) * 256]), reads=du, writes=[b_wu[k]])
            P.dma("sp", lambda e: e.dma_start(out=wd[k], in_=Wd[:, fg * 2:fg * 2 + 2, :]), reads=dd_, writes=[b_wd[k]])

        for tt in range(T // 1024):
            for s in range(8):
                r0 = tt * 1024 + s * 128
                P.dma("sp", lambda e, s=s, r0=r0: e.dma_start(out=xq[:, s, :], in_=XB[r0:r0 + 128, :]), writes=[xqb[s]])
            P.dma("sp", lambda e: e.dma_start(out=gbc, in_=norm_ffn[l:l + 1, :].partition_broadcast(128)), writes=[b_g])
            load_f(0)
            for s in range(8):
                norm_tile(xq[:, s, :], xqb[s], gbc[:, 0, :], b_g, nb, hT, hTb[s], s * 128)
            if final:
                P.dma("sp", lambda e: e.dma_start(out=gbc, in_=norm_final[0:1, :].partition_broadcast(128)), writes=[b_g])
            for fg in range(NFG):
                if fg + 1 < NFG:
                    load_f(fg + 1)
                k = fg % 2
                for c in range(2):
                    for tq in range(2):
                        G, bG = next_bank()
                        U, bU = next_bank()
                        for kc in range(16):
                            P.op("pe", lambda e, kc=kc, G=G, c=c, tq=tq, k=k: e.matmul(
                                G, lhsT=wg[k][:, kc, c * 128:(c + 1) * 128], rhs=hT[:, kc, tq * 512:(tq + 1) * 512], start=(kc == 0), stop=(kc == 15)),
                                reads=[b_wg[k]] + hTb[tq * 4:tq * 4 + 4], writes=[bG])
                        for kc in range(16):
                            P.op("pe", lambda e, kc=kc, U=U, c=c, tq=tq, k=k: e.matmul(
                                U, lhsT=wu[k][:, kc, c * 128:(c + 1) * 128], rhs=hT[:, kc, tq * 512:(tq + 1) * 512], start=(kc == 0), stop=(kc == 15)),
                                reads=[b_wu[k]] + hTb[tq * 4:tq * 4 + 4], writes=[bU])
                        si = sgi[0] % 2
                        sgi[0] += 1
                        P.op("act", lambda e, G=G, si=si: e.activation(out=sg[si], in_=G, func=AF.Silu), reads=[bG], writes=[b_sg[si]])
                        P.op("dve", lambda e, U=U, si=si, c=c, tq=tq, k=k: e.tensor_tensor(
                            out=gT[k][:, c, tq * 512:(tq + 1) * 512], in0=sg[si], in1=U, op=ALU.mult),
                            reads=[b_sg[si], bU], writes=[b_gT[k]])
                for sub in range(8):
                    for n in range(4):
                        bank, bb = next_bank()
                        for c in range(2):
                            P.op("pe", lambda e, c=c, bank=bank, sub=sub, n=n, k=k: e.matmul(
                                bank, lhsT=gT[k][:, c, sub * 128:(sub + 1) * 128], rhs=wd[k][:, c, n * 512:(n + 1) * 512], start=(c == 0), stop=(c == 1)),
                                reads=[b_gT[k], b_wd[k]], writes=[bb])
                        P.op("dve", lambda e, bank=bank, sub=sub, n=n: e.tensor_tensor(
                            out=xq[:, sub, n * 512:(n + 1) * 512], in0=xq[:, sub, n * 512:(n + 1) * 512], in1=bank, op=ALU.add),
                            reads=[bb], writes=[xqb[sub]])
            for s in range(8):
                r0 = tt * 1024 + s * 128
                if final:
                    norm_tile(xq[:, s, :], xqb[s], gbc[:, 0, :], b_g, nb, None, None, 0, out32=(nb["sq"], nb["b_sq"]))
                    P.dma("sp", lambda e, r0=r0: e.dma_start(out=y_out[r0:r0 + 128, :], in_=nb["sq"]), reads=[nb["b_sq"]])
                else:
                    P.dma("sp", lambda e, s=s, r0=r0: e.dma_start(out=xdst[r0:r0 + 128, :], in_=xq[:, s, :]), reads=[xqb[s]])
        P.barrier()

    P.barrier(queues=())
    xcur = x_in
    def copy_out(src):
        for r0 in range(0, T, 1024):
            P.dma("sp", lambda e, r0=r0: e.dma_start(out=y_out[r0:r0 + 1024, :], in_=src[r0:r0 + 1024, :]))

    for l in range(depth):
        if stop == ("pre", l):
            copy_out(x_in)
            break
        phase_proj(l, xcur)
        if stop == ("proj", l):
            copy_out(x_in)
            break
        if l % 2 == 0:
            phase_attn_ab(l, xcur)
        else:
            phase_attn_c(l, xcur)
        if stop == ("mix", l):
            copy_out(XB)
            break
        phase_ffn(l, XA, l == depth - 1 and stop is None)
        xcur = XA
        if stop == ("ffn", l):
            copy_out(XA)
            break
    P.barrier()
    P.emit(nc)
    es.close()
    return nc


def _t5_bucket(rel):
    nb = 16
    max_exact = 8
    ret = (rel > 0).astype(np.int32) * nb
    n = np.abs(rel)
    large = max_exact + (np.log(np.maximum(n, 1) / max_exact) / np.log(1024 / max_exact) * (nb - max_exact)).astype(np.int32)
    large = np.minimum(large, nb - 1)
    return (ret + np.where(n < max_exact, n, large)).astype(np.int32)


def _host_tables(t5_table, rpb_c):
    t5 = np.asarray(t5_table, np.float32)
    p = np.arange(128)[:, None, None]
    oi = np.arange(3)[None, :, None]
    qq = np.arange(128)[None, None, :]
    rel = 128 * (oi - 1) + (127 - p) - qq
    biasab = np.empty((128, 20, 3, 128), np.float32)
    for h in range(20):
        d = DIL[h // 4] if h < 12 else 1
        hw = 64 if h < 12 else 128
        vals = t5[_t5_bucket(rel * d), h]
        biasab[:, h] = np.where(np.abs(rel) <= hw, vals, np.float32(NEG))
    rpb = np.asarray(rpb_c, np.float32)
    kk = 127 - np.arange(128)
    ky, kx = kk // 64, kk % 64
    c = np.arange(128)
    qy, qx = c // 64, c % 64
    cs = np.clip(qx - 8, 0, 48)
    colv = (kx[:, None] >= cs[None, :]) & (kx[:, None] < cs[None, :] + 16)
    dc = np.clip(kx[:, None] - qx[None, :] + 15, 0, 30)
    biasc = np.empty((2, 128, 16, 9, 128), np.float32)
    for ti in range(9):
        dd = ti - 3 if ti < 7 else (-2 if ti == 7 else 2)
        dr = 2 * dd + ky[:, None] - qy[None, :]
        valid = colv.copy()
        if ti >= 7:
            valid &= (dr >= -4) & (dr <= 3)
        dri = np.clip(dr + 7, 0, 14)
        vals = rpb[:, :, dri, dc]
        biasc[:, :, :, ti, :] = np.where(valid[None, None], vals, np.float32(NEG)).transpose(0, 2, 1, 3)
    return biasab.reshape(128, 20 * 384), biasc.reshape(2, 128, 16 * 9 * 128)


def _mode_masks(T, two_seq):
    NB = T // 128
    HALF = NB // 2
    rows = T // 64
    midab = np.full((1, 128), NEG if two_seq else 0.0, np.float32)

    def key_rows(r):
        if two_seq:
            hr = rows // 2
            base = 0 if r < hr else hr
            rs = base + min(max((r - base) - 4, 0), hr - 8)
        else:
            rs = min(max(r - 4, 0), rows - 8)
        return rs, rs + 8
    midc = np.zeros((2, 4, 7, 128), np.float32)
    for qi in range(4):
        b = HALF - 2 + qi
        for di in range(7):
            dd = di - 3
            for kyp in range(2):
                yk = 2 * (b + dd) + kyp
                for qy in range(2):
                    r = 2 * b + qy
                    lo, hi = key_rows(r)
                    if not (lo <= yk < hi):
                        midc[kyp, qi, di, qy * 64:(qy + 1) * 64] = NEG
    return midab, midc.reshape(2, 28 * 128)


def _consts():
    c = np.zeros((128, 3, 128), np.float32)
    c[:, 0] = np.eye(128)
    c[:, 1] = np.eye(128)[::-1]
    c[:, 2] = 1.0
    ky = np.zeros((2, 128), np.float32)
    ky[0, :64] = 1.0
    ky[1, 64:] = 1.0
    return c.reshape(128, 384), ky


def make_in_maps(xs, modes, T, weights):
    biasab, biasc = _host_tables(weights["t5_table"], weights["rpb_c"])
    consts, kyind = _consts()
    shared = {k: np.ascontiguousarray(np.asarray(weights[k], np.float32)) for k in
              ("w_in_ab", "w_out_ab", "w_in_c", "w_out_c", "w_gate", "w_up", "w_down", "norm_mix", "norm_ffn", "sink_b")}
    shared["norm_final"] = np.ascontiguousarray(np.asarray(weights["norm_final"], np.float32).reshape(1, D))
    shared.update(biasab=biasab, biasc=biasc, consts=consts, kyind=kyind)
    mm = {m: _mode_masks(T, m) for m in set(modes)}
    maps = []
    for x, m in zip(xs, modes):
        d = dict(shared)
        d["x"] = np.ascontiguousarray(x, dtype=np.float32)
        d["midab"], d["midc"] = mm[m]
        maps.append(d)
    return maps


def kernel(x_prompt, x_sample, w_in_ab, w_out_ab, sink_b, w_in_c, w_out_c, rpb_c, t5_table,
           norm_mix, norm_ffn, w_gate, w_up, w_down, norm_final):
    T = 8192
    xp = np.asarray(x_prompt, np.float32)
    xsm = np.asarray(x_sample, np.float32)
    weights = dict(w_in_ab=w_in_ab, w_out_ab=w_out_ab, sink_b=sink_b, w_in_c=w_in_c, w_out_c=w_out_c,
                   rpb_c=rpb_c, t5_table=t5_table, norm_mix=norm_mix, norm_ffn=norm_ffn,
                   w_gate=w_gate, w_up=w_up, w_down=w_down, norm_final=norm_final)
    zero = np.zeros((T, D), np.float32)
    xs = [xsm[0], xsm[1], xsm[2], xsm[3],
          np.concatenate([xp[0], xp[1]], 0), np.concatenate([xp[2], xp[3]], 0), zero, zero]
    modes = [False, False, False, False, True, True, False, False]
    nc = build(T, 4)
    in_maps = make_in_maps(xs, modes, T, weights)
    res = run_bass_kernel_spmd(nc, in_maps, core_ids=list(range(8)))
    ys = [np.asarray(r["y"]) for r in res.results]
    y_sample = np.stack(ys[0:4], 0)
    y_prompt = np.stack([ys[4][:4096], ys[4][4096:], ys[5][:4096], ys[5][4096:]], 0)
    return (y_prompt, y_sample)
```

```python
import contextlib
import numpy as np
import concourse.bass as bass
import concourse.mybir as mybir
from concourse.bass_utils import run_bass_kernel_spmd

F32 = mybir.dt.float32
BF16 = mybir.dt.bfloat16
AF = mybir.ActivationFunctionType
ALU = mybir.AluOpType
AX = mybir.AxisListType

D = 2048
KC = 16
DFF = 5632
SCALE = 128 ** -0.5
NEG = -1e30
DIL = (1, 4, 16)

COMPUTE = ("pe", "act", "dve", "pool")
QUEUES = {"sp": 8, "poolq": 8}
QUEUE_ENGINE = {"sp": "sp", "poolq": "pool"}


class Buf:
    __slots__ = ("name", "w", "rs")

    def __init__(self, name=""):
        self.name = name
        self.w = None
        self.rs = []


class Prog:
    def __init__(self):
        self.streams = {e: [] for e in ("pe", "act", "dve", "pool", "sp")}
        self.cnt = {e: 0 for e in COMPUTE}
        self.dcnt = {q: 0 for q in QUEUES}
        self.seen = {}

    def _waits_for(self, stream, toks):
        waits = {}
        for t in toks:
            if t is None:
                continue
            if t[0] == "c":
                _, eng, seq = t
                if eng == "pe" and stream == "pe":
                    continue
                key = ("c", eng)
                val = seq
            else:
                _, q, n = t
                k = QUEUES[q]
                key = ("d", q, n % k)
                val = 16 * (n // k + 1)
            if self.seen.get((stream, key), 0) >= val:
                continue
            if waits.get(key, 0) < val:
                waits[key] = val
        for key, val in waits.items():
            self.seen[(stream, key)] = val
        return list(waits.items())

    @staticmethod
    def _deps(reads, writes):
        toks = []
        for b in reads:
            toks.append(b.w)
        for b in writes:
            toks.append(b.w)
            toks.extend(b.rs)
        return toks

    @staticmethod
    def _mark(tok, reads, writes):
        for b in reads:
            b.rs.append(tok)
            if len(b.rs) > 64:
                last = {}
                for t in b.rs:
                    key = (t[0], t[1]) if t[0] == "c" else (t[0], t[1], t[2] % QUEUES[t[1]])
                    if key not in last or last[key][2] < t[2]:
                        last[key] = t
                b.rs = list(last.values())
        for b in writes:
            b.w = tok
            b.rs = []

    def op(self, eng, fn, reads=(), writes=()):
        waits = self._waits_for(eng, self._deps(reads, writes))
        self.cnt[eng] += 1
        tok = ("c", eng, self.cnt[eng])
        self.streams[eng].append((fn, waits, ("c", eng)))
        self._mark(tok, reads, writes)
        return tok

    def dma(self, q, fn, reads=(), writes=()):
        stream = QUEUE_ENGINE[q]
        n = self.dcnt[q]
        k = QUEUES[q]
        toks = self._deps(reads, writes)
        if n >= k:
            toks.append(("d", q, n - k))
        waits = self._waits_for(stream, toks)
        self.dcnt[q] += 1
        tok = ("d", q, n)
        self.streams[stream].append((fn, waits, ("d", q, n % k)))
        self._mark(tok, reads, writes)
        return tok

    def barrier(self, queues=("sp",)):
        toks = [("c", e, self.cnt[e]) for e in COMPUTE if self.cnt[e] > 0]
        for q in queues:
            n = self.dcnt[q]
            for i in range(max(0, n - QUEUES[q]), n):
                toks.append(("d", q, i))
        for stream in self.streams:
            waits = self._waits_for(stream, toks)
            if waits:
                self.streams[stream].append((None, waits, None))

    def emit(self, nc):
        with contextlib.ExitStack() as es:
            sems = {}
            for e in COMPUTE:
                sems[("c", e)] = es.enter_context(nc.semaphore("s_" + e))
            for q, k in QUEUES.items():
                for i in range(k):
                    sems[("d", q, i)] = es.enter_context(nc.semaphore("s_%s%d" % (q, i)))
            block = es.enter_context(nc.Block())
            streams = self.streams

            def run(engine, ops):
                for fn, waits, inc in ops:
                    for key, val in waits:
                        engine.wait_ge(sems[key], val)
                    if fn is None:
                        continue
                    ins = fn(engine)
                    ins.then_inc(sems[inc], 1 if inc[0] == "c" else 16)

            @block.tensor
            def _(e):
                run(e, streams["pe"])

            @block.scalar
            def _(e):
                run(e, streams["act"])

            @block.vector
            def _(e):
                run(e, streams["dve"])

            @block.gpsimd
            def _(e):
                run(e, streams["pool"])

            @block.sync
            def _(e):
                run(e, streams["sp"])


ARENA_WORDS = 53200


class Arena:
    def __init__(self, t):
        self.t = t
        self.off = 0

    def get(self, parts, free, dt):
        n = int(np.prod(free))
        words = n if dt == F32 else (n + 1) // 2
        ap = self.t[0:parts, self.off:self.off + words]
        self.off += words
        assert self.off <= ARENA_WORDS, ("arena overflow", self.off)
        if dt == BF16:
            ap = ap.bitcast(BF16)[:, 0:n]
        if len(free) == 2:
            ap = ap.rearrange("p (a b) -> p a b", a=free[0])
        elif len(free) == 3:
            ap = ap.rearrange("p (a b c) -> p a b c", a=free[0], b=free[1])
        return ap


def build(T, depth, stop=None):
    NB = T // 128
    HALF = NB // 2
    nc = bass.Bass("TRN2", target_bir_lowering=False)

    def din(name, shape, dt=F32):
        return nc.dram_tensor(name, list(shape), dt, kind="ExternalInput").ap()

    def dscr(name, shape, dt):
        return nc.dram_tensor(name, list(shape), dt, kind="Internal").ap()

    x_in = din("x", [T, D])
    W = {
        "w_in_ab": din("w_in_ab", [2, D, 6144]), "w_out_ab": din("w_out_ab", [2, 1536, D]),
        "w_in_c": din("w_in_c", [2, D, 6144]), "w_out_c": din("w_out_c", [2, D, D]),
        "w_gate": din("w_gate", [4, D, DFF]), "w_up": din("w_up", [4, D, DFF]),
        "w_down": din("w_down", [4, DFF, D]),
    }
    norm_mix = din("norm_mix", [4, D])
    norm_ffn = din("norm_ffn", [4, D])
    norm_final = din("norm_final", [1, D])
    sink_b = din("sink_b", [2, 8])
    biasab_in = din("biasab", [128, 20 * 384])
    biasc_in = din("biasc", [2, 128, 16 * 9 * 128])
    midab_in = din("midab", [1, 128])
    midc_in = din("midc", [2, 28 * 128])
    consts_in = din("consts", [128, 3 * 128])
    kyind_in = din("kyind", [2, 128])
    y_out = nc.dram_tensor("y", [T, D], F32, kind="ExternalOutput").ap()

    Wb = {k: dscr("b_" + k, v.shape, BF16) for k, v in W.items()}
    QKs = dscr("qks", [34, 128, T], BF16)
    Vs = dscr("vs", [T, 2048], BF16)
    XA = dscr("xa", [T, D], F32)
    XB = dscr("xb", [T, D], F32)

    P = Prog()
    es = contextlib.ExitStack()
    arena_t = es.enter_context(nc.sbuf_tensor("arena", [128, ARENA_WORDS], F32))
    ar = Arena(arena_t)
    A0 = es.enter_context(nc.psum_tensor("pa0", [128, 1024], F32))
    A1 = es.enter_context(nc.psum_tensor("pa1", [128, 1024], F32))
    B0 = es.enter_context(nc.psum_tensor("pb0", [128, 512], F32))
    B1 = es.enter_context(nc.psum_tensor("pb1", [128, 512], F32))
    TPa = es.enter_context(nc.psum_tensor("ptpa", [128, 1024], BF16))
    TPb = es.enter_context(nc.psum_tensor("ptpb", [128, 1024], BF16))
    bA0l, bA0h, bA1l, bA1h, bB0, bB1, bTP0, bTP1 = [Buf() for _ in range(8)]
    banks = [(A0[:, 0:512], bA0l), (A0[:, 512:1024], bA0h), (A1[:, 0:512], bA1l),
             (A1[:, 512:1024], bA1h), (B0[:, :], bB0), (B1[:, :], bB1)]
    bank_ctr = [0]

    def next_bank():
        b = banks[bank_ctr[0] % 6]
        bank_ctr[0] += 1
        return b

    TPv = [(TPa[:, :], bTP0), (TPb[:, :], bTP1)]

    cst = ar.get(128, [3, 128], BF16)
    ident, jrev, ones = cst[:, 0, :], cst[:, 1, :], cst[:, 2, :]
    b_cst = Buf()
    biasab = ar.get(128, [20, 384], BF16)
    b_biasab = Buf()
    midab = ar.get(1, [128], BF16)
    midc = ar.get(2, [28 * 128], BF16)
    kyind = ar.get(2, [128], BF16)
    b_mid = Buf()
    epsT = ar.get(128, [1], F32)
    b_eps = Buf()
    esb = ar.get(128, [1, 8], F32)
    b_esb = Buf()
    small = ar.get(128, [16], F32)
    b_small = [Buf() for _ in range(6)]
    PERSIST = ar.off

    P.dma("poolq", lambda e: e.dma_start(out=cst, in_=consts_in.rearrange("p (a b) -> p a b", a=3)), writes=[b_cst])
    P.dma("poolq", lambda e: e.dma_start(out=biasab, in_=biasab_in.rearrange("p (a b) -> p a b", a=20)), writes=[b_biasab])
    P.dma("poolq", lambda e: e.dma_start(out=midab, in_=midab_in), writes=[b_mid])
    P.dma("poolq", lambda e: e.dma_start(out=midc, in_=midc_in), writes=[b_mid])
    P.dma("poolq", lambda e: e.dma_start(out=kyind, in_=kyind_in), writes=[b_mid])
    P.op("dve", lambda e: e.memset(epsT, 1e-6), writes=[b_eps])

    wc = {}

    def precast(name, idx):
        src = W[name][idx]
        dst = Wb[name][idx]
        rows = src.shape[0]
        bl = []
        if name in ("w_in_ab", "w_in_c"):
            for g in range(12):
                b = Buf()
                P.dma("poolq", lambda e, g=g: e.dma_start(out=dst[:, g * 512:(g + 1) * 512], in_=src[:, g * 512:(g + 1) * 512]), writes=[b])
                bl.append(b)
            wc[(name, idx)] = bl
            return
        for r0 in range(0, rows, 256):
            r1 = min(rows, r0 + 256)
            b = Buf()
            P.dma("poolq", lambda e, r0=r0, r1=r1: e.dma_start(out=dst[r0:r1, :], in_=src[r0:r1, :]), writes=[b])
            bl.append(b)
        wc[(name, idx)] = bl

    for l in range(depth):
        j = l // 2
        if l % 2 == 0:
            precast("w_in_ab", j)
            precast("w_out_ab", j)
        else:
            precast("w_in_c", j)
            precast("w_out_c", j)
        precast("w_gate", l)
        precast("w_up", l)
        precast("w_down", l)

    norm_ctr = [0]

    def norm_tile(xin, xin_buf, gbc, b_g, nb, dstT, dst_buf, col0, out32=None):
        k = norm_ctr[0] % 2
        norm_ctr[0] += 1
        sq, b_sq, hb, b_hb = nb["sq"], nb["b_sq"], nb["hb"][k], nb["b_hb"][k]
        ss, sd, rs = small[:, k:k + 1], small[:, 2 + k:3 + k], small[:, 4 + k:5 + k]
        bss, bsd, brs = b_small[k], b_small[2 + k], b_small[4 + k]
        P.op("act", lambda e: e.activation(out=sq, in_=xin, func=AF.Square), reads=[xin_buf], writes=[b_sq])
        P.op("dve", lambda e: e.tensor_reduce(out=ss, in_=sq, axis=AX.X, op=ALU.add), reads=[b_sq], writes=[bss])
        P.op("act", lambda e: e.activation(out=sd, in_=ss, func=AF.Sqrt, bias=epsT, scale=1.0 / D),
             reads=[bss, b_eps], writes=[bsd])
        P.op("dve", lambda e: e.reciprocal(out=rs, in_=sd), reads=[bsd], writes=[brs])
        if out32 is not None:
            oap, ob = out32
            P.op("dve", lambda e: e.scalar_tensor_tensor(out=oap, in0=xin, scalar=rs, in1=gbc, op0=ALU.mult, op1=ALU.mult),
                 reads=[xin_buf, brs, b_g], writes=[ob])
            return
        P.op("dve", lambda e: e.scalar_tensor_tensor(out=hb, in0=xin, scalar=rs, in1=gbc, op0=ALU.mult, op1=ALU.mult),
             reads=[xin_buf, brs, b_g], writes=[b_hb])
        for q4 in range(2):
            tpv, btp = TPv[q4 % 2]
            for jj in range(8):
                kc = q4 * 8 + jj
                P.op("pe", lambda e, jj=jj, kc=kc, tpv=tpv: e.transpose(out=tpv[:, jj * 128:(jj + 1) * 128],
                                                                      in_=hb[:, kc * 128:(kc + 1) * 128], identity=ident),
                     reads=[b_hb, b_cst], writes=[btp])
            P.op("act", lambda e, q4=q4, tpv=tpv: e.activation(out=dstT[:, q4 * 8:q4 * 8 + 8, col0:col0 + 128],
                                                               in_=tpv.rearrange("p (k c) -> p k c", c=128), func=AF.Copy),
                 reads=[btp], writes=[dst_buf])

    def alloc_norm(nhb=2):
        hbs = [ar.get(128, [D], BF16) for _ in range(nhb)]
        bhb = [Buf() for _ in range(nhb)]
        nb = {"sq": ar.get(128, [D], F32), "b_sq": Buf(),
              "hb": [hbs[i % nhb] for i in range(2)], "b_hb": [bhb[i % nhb] for i in range(2)]}
        gbc = ar.get(128, [1, D], F32)
        return nb, gbc, Buf()

    def phase_proj(l, xsrc):
        ar.off = PERSIST
        j = l // 2
        ab = (l % 2 == 0)
        wname = "w_in_ab" if ab else "w_in_c"
        Wsrc = Wb[wname][j].rearrange("(k p) n -> p k n", p=128)
        wdeps = wc[(wname, j)]
        nb, gbc, b_g = alloc_norm()
        hT = ar.get(128, [16, 2048], BF16)
        hTb = [Buf() for _ in range(16)]
        wbuf = [ar.get(128, [16, 512], BF16) for _ in range(2)]
        b_w = [Buf(), Buf()]
        stg = [ar.get(128, [2048], BF16) for _ in range(2)]
        b_stg = [Buf(), Buf()]
        vst = [ar.get(128, [512], BF16) for _ in range(2)]
        b_vst = [Buf(), Buf()]
        xt = [ar.get(128, [D], F32) for _ in range(2)]
        b_xt = [Buf(), Buf()]
        P.dma("sp", lambda e: e.dma_start(out=gbc, in_=norm_mix[l:l + 1, :].partition_broadcast(128)), writes=[b_g])
        groups = []
        if ab:
            for g in range(3):
                groups.append([("fm", DIL[g], g * 8 + c, True, c * 128) for c in range(4)])
                groups.append([("fm", DIL[g], g * 8 + 4 + c, False, c * 128) for c in range(4)])
                groups.append([("tm", DIL[g], 512 * g, 512, 0)])
            groups.append([("fm", 1, 24 + c, True, c * 128) for c in range(4)])
            groups.append([("fm", 1, 28 + c, True, c * 128) for c in range(4)])
            groups.append([("fm", 1, 32, False, 0), ("fm", 1, 33, False, 128), ("tm", 1, 1536, 256, 256)])
        else:
            for g in range(4):
                groups.append([("fm", 1, g * 4 + c, True, c * 128) for c in range(4)])
            for g in range(4):
                groups.append([("fm", 1, 16 + g * 4 + c, False, c * 128) for c in range(4)])
            for g in range(4):
                groups.append([("tm", 1, 512 * g, 512, 0)])
        ctr = {"stg": 0, "vst": 0}

        def load_w(gi):
            P.dma("sp", lambda e: e.dma_start(out=wbuf[gi % 2], in_=Wsrc[:, :, gi * 512:(gi + 1) * 512]),
                  reads=[wdeps[gi]], writes=[b_w[gi % 2]])

        for tt in range(T // 2048):
            for s in range(16):
                k = s % 2
                r0 = tt * 2048 + s * 128
                P.dma("sp", lambda e, k=k, r0=r0: e.dma_start(out=xt[k], in_=xsrc[r0:r0 + 128, :]), writes=[b_xt[k]])
                norm_tile(xt[k], b_xt[k], gbc[:, 0, :], b_g, nb, hT, hTb[s], s * 128)
            load_w(0)
            for gi in range(12):
                if gi + 1 < 12:
                    load_w(gi + 1)
                wb_, bw_ = wbuf[gi % 2], b_w[gi % 2]
                for spec in groups[gi]:
                    if spec[0] == "fm":
                        _, d, row, scaled, c0 = spec
                        si = ctr["stg"] % 2
                        ctr["stg"] += 1
                        eng = "act" if si == 0 else "dve"
                        st_ = stg[si]
                        for tq in range(4):
                            bank, bb = next_bank()
                            for kc in range(16):
                                P.op("pe", lambda e, kc=kc, bank=bank, tq=tq, c0=c0, wb_=wb_: e.matmul(
                                    bank, lhsT=wb_[:, kc, c0:c0 + 128], rhs=hT[:, kc, tq * 512:(tq + 1) * 512],
                                    start=(kc == 0), stop=(kc == 15)),
                                    reads=[bw_] + hTb[tq * 4:tq * 4 + 4], writes=[bb])
                            m0, m1 = tq * 512 // d, (tq + 1) * 512 // d
                            if d == 1:
                                oap = st_[:, m0:m1]
                                iap = bank
                            else:
                                oap = st_.rearrange("p (r m) -> p r m", r=d)[:, :, m0:m1]
                                iap = bank.rearrange("p (m r) -> p r m", r=d)
                            sc = SCALE if scaled else 1.0
                            if eng == "act":
                                P.op("act", lambda e, oap=oap, iap=iap, sc=sc: e.activation(out=oap, in_=iap, func=AF.Copy, scale=sc),
                                     reads=[bb], writes=[b_stg[si]])
                            else:
                                P.op("dve", lambda e, oap=oap, iap=iap, sc=sc: e.tensor_scalar(out=oap, in0=iap, scalar1=sc, scalar2=None, op0=ALU.mult),
                                     reads=[bb], writes=[b_stg[si]])
                        M = T // d
                        ml = 2048 // d
                        if d == 1:
                            dst = QKs[row][:, tt * 2048:(tt + 1) * 2048]
                            src = st_
                        else:
                            dst = QKs[row].rearrange("p (r m) -> p r m", r=d)[:, :, tt * ml:(tt + 1) * ml]
                            src = st_.rearrange("p (r m) -> p r m", r=d)
                        P.dma("sp", lambda e, dst=dst, src=src: e.dma_start(out=dst, in_=src), reads=[b_stg[si]])
                    else:
                        _, d, vcol, N, c0 = spec
                        nbu = 16 // d
                        for r in range(d):
                            for mb in range(nbu):
                                vi = ctr["vst"] % 2
                                ctr["vst"] += 1
                                bank, bb = next_bank()
                                t0 = r + d * 128 * mb
                                for kc in range(16):
                                    if d == 1:
                                        lap = hT[:, kc, t0:t0 + 128]
                                    else:
                                        lap = hT[:, kc, t0:t0 + d * 127 + 1:d]
                                    P.op("pe", lambda e, kc=kc, bank=bank, lap=lap, c0=c0, N=N, wb_=wb_: e.matmul(
                                        bank[:, 0:N], lhsT=lap, rhs=wb_[:, kc, c0:c0 + N], start=(kc == 0), stop=(kc == 15)),
                                        reads=[bw_] + hTb[mb * d:mb * d + d], writes=[bb])
                                if vi == 0:
                                    P.op("act", lambda e, bank=bank, N=N, vi=vi: e.activation(out=vst[vi][:, 0:N], in_=bank[:, 0:N], func=AF.Copy),
                                         reads=[bb], writes=[b_vst[vi]])
                                else:
                                    P.op("dve", lambda e, bank=bank, N=N, vi=vi: e.tensor_copy(out=vst[vi][:, 0:N], in_=bank[:, 0:N]),
                                         reads=[bb], writes=[b_vst[vi]])
                                p0 = r * (T // d) + tt * (2048 // d) + mb * 128
                                P.dma("sp", lambda e, p0=p0, vcol=vcol, N=N, vi=vi: e.dma_start(
                                    out=Vs[p0:p0 + 128, vcol:vcol + N], in_=vst[vi][:, 0:N]), reads=[b_vst[vi]])
        P.barrier()

    class AttnBufs:
        pass

    def alloc_attn(nkc):
        a = AttnBufs()
        a.Qb = [ar.get(128, [2048], BF16) for _ in range(2)]
        a.Kb = [ar.get(128, [3072], BF16) for _ in range(2)]
        a.Vb = [ar.get(128, [24, 128], BF16) for _ in range(2)]
        a.bQ = [Buf(), Buf()]
        a.bK = [Buf(), Buf()]
        a.bV = [Buf(), Buf()]
        a.PT = [ar.get(128, [1024], BF16) for _ in range(2)]
        a.bPT = [Buf(), Buf()]
        a.dtmp = [ar.get(128, [128], F32) for _ in range(2)]
        a.bdt = [Buf(), Buf()]
        a.wo = [ar.get(128, [nkc, 512], BF16) for _ in range(2)]
        a.bwo = [Buf(), Buf()]
        a.xc = [ar.get(128, [512], F32) for _ in range(3)]
        a.bxc = [Buf() for _ in range(3)]
        a.xo = [ar.get(128, [512], F32) for _ in range(2)]
        a.bxo = [Buf(), Buf()]
        a.S = [(A0, [bA0l, bA0h]), (A1, [bA1l, bA1h])]
        a.ND = [(B0, bB0), (B1, bB1)]
        a.qb = 0
        a.pending = None
        return a

    def qblock(a, parts, evac):
        i = a.qb % 2
        a.qb += 1
        S, bS = a.S[i]
        PT, bPT = a.PT[i], a.bPT[i]
        sc = 0
        for p in parts:
            n = p["n"]
            p["sc"] = sc
            so = S[:, sc:sc + n]
            P.op("pe", lambda e, so=so, p=p: e.matmul(so, lhsT=p["K"], rhs=p["Q"], start=True, stop=False),
                 reads=[p["Kb"], p["Qb"]], writes=bS)
            P.op("pe", lambda e, so=so, p=p: e.matmul(so, lhsT=jrev, rhs=p["bias"], start=False, stop=(p["mask"] is None)),
                 reads=[b_cst, p["bbuf"]], writes=bS)
            if p["mask"] is not None:
                ml, mr, mbuf = p["mask"]
                P.op("pe", lambda e, so=so, ml=ml, mr=mr: e.matmul(so, lhsT=ml, rhs=mr, start=False, stop=True),
                     reads=[b_cst, mbuf], writes=bS)
            sc += n
        tot = sc
        P.op("act", lambda e, S=S, PT=PT, tot=tot: e.activation(out=PT[:, 0:tot], in_=S[:, 0:tot], func=AF.Exp),
             reads=bS, writes=[bPT])
        cur = (i, parts, evac)
        prev = a.pending
        a.pending = cur
        if prev is not None:
            finish(a, prev)

    def finish(a, item):
        i, parts, evac = item
        PT, bPT = a.PT[i], a.bPT[i]
        nd, bnd = a.ND[i]
        last = len(parts) - 1
        for pi, p in enumerate(parts):
            n, sc, oc0 = p["n"], p["sc"], p["oc0"]
            P.op("pe", lambda e, p=p, n=n, sc=sc, oc0=oc0, pi=pi, nd=nd, PT=PT: e.matmul(
                nd[:, oc0:oc0 + n], lhsT=p["V"], rhs=PT[:, sc:sc + n], start=(pi == 0), stop=(pi == last)),
                reads=[p["Vbuf"], bPT], writes=[bnd])
        for pi, p in enumerate(parts):
            n, sc, oc0 = p["n"], p["sc"], p["oc0"]
            P.op("pe", lambda e, n=n, sc=sc, oc0=oc0, pi=pi, nd=nd, PT=PT: e.matmul(
                nd[:, 128 + oc0:128 + oc0 + n], lhsT=ones, rhs=PT[:, sc:sc + n], start=(pi == 0), stop=(pi == last)),
                reads=[b_cst, bPT], writes=[bnd])
        evac(nd, bnd, i)

    def flush(a):
        if a.pending is not None:
            finish(a, a.pending)
            a.pending = None

    def out_proj(a, wname, j, nkc, mixedT, b_mx, xsrc, tok0, ntok):
        Wsrc = Wb[wname][j].rearrange("(k p) n -> p k n", p=128)
        wdeps = wc[(wname, j)]
        ci = 0

        def load_wo(n):
            P.dma("sp", lambda e, n=n: e.dma_start(out=a.wo[n % 2], in_=Wsrc[:, :, n * 512:(n + 1) * 512]),
                  reads=wdeps, writes=[a.bwo[n % 2]])
        load_wo(0)
        for n in range(4):
            if n + 1 < 4:
                load_wo(n + 1)
            wo, bwo = a.wo[n % 2], a.bwo[n % 2]
            for sub in range(ntok // 128):
                r0 = tok0 + sub * 128
                xi, oi = ci % 3, ci % 2
                ci += 1
                P.dma("sp", lambda e, r0=r0, n=n, xi=xi: e.dma_start(out=a.xc[xi], in_=xsrc[r0:r0 + 128, n * 512:(n + 1) * 512]),
                      writes=[a.bxc[xi]])
                bank, bb = next_bank()
                for kc in range(nkc):
                    P.op("pe", lambda e, kc=kc, bank=bank, sub=sub, wo=wo: e.matmul(
                        bank, lhsT=mixedT[:, kc, sub * 128:(sub + 1) * 128], rhs=wo[:, kc, :], start=(kc == 0), stop=(kc == nkc - 1)),
                        reads=[b_mx, bwo], writes=[bb])
                P.op("dve", lambda e, bank=bank, xi=xi, oi=oi: e.tensor_tensor(out=a.xo[oi], in0=bank, in1=a.xc[xi], op=ALU.add),
                     reads=[bb, a.bxc[xi]], writes=[a.bxo[oi]])
                P.dma("sp", lambda e, r0=r0, n=n, oi=oi: e.dma_start(out=XB[r0:r0 + 128, n * 512:(n + 1) * 512], in_=a.xo[oi]),
                      reads=[a.bxo[oi]])

    def phase_attn_ab(l, xsrc):
        ar.off = PERSIST
        j = l // 2
        a = alloc_attn(12)
        mixedT = ar.get(128, [12, 2048], BF16)
        b_mx = Buf()
        accn = ar.get(128, [2048], F32)
        accd = ar.get(128, [2048], F32)
        b_acc = Buf()
        P.dma("sp", lambda e: e.dma_start(out=esb, in_=sink_b[j:j + 1, :].partition_broadcast(128)), writes=[b_esb])
        P.op("act", lambda e: e.activation(out=esb, in_=esb, func=AF.Exp), reads=[b_esb], writes=[b_esb])
        qi = [0]
        kvi = [0]
        for u in range(T // 2048):
            for s in range(4):
                for g in range(3):
                    d = DIL[g]
                    nbu = 16 // d
                    Mb = NB // d
                    midb = Mb // 2
                    RB = min(d, 4)
                    lo = max(u * nbu - 1, 0)
                    hi = min((u + 1) * nbu + 1, Mb)
                    nk = hi - lo
                    h = g * 4 + s
                    for rb in range(d // RB):
                        q_i = qi[0] % 2
                        qi[0] += 1
                        k_i = kvi[0] % 2
                        kvi[0] += 1
                        Qv = a.Qb[q_i][:, 0:RB * nbu * 128].rearrange("p (r m) -> p r m", r=RB)
                        Kv = a.Kb[k_i][:, 0:RB * nk * 128].rearrange("p (r m) -> p r m", r=RB)
                        Vv = a.Vb[k_i][:, 0:RB * nk, :].rearrange("p (r b) c -> p r b c", r=RB)
                        qsrc = QKs[g * 8 + s].rearrange("p (r m) -> p r m", r=d)[:, rb * RB:(rb + 1) * RB, u * nbu * 128:(u + 1) * nbu * 128]
                        ksrc = QKs[g * 8 + 4 + s].rearrange("p (r m) -> p r m", r=d)[:, rb * RB:(rb + 1) * RB, lo * 128:hi * 128]
                        P.dma("sp", lambda e, Qv=Qv, qsrc=qsrc: e.dma_start(out=Qv, in_=qsrc), writes=[a.bQ[q_i]])
                        P.dma("sp", lambda e, Kv=Kv, ksrc=ksrc: e.dma_start(out=Kv, in_=ksrc), writes=[a.bK[k_i]])
                        for rr in range(RB):
                            r = rb * RB + rr
                            vsrc = Vs[(r * Mb + lo) * 128:(r * Mb + hi) * 128, 512 * g + s * 128:512 * g + s * 128 + 128].rearrange("(b p) c -> p b c", p=128)
                            P.dma("sp", lambda e, rr=rr, Vv=Vv, vsrc=vsrc: e.dma_start(out=Vv[:, rr, :, :], in_=vsrc), writes=[a.bV[k_i]])
                        for rr in range(RB):
                            r = rb * RB + rr
                            for mbl in range(nbu):
                                mb = u * nbu + mbl
                                parts = []

                                def mk(kblk, qc0, n, bc0, mask):
                                    return dict(K=Kv[:, rr, (kblk - lo) * 128:(kblk - lo + 1) * 128], Kb=a.bK[k_i],
                                                Q=Qv[:, rr, mbl * 128 + qc0:mbl * 128 + qc0 + n], Qb=a.bQ[q_i], n=n,
                                                bias=biasab[:, h, bc0:bc0 + n], bbuf=b_biasab,
                                                mask=((ones[0:1, :], midab[0:1, 0:n], b_mid) if mask else None),
                                                V=Vv[:, rr, kblk - lo, :], Vbuf=a.bV[k_i], oc0=qc0)
                                parts.append(mk(mb, 0, 128, 128, False))
                                if mb > 0:
                                    parts.append(mk(mb - 1, 0, 64, 0, mb == midb))
                                if mb < Mb - 1:
                                    parts.append(mk(mb + 1, 64, 64, 320, mb == midb - 1))
                                c0 = r + d * 128 * mbl
                                if d == 1:
                                    cn = accn[:, c0:c0 + 128]
                                    cd = accd[:, c0:c0 + 128]
                                else:
                                    cn = accn[:, c0:c0 + d * 127 + 1:d]
                                    cd = accd[:, c0:c0 + d * 127 + 1:d]

                                def evac(nd, bnd, i, g=g, cn=cn, cd=cd):
                                    if g == 0:
                                        P.op("dve", lambda e: e.tensor_copy(out=cn, in_=nd[:, 0:128]), reads=[bnd], writes=[b_acc])
                                        P.op("dve", lambda e: e.tensor_copy(out=cd, in_=nd[:, 128:256]), reads=[bnd], writes=[b_acc])
                                    else:
                                        P.op("dve", lambda e: e.tensor_tensor(out=cn, in0=cn, in1=nd[:, 0:128], op=ALU.add), reads=[bnd], writes=[b_acc])
                                        P.op("dve", lambda e: e.tensor_tensor(out=cd, in0=cd, in1=nd[:, 128:256], op=ALU.add), reads=[bnd], writes=[b_acc])
                                qblock(a, parts, evac)
                flush(a)
                P.op("dve", lambda e: e.reciprocal(out=accd, in_=accd), writes=[b_acc])
                P.op("dve", lambda e, s=s: e.tensor_tensor(out=mixedT[:, s, :], in0=accn, in1=accd, op=ALU.mult), reads=[b_acc], writes=[b_mx])
            lo = max(16 * u - 1, 0)
            hi = min(16 * u + 17, NB)
            nk = hi - lo
            for kvh in range(2):
                k_i = kvi[0] % 2
                kvi[0] += 1
                Kf = a.Kb[k_i]
                Vf = a.Vb[k_i]
                P.dma("sp", lambda e, Kf=Kf, kvh=kvh, lo=lo, hi=hi, nk=nk: e.dma_start(out=Kf[:, 0:nk * 128], in_=QKs[32 + kvh][:, lo * 128:hi * 128]),
                      writes=[a.bK[k_i]])
                P.dma("sp", lambda e, Vf=Vf, kvh=kvh, lo=lo, hi=hi, nk=nk: e.dma_start(
                    out=Vf[:, 0:nk, :], in_=Vs[lo * 128:hi * 128, 1536 + kvh * 128:1536 + kvh * 128 + 128].rearrange("(b p) c -> p b c", p=128)),
                    writes=[a.bV[k_i]])
                for hq in range(4):
                    h = kvh * 4 + hq
                    q_i = qi[0] % 2
                    qi[0] += 1
                    Qf = a.Qb[q_i]
                    P.dma("sp", lambda e, Qf=Qf, h=h, u=u: e.dma_start(out=Qf, in_=QKs[24 + h][:, u * 2048:(u + 1) * 2048]), writes=[a.bQ[q_i]])
                    for mbl in range(16):
                        mb = 16 * u + mbl

                        def mk(kblk, bc0, mask):
                            return dict(K=Kf[:, (kblk - lo) * 128:(kblk - lo + 1) * 128], Kb=a.bK[k_i],
                                        Q=Qf[:, mbl * 128:(mbl + 1) * 128], Qb=a.bQ[q_i], n=128,
                                        bias=biasab[:, 12 + h, bc0:bc0 + 128], bbuf=b_biasab,
                                        mask=((ones[0:1, :], midab[0:1, :], b_mid) if mask else None),
                                        V=Vf[:, kblk - lo, :], Vbuf=a.bV[k_i], oc0=0)
                        parts = [mk(mb, 128, False)]
                        if mb > 0:
                            parts.append(mk(mb - 1, 0, mb == HALF))
                        if mb < NB - 1:
                            parts.append(mk(mb + 1, 256, mb == HALF - 1))

                        def evac(nd, bnd, i, h=h, mbl=mbl):
                            dt_, bdt = a.dtmp[i], a.bdt[i]
                            P.op("dve", lambda e: e.tensor_scalar(out=dt_, in0=nd[:, 128:256], scalar1=esb[:, 0, h:h + 1], scalar2=None, op0=ALU.add),
                                 reads=[bnd, b_esb], writes=[bdt])
                            P.op("dve", lambda e: e.reciprocal(out=dt_, in_=dt_), writes=[bdt])
                            P.op("dve", lambda e: e.tensor_tensor(out=mixedT[:, 4 + h, mbl * 128:(mbl + 1) * 128], in0=nd[:, 0:128], in1=dt_, op=ALU.mult),
                                 reads=[bnd, bdt], writes=[b_mx])
                        qblock(a, parts, evac)
            flush(a)
            out_proj(a, "w_out_ab", j, 12, mixedT, b_mx, xsrc, u * 2048, 2048)
        P.barrier()

    def phase_attn_c(l, xsrc):
        ar.off = PERSIST
        j = l // 2
        a = alloc_attn(16)
        mixedT = ar.get(128, [16, 1024], BF16)
        b_mx = Buf()
        bc = [ar.get(128, [9, 128], BF16) for _ in range(2)]
        b_bc = [Buf(), Buf()]
        bsrc = biasc_in[j].rearrange("p (h t c) -> p h t c", h=16, t=9)
        it = 0
        for u in range(T // 1024):
            lo = max(8 * u - 3, 0)
            hi = min(8 * u + 11, NB)
            nk = hi - lo
            for h in range(16):
                i2 = it % 2
                it += 1
                Qf, Kf, Vf = a.Qb[i2], a.Kb[i2], a.Vb[i2]
                P.dma("poolq", lambda e, i2=i2, h=h: e.dma_start(out=bc[i2], in_=bsrc[:, h, :, :]), writes=[b_bc[i2]])
                P.dma("sp", lambda e, Qf=Qf, h=h, u=u: e.dma_start(out=Qf[:, 0:1024], in_=QKs[h][:, u * 1024:(u + 1) * 1024]), writes=[a.bQ[i2]])
                P.dma("sp", lambda e, Kf=Kf, h=h, lo=lo, hi=hi, nk=nk: e.dma_start(out=Kf[:, 0:nk * 128], in_=QKs[16 + h][:, lo * 128:hi * 128]), writes=[a.bK[i2]])
                P.dma("sp", lambda e, Vf=Vf, h=h, lo=lo, hi=hi, nk=nk: e.dma_start(
                    out=Vf[:, 0:nk, :], in_=Vs[lo * 128:hi * 128, h * 128:(h + 1) * 128].rearrange("(b p) c -> p b c", p=128)), writes=[a.bV[i2]])
                for bl in range(8):
                    b = 8 * u + bl
                    if HALF - 2 <= b <= HALF + 1:
                        dl = [(dd, dd + 3, (b - (HALF - 2)) * 7 + dd + 3) for dd in range(-3, 4)]
                    elif b == 0:
                        dl = [(dd, dd + 3, None) for dd in range(0, 4)]
                    elif b == 1:
                        dl = [(dd, dd + 3, None) for dd in range(-1, 3)]
                    elif b == NB - 2:
                        dl = [(dd, dd + 3, None) for dd in range(-2, 2)]
                    elif b == NB - 1:
                        dl = [(dd, dd + 3, None) for dd in range(-3, 1)]
                    else:
                        dl = [(-2, 7, None), (-1, 2, None), (0, 3, None), (1, 4, None), (2, 8, None)]
                    parts = []
                    for dd, tile, mi in dl:
                        kb_ = b + dd
                        if kb_ < 0 or kb_ >= NB:
                            continue
                        parts.append(dict(K=Kf[:, (kb_ - lo) * 128:(kb_ - lo + 1) * 128], Kb=a.bK[i2],
                                          Q=Qf[:, bl * 128:(bl + 1) * 128], Qb=a.bQ[i2], n=128,
                                          bias=bc[i2][:, tile, :], bbuf=b_bc[i2],
                                          mask=((kyind[0:2, :], midc[0:2, mi * 128:(mi + 1) * 128], b_mid) if mi is not None else None),
                                          V=Vf[:, kb_ - lo, :], Vbuf=a.bV[i2], oc0=0))

                    def evac(nd, bnd, i, h=h, bl=bl):
                        dt_, bdt = a.dtmp[i], a.bdt[i]
                        P.op("dve", lambda e: e.reciprocal(out=dt_, in_=nd[:, 128:256]), reads=[bnd], writes=[bdt])
                        P.op("dve", lambda e: e.tensor_tensor(out=mixedT[:, h, bl * 128:(bl + 1) * 128], in0=nd[:, 0:128], in1=dt_, op=ALU.mult),
                             reads=[bnd, bdt], writes=[b_mx])
                    qblock(a, parts, evac)
            flush(a)
            out_proj(a, "w_out_c", j, 16, mixedT, b_mx, xsrc, u * 1024, 1024)
        P.barrier()

    def phase_ffn(l, xdst, final):
        ar.off = PERSIST
        nb, gbc, b_g = alloc_norm(1)
        xq = ar.get(128, [8, D], F32)
        xqb = [Buf() for _ in range(8)]
        hT = ar.get(128, [16, 1024], BF16)
        hTb = [Buf() for _ in range(8)]
        wg = [ar.get(128, [16, 256], BF16) for _ in range(2)]
        wu = [ar.get(128, [16, 256], BF16) for _ in range(2)]
        wd = [ar.get(128, [2, D], BF16) for _ in range(3)]
        b_wg, b_wu, b_wd = [Buf(), Buf()], [Buf(), Buf()], [Buf(), Buf(), Buf()]
        gT = [ar.get(128, [2, 1024], BF16) for _ in range(2)]
        b_gT = [Buf(), Buf()]
        sg = [ar.get(128, [512], F32) for _ in range(2)]
        b_sg = [Buf(), Buf()]
        Wg = Wb["w_gate"][l].rearrange("(k p) n -> p k n", p=128)
        Wu = Wb["w_up"][l].rearrange("(k p) n -> p k n", p=128)
        Wd = Wb["w_down"][l].rearrange("(c p) n -> p c n", p=128)
        dg, du, dd_ = wc[("w_gate", l)], wc[("w_up", l)], wc[("w_down", l)]
        sgi = [0]
        NFG = DFF // 256

        def load_f(fg):
            k = fg % 2
            P.dma("sp", lambda e: e.dma_start(out=wg[k], in_=Wg[:, :, fg * 256:(fg + 1) * 256]), reads=dg, writes=[b_wg[k]])
            P.dma("sp", lambda e: e.dma_start(out=wu[k], in_=Wu[:, :, fg * 256:(fg + 1) * 256]), reads=du, writes=[b_wu[k]])
            k3 = fg % 3
            P.dma("sp", lambda e: e.dma_start(out=wd[k3], in_=Wd[:, fg * 2:fg * 2 + 2, :]), reads=dd_, writes=[b_wd[k3]])

        def down(fg):
            k, k3 = fg % 2, fg % 3
            for sub in range(8):
                for n in range(4):
                    bank, bb = next_bank()
                    for c in range(2):
                        P.op("pe", lambda e, c=c, bank=bank, sub=sub, n=n: e.matmul(
                            bank, lhsT=gT[k][:, c, sub * 128:(sub + 1) * 128], rhs=wd[k3][:, c, n * 512:(n + 1) * 512], start=(c == 0), stop=(c == 1)),
                            reads=[b_gT[k], b_wd[k3]], writes=[bb])
                    P.op("dve", lambda e, bank=bank, sub=sub, n=n: e.tensor_tensor(
                        out=xq[:, sub, n * 512:(n + 1) * 512], in0=xq[:, sub, n * 512:(n + 1) * 512], in1=bank, op=ALU.add),
                        reads=[bb], writes=[xqb[sub]])

        for tt in range(T // 1024):
            for s in range(8):
                r0 = tt * 1024 + s * 128
                P.dma("sp", lambda e, s=s, r0=r0: e.dma_start(out=xq[:, s, :], in_=XB[r0:r0 + 128, :]), writes=[xqb[s]])
            P.dma("sp", lambda e: e.dma_start(out=gbc, in_=norm_ffn[l:l + 1, :].partition_broadcast(128)), writes=[b_g])
            load_f(0)
            for s in range(8):
                norm_tile(xq[:, s, :], xqb[s], gbc[:, 0, :], b_g, nb, hT, hTb[s], s * 128)
            if final:
                P.dma("sp", lambda e: e.dma_start(out=gbc, in_=norm_final[0:1, :].partition_broadcast(128)), writes=[b_g])
            for fg in range(NFG):
                if fg + 1 < NFG:
                    load_f(fg + 1)
                k = fg % 2
                for c in range(2):
                    for tq in range(2):
                        G, bG = next_bank()
                        U, bU = next_bank()
                        for kc in range(16):
                            P.op("pe", lambda e, kc=kc, G=G, c=c, tq=tq, k=k: e.matmul(
                                G, lhsT=wg[k][:, kc, c * 128:(c + 1) * 128], rhs=hT[:, kc, tq * 512:(tq + 1) * 512], start=(kc == 0), stop=(kc == 15)),
                                reads=[b_wg[k]] + hTb[tq * 4:tq * 4 + 4], writes=[bG])
                        for kc in range(16):
                            P.op("pe", lambda e, kc=kc, U=U, c=c, tq=tq, k=k: e.matmul(
                                U, lhsT=wu[k][:, kc, c * 128:(c + 1) * 128], rhs=hT[:, kc, tq * 512:(tq + 1) * 512], start=(kc == 0), stop=(kc == 15)),
                                reads=[b_wu[k]] + hTb[tq * 4:tq * 4 + 4], writes=[bU])
                        si = sgi[0] % 2
                        sgi[0] += 1
                        P.op("act", lambda e, G=G, si=si: e.activation(out=sg[si], in_=G, func=AF.Silu), reads=[bG], writes=[b_sg[si]])
                        P.op("dve", lambda e, U=U, si=si, c=c, tq=tq, k=k: e.tensor_tensor(
                            out=gT[k][:, c, tq * 512:(tq + 1) * 512], in0=sg[si], in1=U, op=ALU.mult),
                            reads=[b_sg[si], bU], writes=[b_gT[k]])
                if fg >= 1:
                    down(fg - 1)
            down(NFG - 1)
            for s in range(8):
                r0 = tt * 1024 + s * 128
                if final:
                    norm_tile(xq[:, s, :], xqb[s], gbc[:, 0, :], b_g, nb, None, None, 0, out32=(nb["sq"], nb["b_sq"]))
                    P.dma("sp", lambda e, r0=r0: e.dma_start(out=y_out[r0:r0 + 128, :], in_=nb["sq"]), reads=[nb["b_sq"]])
                else:
                    P.dma("sp", lambda e, s=s, r0=r0: e.dma_start(out=xdst[r0:r0 + 128, :], in_=xq[:, s, :]), reads=[xqb[s]])
        P.barrier()

    P.barrier(queues=())
    xcur = x_in
    def copy_out(src):
        for r0 in range(0, T, 1024):
            P.dma("sp", lambda e, r0=r0: e.dma_start(out=y_out[r0:r0 + 1024, :], in_=src[r0:r0 + 1024, :]))

    for l in range(depth):
        if stop == ("pre", l):
            copy_out(x_in)
            break
        phase_proj(l, xcur)
        if stop == ("proj", l):
            copy_out(x_in)
            break
        if l % 2 == 0:
            phase_attn_ab(l, xcur)
        else:
            phase_attn_c(l, xcur)
        if stop == ("mix", l):
            copy_out(XB)
            break
        phase_ffn(l, XA, l == depth - 1 and stop is None)
        xcur = XA
        if stop == ("ffn", l):
            copy_out(XA)
            break
    P.barrier()
    P.emit(nc)
    es.close()
    return nc


def _t5_bucket(rel):
    nb = 16
    max_exact = 8
    ret = (rel > 0).astype(np.int32) * nb
    n = np.abs(rel)
    large = max_exact + (np.log(np.maximum(n, 1) / max_exact) / np.log(1024 / max_exact) * (nb - max_exact)).astype(np.int32)
    large = np.minimum(large, nb - 1)
    return (ret + np.where(n < max_exact, n, large)).astype(np.int32)


def _host_tables(t5_table, rpb_c):
    t5 = np.asarray(t5_table, np.float32)
    p = np.arange(128)[:, None, None]
    oi = np.arange(3)[None, :, None]
    qq = np.arange(128)[None, None, :]
    rel = 128 * (oi - 1) + (127 - p) - qq
    biasab = np.empty((128, 20, 3, 128), np.float32)
    for h in range(20):
        d = DIL[h // 4] if h < 12 else 1
        hw = 64 if h < 12 else 128
        vals = t5[_t5_bucket(rel * d), h]
        biasab[:, h] = np.where(np.abs(rel) <= hw, vals, np.float32(NEG))
    rpb = np.asarray(rpb_c, np.float32)
    kk = 127 - np.arange(128)
    ky, kx = kk // 64, kk % 64
    c = np.arange(128)
    qy, qx = c // 64, c % 64
    cs = np.clip(qx - 8, 0, 48)
    colv = (kx[:, None] >= cs[None, :]) & (kx[:, None] < cs[None, :] + 16)
    dc = np.clip(kx[:, None] - qx[None, :] + 15, 0, 30)
    biasc = np.empty((2, 128, 16, 9, 128), np.float32)
    for ti in range(9):
        dd = ti - 3 if ti < 7 else (-2 if ti == 7 else 2)
        dr = 2 * dd + ky[:, None] - qy[None, :]
        valid = colv.copy()
        if ti >= 7:
            valid &= (dr >= -4) & (dr <= 3)
        dri = np.clip(dr + 7, 0, 14)
        vals = rpb[:, :, dri, dc]
        biasc[:, :, :, ti, :] = np.where(valid[None, None], vals, np.float32(NEG)).transpose(0, 2, 1, 3)
    return biasab.reshape(128, 20 * 384), biasc.reshape(2, 128, 16 * 9 * 128)


def _mode_masks(T, two_seq):
    NB = T // 128
    HALF = NB // 2
    rows = T // 64
    midab = np.full((1, 128), NEG if two_seq else 0.0, np.float32)

    def key_rows(r):
        if two_seq:
            hr = rows // 2
            base = 0 if r < hr else hr
            rs = base + min(max((r - base) - 4, 0), hr - 8)
        else:
            rs = min(max(r - 4, 0), rows - 8)
        return rs, rs + 8
    midc = np.zeros((2, 4, 7, 128), np.float32)
    for qi in range(4):
        b = HALF - 2 + qi
        for di in range(7):
            dd = di - 3
            for kyp in range(2):
                yk = 2 * (b + dd) + kyp
                for qy in range(2):
                    r = 2 * b + qy
                    lo, hi = key_rows(r)
                    if not (lo <= yk < hi):
                        midc[kyp, qi, di, qy * 64:(qy + 1) * 64] = NEG
    return midab, midc.reshape(2, 28 * 128)


def _consts():
    c = np.zeros((128, 3, 128), np.float32)
    c[:, 0] = np.eye(128)
    c[:, 1] = np.eye(128)[::-1]
    c[:, 2] = 1.0
    ky = np.zeros((2, 128), np.float32)
    ky[0, :64] = 1.0
    ky[1, 64:] = 1.0
    return c.reshape(128, 384), ky


def make_in_maps(xs, modes, T, weights):
    biasab, biasc = _host_tables(weights["t5_table"], weights["rpb_c"])
    consts, kyind = _consts()
    shared = {k: np.ascontiguousarray(np.asarray(weights[k], np.float32)) for k in
              ("w_in_ab", "w_out_ab", "w_in_c", "w_out_c", "w_gate", "w_up", "w_down", "norm_mix", "norm_ffn", "sink_b")}
    shared["norm_final"] = np.ascontiguousarray(np.asarray(weights["norm_final"], np.float32).reshape(1, D))
    shared.update(biasab=biasab, biasc=biasc, consts=consts, kyind=kyind)
    mm = {m: _mode_masks(T, m) for m in set(modes)}
    maps = []
    for x, m in zip(xs, modes):
        d = dict(shared)
        d["x"] = np.ascontiguousarray(x, dtype=np.float32)
        d["midab"], d["midc"] = mm[m]
        maps.append(d)
    return maps


def kernel(x_prompt, x_sample, w_in_ab, w_out_ab, sink_b, w_in_c, w_out_c, rpb_c, t5_table,
           norm_mix, norm_ffn, w_gate, w_up, w_down, norm_final):
    T = 8192
    xp = np.asarray(x_prompt, np.float32)
    xsm = np.asarray(x_sample, np.float32)
    weights = dict(w_in_ab=w_in_ab, w_out_ab=w_out_ab, sink_b=sink_b, w_in_c=w_in_c, w_out_c=w_out_c,
                   rpb_c=rpb_c, t5_table=t5_table, norm_mix=norm_mix, norm_ffn=norm_ffn,
                   w_gate=w_gate, w_up=w_up, w_down=w_down, norm_final=norm_final)
    zero = np.zeros((T, D), np.float32)
    xs = [xsm[0], xsm[1], xsm[2], xsm[3],
          np.concatenate([xp[0], xp[1]], 0), np.concatenate([xp[2], xp[3]], 0), zero, zero]
    modes = [False, False, False, False, True, True, False, False]
    nc = build(T, 4)
    in_maps = make_in_maps(xs, modes, T, weights)
    res = run_bass_kernel_spmd(nc, in_maps, core_ids=list(range(8)))
    ys = [np.asarray(r["y"]) for r in res.results]
    y_sample = np.stack(ys[0:4], 0)
    y_prompt = np.stack([ys[4][:4096], ys[4][4096:], ys[5][:4096], ys[5][4096:]], 0)
    return (y_prompt, y_sample)
```

```python
import contextlib
import numpy as np
import concourse.bass as bass
import concourse.mybir as mybir
from concourse.bass_utils import run_bass_kernel_spmd

F32 = mybir.dt.float32
BF16 = mybir.dt.bfloat16
AF = mybir.ActivationFunctionType
ALU = mybir.AluOpType
AX = mybir.AxisListType

D = 2048
KC = 16
DFF = 5632
SCALE = 128 ** -0.5
NEG = -1e30
DIL = (1, 4, 16)

COMPUTE = ("pe", "act", "dve", "pool")
QUEUES = {"sp": 8, "poolq": 8}
QUEUE_ENGINE = {"sp": "sp", "poolq": "pool"}


class Buf:
    __slots__ = ("name", "w", "rs")

    def __init__(self, name=""):
        self.name = name
        self.w = None
        self.rs = []


class Prog:
    def __init__(self):
        self.streams = {e: [] for e in ("pe", "act", "dve", "pool", "sp")}
        self.cnt = {e: 0 for e in COMPUTE}
        self.dcnt = {q: 0 for q in QUEUES}
        self.seen = {}

    def _waits_for(self, stream, toks):
        waits = {}
        for t in toks:
            if t is None:
                continue
            if t[0] == "c":
                _, eng, seq = t
                if eng == "pe" and stream == "pe":
                    continue
                key = ("c", eng)
                val = seq
            else:
                _, q, n = t
                k = QUEUES[q]
                key = ("d", q, n % k)
                val = 16 * (n // k + 1)
            if self.seen.get((stream, key), 0) >= val:
                continue
            if waits.get(key, 0) < val:
                waits[key] = val
        for key, val in waits.items():
            self.seen[(stream, key)] = val
        return list(waits.items())

    @staticmethod
    def _deps(reads, writes):
        toks = []
        for b in reads:
            toks.append(b.w)
        for b in writes:
            toks.append(b.w)
            toks.extend(b.rs)
        return toks

    @staticmethod
    def _mark(tok, reads, writes):
        for b in reads:
            b.rs.append(tok)
            if len(b.rs) > 64:
                last = {}
                for t in b.rs:
                    key = (t[0], t[1]) if t[0] == "c" else (t[0], t[1], t[2] % QUEUES[t[1]])
                    if key not in last or last[key][2] < t[2]:
                        last[key] = t
                b.rs = list(last.values())
        for b in writes:
            b.w = tok
            b.rs = []

    def op(self, eng, fn, reads=(), writes=()):
        waits = self._waits_for(eng, self._deps(reads, writes))
        self.cnt[eng] += 1
        tok = ("c", eng, self.cnt[eng])
        self.streams[eng].append((fn, waits, ("c", eng)))
        self._mark(tok, reads, writes)
        return tok

    def dma(self, q, fn, reads=(), writes=()):
        stream = QUEUE_ENGINE[q]
        n = self.dcnt[q]
        k = QUEUES[q]
        toks = self._deps(reads, writes)
        if n >= k:
            toks.append(("d", q, n - k))
        waits = self._waits_for(stream, toks)
        self.dcnt[q] += 1
        tok = ("d", q, n)
        self.streams[stream].append((fn, waits, ("d", q, n % k)))
        self._mark(tok, reads, writes)
        return tok

    def barrier(self, queues=("sp",)):
        toks = [("c", e, self.cnt[e]) for e in COMPUTE if self.cnt[e] > 0]
        for q in queues:
            n = self.dcnt[q]
            for i in range(max(0, n - QUEUES[q]), n):
                toks.append(("d", q, i))
        for stream in self.streams:
            waits = self._waits_for(stream, toks)
            if waits:
                self.streams[stream].append((None, waits, None))

    def emit(self, nc):
        with contextlib.ExitStack() as es:
            sems = {}
            for e in COMPUTE:
                sems[("c", e)] = es.enter_context(nc.semaphore("s_" + e))
            for q, k in QUEUES.items():
                for i in range(k):
                    sems[("d", q, i)] = es.enter_context(nc.semaphore("s_%s%d" % (q, i)))
            block = es.enter_context(nc.Block())
            streams = self.streams

            def run(engine, ops):
                for fn, waits, inc in ops:
                    for key, val in waits:
                        engine.wait_ge(sems[key], val)
                    if fn is None:
                        continue
                    ins = fn(engine)
                    ins.then_inc(sems[inc], 1 if inc[0] == "c" else 16)

            @block.tensor
            def _(e):
                run(e, streams["pe"])

            @block.scalar
            def _(e):
                run(e, streams["act"])

            @block.vector
            def _(e):
                run(e, streams["dve"])

            @block.gpsimd
            def _(e):
                run(e, streams["pool"])

            @block.sync
            def _(e):
                run(e, streams["sp"])


ARENA_WORDS = 53200


class Arena:
    def __init__(self, t):
        self.t = t
        self.off = 0

    def get(self, parts, free, dt):
        n = int(np.prod(free))
        words = n if dt == F32 else (n + 1) // 2
        ap = self.t[0:parts, self.off:self.off + words]
        self.off += words
        assert self.off <= ARENA_WORDS, ("arena overflow", self.off)
        if dt == BF16:
            ap = ap.bitcast(BF16)[:, 0:n]
        if len(free) == 2:
            ap = ap.rearrange("p (a b) -> p a b", a=free[0])
        elif len(free) == 3:
            ap = ap.rearrange("p (a b c) -> p a b c", a=free[0], b=free[1])
        return ap


def build(T, depth, stop=None):
    NB = T // 128
    HALF = NB // 2
    nc = bass.Bass("TRN2", target_bir_lowering=False)

    def din(name, shape, dt=F32):
        return nc.dram_tensor(name, list(shape), dt, kind="ExternalInput").ap()

    def dscr(name, shape, dt):
        return nc.dram_tensor(name, list(shape), dt, kind="Internal").ap()

    x_in = din("x", [T, D])
    W = {
        "w_in_ab": din("w_in_ab", [2, D, 6144]), "w_out_ab": din("w_out_ab", [2, 1536, D]),
        "w_in_c": din("w_in_c", [2, D, 6144]), "w_out_c": din("w_out_c", [2, D, D]),
        "w_gate": din("w_gate", [4, D, DFF]), "w_up": din("w_up", [4, D, DFF]),
        "w_down": din("w_down", [4, DFF, D]),
    }
    norm_mix = din("norm_mix", [4, D])
    norm_ffn = din("norm_ffn", [4, D])
    norm_final = din("norm_final", [1, D])
    sink_b = din("sink_b", [2, 8])
    biasab_in = din("biasab", [128, 20 * 384])
    biasc_in = din("biasc", [2, 128, 16 * 9 * 128])
    midab_in = din("midab", [1, 128])
    midc_in = din("midc", [2, 28 * 128])
    consts_in = din("consts", [128, 3 * 128])
    kyind_in = din("kyind", [2, 128])
    y_out = nc.dram_tensor("y", [T, D], F32, kind="ExternalOutput").ap()

    Wb = {k: dscr("b_" + k, v.shape, BF16) for k, v in W.items()}
    QKs = dscr("qks", [34, 128, T], BF16)
    Vs = dscr("vs", [T, 2048], BF16)
    XA = dscr("xa", [T, D], F32)
    XB = dscr("xb", [T, D], F32)

    P = Prog()
    es = contextlib.ExitStack()
    arena_t = es.enter_context(nc.sbuf_tensor("arena", [128, ARENA_WORDS], F32))
    ar = Arena(arena_t)
    A0 = es.enter_context(nc.psum_tensor("pa0", [128, 1024], F32))
    A1 = es.enter_context(nc.psum_tensor("pa1", [128, 1024], F32))
    B0 = es.enter_context(nc.psum_tensor("pb0", [128, 512], F32))
    B1 = es.enter_context(nc.psum_tensor("pb1", [128, 512], F32))
    TPa = es.enter_context(nc.psum_tensor("ptpa", [128, 1024], BF16))
    TPb = es.enter_context(nc.psum_tensor("ptpb", [128, 1024], BF16))
    bA0l, bA0h, bA1l, bA1h, bB0, bB1, bTP0, bTP1 = [Buf() for _ in range(8)]
    banks = [(A0[:, 0:512], bA0l), (A0[:, 512:1024], bA0h), (A1[:, 0:512], bA1l),
             (A1[:, 512:1024], bA1h), (B0[:, :], bB0), (B1[:, :], bB1)]
    bank_ctr = [0]

    def next_bank():
        b = banks[bank_ctr[0] % 6]
        bank_ctr[0] += 1
        return b

    TPv = [(TPa[:, :], bTP0), (TPb[:, :], bTP1)]

    cst = ar.get(128, [3, 128], BF16)
    ident, jrev, ones = cst[:, 0, :], cst[:, 1, :], cst[:, 2, :]
    b_cst = Buf()
    biasab = ar.get(128, [20, 384], BF16)
    b_biasab = Buf()
    midab = ar.get(1, [128], BF16)
    midc = ar.get(2, [28 * 128], BF16)
    kyind = ar.get(2, [128], BF16)
    b_mid = Buf()
    epsT = ar.get(128, [1], F32)
    b_eps = Buf()
    esb = ar.get(128, [1, 8], F32)
    b_esb = Buf()
    small = ar.get(128, [16], F32)
    b_small = [Buf() for _ in range(6)]
    PERSIST = ar.off

    P.dma("poolq", lambda e: e.dma_start(out=cst, in_=consts_in.rearrange("p (a b) -> p a b", a=3)), writes=[b_cst])
    P.dma("poolq", lambda e: e.dma_start(out=biasab, in_=biasab_in.rearrange("p (a b) -> p a b", a=20)), writes=[b_biasab])
    P.dma("poolq", lambda e: e.dma_start(out=midab, in_=midab_in), writes=[b_mid])
    P.dma("poolq", lambda e: e.dma_start(out=midc, in_=midc_in), writes=[b_mid])
    P.dma("poolq", lambda e: e.dma_start(out=kyind, in_=kyind_in), writes=[b_mid])
    P.op("dve", lambda e: e.memset(epsT, 1e-6), writes=[b_eps])

    wc = {}

    def precast(name, idx):
        src = W[name][idx]
        dst = Wb[name][idx]
        rows = src.shape[0]
        bl = []
        if name in ("w_in_ab", "w_in_c"):
            for g in range(12):
                b = Buf()
                P.dma("poolq", lambda e, g=g: e.dma_start(out=dst[:, g * 512:(g + 1) * 512], in_=src[:, g * 512:(g + 1) * 512]), writes=[b])
                bl.append(b)
            wc[(name, idx)] = bl
            return
        for r0 in range(0, rows, 256):
            r1 = min(rows, r0 + 256)
            b = Buf()
            P.dma("poolq", lambda e, r0=r0, r1=r1: e.dma_start(out=dst[r0:r1, :], in_=src[r0:r1, :]), writes=[b])
            bl.append(b)
        wc[(name, idx)] = bl

    for l in range(depth):
        j = l // 2
        if l % 2 == 0:
            precast("w_in_ab", j)
            precast("w_out_ab", j)
        else:
            precast("w_in_c", j)
            precast("w_out_c", j)
        precast("w_gate", l)
        precast("w_up", l)
        precast("w_down", l)

    norm_ctr = [0]

    def norm_tile(xin, xin_buf, gbc, b_g, nb, dstT, dst_buf, col0, out32=None):
        k = norm_ctr[0] % 2
        norm_ctr[0] += 1
        sq, b_sq, hb, b_hb = nb["sq"], nb["b_sq"], nb["hb"][k], nb["b_hb"][k]
        ss, sd, rs = small[:, k:k + 1], small[:, 2 + k:3 + k], small[:, 4 + k:5 + k]
        bss, bsd, brs = b_small[k], b_small[2 + k], b_small[4 + k]
        P.op("act", lambda e: e.activation(out=sq, in_=xin, func=AF.Square), reads=[xin_buf], writes=[b_sq])
        P.op("dve", lambda e: e.tensor_reduce(out=ss, in_=sq, axis=AX.X, op=ALU.add), reads=[b_sq], writes=[bss])
        P.op("act", lambda e: e.activation(out=sd, in_=ss, func=AF.Sqrt, bias=epsT, scale=1.0 / D),
             reads=[bss, b_eps], writes=[bsd])
        P.op("dve", lambda e: e.reciprocal(out=rs, in_=sd), reads=[bsd], writes=[brs])
        if out32 is not None:
            oap, ob = out32
            P.op("dve", lambda e: e.scalar_tensor_tensor(out=oap, in0=xin, scalar=rs, in1=gbc, op0=ALU.mult, op1=ALU.mult),
                 reads=[xin_buf, brs, b_g], writes=[ob])
            return
        P.op("dve", lambda e: e.scalar_tensor_tensor(out=hb, in0=xin, scalar=rs, in1=gbc, op0=ALU.mult, op1=ALU.mult),
             reads=[xin_buf, brs, b_g], writes=[b_hb])
        for q4 in range(2):
            tpv, btp = TPv[q4 % 2]
            for jj in range(8):
                kc = q4 * 8 + jj
                P.op("pe", lambda e, jj=jj, kc=kc, tpv=tpv: e.transpose(out=tpv[:, jj * 128:(jj + 1) * 128],
                                                                      in_=hb[:, kc * 128:(kc + 1) * 128], identity=ident),
                     reads=[b_hb, b_cst], writes=[btp])
            P.op("act", lambda e, q4=q4, tpv=tpv: e.activation(out=dstT[:, q4 * 8:q4 * 8 + 8, col0:col0 + 128],
                                                               in_=tpv.rearrange("p (k c) -> p k c", c=128), func=AF.Copy),
                 reads=[btp], writes=[dst_buf])

    def alloc_norm(nhb=2):
        hbs = [ar.get(128, [D], BF16) for _ in range(nhb)]
        bhb = [Buf() for _ in range(nhb)]
        nb = {"sq": ar.get(128, [D], F32), "b_sq": Buf(),
              "hb": [hbs[i % nhb] for i in range(2)], "b_hb": [bhb[i % nhb] for i in range(2)]}
        gbc = ar.get(128, [1, D], F32)
        return nb, gbc, Buf()

    def phase_proj(l, xsrc):
        ar.off = PERSIST
        j = l // 2
        ab = (l % 2 == 0)
        wname = "w_in_ab" if ab else "w_in_c"
        Wsrc = Wb[wname][j].rearrange("(k p) n -> p k n", p=128)
        wdeps = wc[(wname, j)]
        nb, gbc, b_g = alloc_norm()
        hT = ar.get(128, [16, 2048], BF16)
        hTb = [Buf() for _ in range(16)]
        wbuf = [ar.get(128, [16, 512], BF16) for _ in range(2)]
        b_w = [Buf(), Buf()]
        stg = [ar.get(128, [2048], BF16) for _ in range(2)]
        b_stg = [Buf(), Buf()]
        vst = [ar.get(128, [512], BF16) for _ in range(2)]
        b_vst = [Buf(), Buf()]
        xt = [ar.get(128, [D], F32) for _ in range(2)]
        b_xt = [Buf(), Buf()]
        P.dma("sp", lambda e: e.dma_start(out=gbc, in_=norm_mix[l:l + 1, :].partition_broadcast(128)), writes=[b_g])
        groups = []
        if ab:
            for g in range(3):
                groups.append([("fm", DIL[g], g * 8 + c, True, c * 128) for c in range(4)])
                groups.append([("fm", DIL[g], g * 8 + 4 + c, False, c * 128) for c in range(4)])
                groups.append([("tm", DIL[g], 512 * g, 512, 0)])
            groups.append([("fm", 1, 24 + c, True, c * 128) for c in range(4)])
            groups.append([("fm", 1, 28 + c, True, c * 128) for c in range(4)])
            groups.append([("fm", 1, 32, False, 0), ("fm", 1, 33, False, 128), ("tm", 1, 1536, 256, 256)])
        else:
            for g in range(4):
                groups.append([("fm", 1, g * 4 + c, True, c * 128) for c in range(4)])
            for g in range(4):
                groups.append([("fm", 1, 16 + g * 4 + c, False, c * 128) for c in range(4)])
            for g in range(4):
                groups.append([("tm", 1, 512 * g, 512, 0)])
        ctr = {"stg": 0, "vst": 0}

        def load_w(gi):
            P.dma("sp", lambda e: e.dma_start(out=wbuf[gi % 2], in_=Wsrc[:, :, gi * 512:(gi + 1) * 512]),
                  reads=[wdeps[gi]], writes=[b_w[gi % 2]])

        for tt in range(T // 2048):
            for s in range(16):
                k = s % 2
                r0 = tt * 2048 + s * 128
                P.dma("sp", lambda e, k=k, r0=r0: e.dma_start(out=xt[k], in_=xsrc[r0:r0 + 128, :]), writes=[b_xt[k]])
                norm_tile(xt[k], b_xt[k], gbc[:, 0, :], b_g, nb, hT, hTb[s], s * 128)
            load_w(0)
            for gi in range(12):
                if gi + 1 < 12:
                    load_w(gi + 1)
                wb_, bw_ = wbuf[gi % 2], b_w[gi % 2]
                for spec in groups[gi]:
                    if spec[0] == "fm":
                        _, d, row, scaled, c0 = spec
                        si = ctr["stg"] % 2
                        ctr["stg"] += 1
                        eng = "act" if si == 0 else "dve"
                        st_ = stg[si]
                        for tq in range(4):
                            bank, bb = next_bank()
                            for kc in range(16):
                                P.op("pe", lambda e, kc=kc, bank=bank, tq=tq, c0=c0, wb_=wb_: e.matmul(
                                    bank, lhsT=wb_[:, kc, c0:c0 + 128], rhs=hT[:, kc, tq * 512:(tq + 1) * 512],
                                    start=(kc == 0), stop=(kc == 15)),
                                    reads=[bw_] + hTb[tq * 4:tq * 4 + 4], writes=[bb])
                            m0, m1 = tq * 512 // d, (tq + 1) * 512 // d
                            if d == 1:
                                oap = st_[:, m0:m1]
                                iap = bank
                            else:
                                oap = st_.rearrange("p (r m) -> p r m", r=d)[:, :, m0:m1]
                                iap = bank.rearrange("p (m r) -> p r m", r=d)
                            sc = SCALE if scaled else 1.0
                            if eng == "act":
                                P.op("act", lambda e, oap=oap, iap=iap, sc=sc: e.activation(out=oap, in_=iap, func=AF.Copy, scale=sc),
                                     reads=[bb], writes=[b_stg[si]])
                            else:
                                P.op("dve", lambda e, oap=oap, iap=iap, sc=sc: e.tensor_scalar(out=oap, in0=iap, scalar1=sc, scalar2=None, op0=ALU.mult),
                                     reads=[bb], writes=[b_stg[si]])
                        M = T // d
                        ml = 2048 // d
                        if d == 1:
                            dst = QKs[row][:, tt * 2048:(tt + 1) * 2048]
                            src = st_
                        else:
                            dst = QKs[row].rearrange("p (r m) -> p r m", r=d)[:, :, tt * ml:(tt + 1) * ml]
                            src = st_.rearrange("p (r m) -> p r m", r=d)
                        P.dma("sp", lambda e, dst=dst, src=src: e.dma_start(out=dst, in_=src), reads=[b_stg[si]])
                    else:
                        _, d, vcol, N, c0 = spec
                        nbu = 16 // d
                        for r in range(d):
                            for mb in range(nbu):
                                vi = ctr["vst"] % 2
                                ctr["vst"] += 1
                                bank, bb = next_bank()
                                t0 = r + d * 128 * mb
                                for kc in range(16):
                                    if d == 1:
                                        lap = hT[:, kc, t0:t0 + 128]
                                    else:
                                        lap = hT[:, kc, t0:t0 + d * 127 + 1:d]
                                    P.op("pe", lambda e, kc=kc, bank=bank, lap=lap, c0=c0, N=N, wb_=wb_: e.matmul(
                                        bank[:, 0:N], lhsT=lap, rhs=wb_[:, kc, c0:c0 + N], start=(kc == 0), stop=(kc == 15)),
                                        reads=[bw_] + hTb[mb * d:mb * d + d], writes=[bb])
                                if vi == 0:
                                    P.op("act", lambda e, bank=bank, N=N, vi=vi: e.activation(out=vst[vi][:, 0:N], in_=bank[:, 0:N], func=AF.Copy),
                                         reads=[bb], writes=[b_vst[vi]])
                                else:
                                    P.op("dve", lambda e, bank=bank, N=N, vi=vi: e.tensor_copy(out=vst[vi][:, 0:N], in_=bank[:, 0:N]),
                                         reads=[bb], writes=[b_vst[vi]])
                                p0 = r * (T // d) + tt * (2048 // d) + mb * 128
                                P.dma("sp", lambda e, p0=p0, vcol=vcol, N=N, vi=vi: e.dma_start(
                                    out=Vs[p0:p0 + 128, vcol:vcol + N], in_=vst[vi][:, 0:N]), reads=[b_vst[vi]])
        P.barrier()

    class AttnBufs:
        pass

    def alloc_attn(nkc):
        a = AttnBufs()
        a.Qb = [ar.get(128, [2048], BF16) for _ in range(2)]
        a.Kb = [ar.get(128, [3072], BF16) for _ in range(2)]
        a.Vb = [ar.get(128, [24, 128], BF16) for _ in range(2)]
        a.bQ = [Buf(), Buf()]
        a.bK = [Buf(), Buf()]
        a.bV = [Buf(), Buf()]
        a.PT = [ar.get(128, [1024], BF16) for _ in range(2)]
        a.bPT = [Buf(), Buf()]
        a.dtmp = [ar.get(128, [128], F32) for _ in range(2)]
        a.bdt = [Buf(), Buf()]
        a.wo = [ar.get(128, [nkc, 512], BF16) for _ in range(2)]
        a.bwo = [Buf(), Buf()]
        a.xc = [ar.get(128, [512], F32) for _ in range(3)]
        a.bxc = [Buf() for _ in range(3)]
        a.xo = [ar.get(128, [512], F32) for _ in range(2)]
        a.bxo = [Buf(), Buf()]
        a.S = [(A0, [bA0l, bA0h]), (A1, [bA1l, bA1h])]
        a.ND = [(B0, bB0), (B1, bB1)]
        a.qb = 0
        a.pending = None
        return a

    def qblock(a, parts, evac):
        i = a.qb % 2
        a.qb += 1
        S, bS = a.S[i]
        PT, bPT = a.PT[i], a.bPT[i]
        sc = 0
        for p in parts:
            n = p["n"]
            p["sc"] = sc
            so = S[:, sc:sc + n]
            P.op("pe", lambda e, so=so, p=p: e.matmul(so, lhsT=p["K"], rhs=p["Q"], start=True, stop=False),
                 reads=[p["Kb"], p["Qb"]], writes=bS)
            P.op("pe", lambda e, so=so, p=p: e.matmul(so, lhsT=jrev, rhs=p["bias"], start=False, stop=(p["mask"] is None)),
                 reads=[b_cst, p["bbuf"]], writes=bS)
            if p["mask"] is not None:
                ml, mr, mbuf = p["mask"]
                P.op("pe", lambda e, so=so, ml=ml, mr=mr: e.matmul(so, lhsT=ml, rhs=mr, start=False, stop=True),
                     reads=[b_cst, mbuf], writes=bS)
            sc += n
        tot = sc
        P.op("act", lambda e, S=S, PT=PT, tot=tot: e.activation(out=PT[:, 0:tot], in_=S[:, 0:tot], func=AF.Exp),
             reads=bS, writes=[bPT])
        cur = (i, parts, evac)
        prev = a.pending
        a.pending = cur
        if prev is not None:
            finish(a, prev)

    def finish(a, item):
        i, parts, evac = item
        PT, bPT = a.PT[i], a.bPT[i]
        nd, bnd = a.ND[i]
        last = len(parts) - 1
        for pi, p in enumerate(parts):
            n, sc, oc0 = p["n"], p["sc"], p["oc0"]
            P.op("pe", lambda e, p=p, n=n, sc=sc, oc0=oc0, pi=pi, nd=nd, PT=PT: e.matmul(
                nd[:, oc0:oc0 + n], lhsT=p["V"], rhs=PT[:, sc:sc + n], start=(pi == 0), stop=(pi == last)),
                reads=[p["Vbuf"], bPT], writes=[bnd])
        for pi, p in enumerate(parts):
            n, sc, oc0 = p["n"], p["sc"], p["oc0"]
            P.op("pe", lambda e, n=n, sc=sc, oc0=oc0, pi=pi, nd=nd, PT=PT: e.matmul(
                nd[:, 128 + oc0:128 + oc0 + n], lhsT=ones, rhs=PT[:, sc:sc + n], start=(pi == 0), stop=(pi == last)),
                reads=[b_cst, bPT], writes=[bnd])
        evac(nd, bnd, i)

    def flush(a):
        if a.pending is not None:
            finish(a, a.pending)
            a.pending = None

    def out_proj(a, wname, j, nkc, mixedT, b_mx, xsrc, tok0, ntok):
        Wsrc = Wb[wname][j].rearrange("(k p) n -> p k n", p=128)
        wdeps = wc[(wname, j)]
        ci = 0

        def load_wo(n):
            P.dma("sp", lambda e, n=n: e.dma_start(out=a.wo[n % 2], in_=Wsrc[:, :, n * 512:(n + 1) * 512]),
                  reads=wdeps, writes=[a.bwo[n % 2]])
        load_wo(0)
        for n in range(4):
            if n + 1 < 4:
                load_wo(n + 1)
            wo, bwo = a.wo[n % 2], a.bwo[n % 2]
            for sub in range(ntok // 128):
                r0 = tok0 + sub * 128
                xi, oi = ci % 3, ci % 2
                ci += 1
                P.dma("sp", lambda e, r0=r0, n=n, xi=xi: e.dma_start(out=a.xc[xi], in_=xsrc[r0:r0 + 128, n * 512:(n + 1) * 512]),
                      writes=[a.bxc[xi]])
                bank, bb = next_bank()
                for kc in range(nkc):
                    P.op("pe", lambda e, kc=kc, bank=bank, sub=sub, wo=wo: e.matmul(
                        bank, lhsT=mixedT[:, kc, sub * 128:(sub + 1) * 128], rhs=wo[:, kc, :], start=(kc == 0), stop=(kc == nkc - 1)),
                        reads=[b_mx, bwo], writes=[bb])
                P.op("dve", lambda e, bank=bank, xi=xi, oi=oi: e.tensor_tensor(out=a.xo[oi], in0=bank, in1=a.xc[xi], op=ALU.add),
                     reads=[bb, a.bxc[xi]], writes=[a.bxo[oi]])
                P.dma("sp", lambda e, r0=r0, n=n, oi=oi: e.dma_start(out=XB[r0:r0 + 128, n * 512:(n + 1) * 512], in_=a.xo[oi]),
                      reads=[a.bxo[oi]])

    def phase_attn_ab(l, xsrc):
        ar.off = PERSIST
        j = l // 2
        a = alloc_attn(12)
        mixedT = ar.get(128, [12, 2048], BF16)
        b_mx = Buf()
        accn = ar.get(128, [2048], F32)
        accd = ar.get(128, [2048], F32)
        b_acc = Buf()
        P.dma("sp", lambda e: e.dma_start(out=esb, in_=sink_b[j:j + 1, :].partition_broadcast(128)), writes=[b_esb])
        P.op("act", lambda e: e.activation(out=esb, in_=esb, func=AF.Exp), reads=[b_esb], writes=[b_esb])
        qi = [0]
        kvi = [0]
        for u in range(T // 2048):
            for s in range(4):
                for g in range(3):
                    d = DIL[g]
                    nbu = 16 // d
                    Mb = NB // d
                    midb = Mb // 2
                    RB = min(d, 4)
                    lo = max(u * nbu - 1, 0)
                    hi = min((u + 1) * nbu + 1, Mb)
                    nk = hi - lo
                    h = g * 4 + s
                    for rb in range(d // RB):
                        q_i = qi[0] % 2
                        qi[0] += 1
                        k_i = kvi[0] % 2
                        kvi[0] += 1
                        Qv = a.Qb[q_i][:, 0:RB * nbu * 128].rearrange("p (r m) -> p r m", r=RB)
                        Kv = a.Kb[k_i][:, 0:RB * nk * 128].rearrange("p (r m) -> p r m", r=RB)
                        Vv = a.Vb[k_i][:, 0:RB * nk, :].rearrange("p (r b) c -> p r b c", r=RB)
                        qsrc = QKs[g * 8 + s].rearrange("p (r m) -> p r m", r=d)[:, rb * RB:(rb + 1) * RB, u * nbu * 128:(u + 1) * nbu * 128]
                        ksrc = QKs[g * 8 + 4 + s].rearrange("p (r m) -> p r m", r=d)[:, rb * RB:(rb + 1) * RB, lo * 128:hi * 128]
                        P.dma("sp", lambda e, Qv=Qv, qsrc=qsrc: e.dma_start(out=Qv, in_=qsrc), writes=[a.bQ[q_i]])
                        P.dma("sp", lambda e, Kv=Kv, ksrc=ksrc: e.dma_start(out=Kv, in_=ksrc), writes=[a.bK[k_i]])
                        for rr in range(RB):
                            r = rb * RB + rr
                            vsrc = Vs[(r * Mb + lo) * 128:(r * Mb + hi) * 128, 512 * g + s * 128:512 * g + s * 128 + 128].rearrange("(b p) c -> p b c", p=128)
                            P.dma("sp", lambda e, rr=rr, Vv=Vv, vsrc=vsrc: e.dma_start(out=Vv[:, rr, :, :], in_=vsrc), writes=[a.bV[k_i]])
                        for rr in range(RB):
                            r = rb * RB + rr
                            for mbl in range(nbu):
                                mb = u * nbu + mbl
                                parts = []

                                def mk(kblk, qc0, n, bc0, mask):
                                    return dict(K=Kv[:, rr, (kblk - lo) * 128:(kblk - lo + 1) * 128], Kb=a.bK[k_i],
                                                Q=Qv[:, rr, mbl * 128 + qc0:mbl * 128 + qc0 + n], Qb=a.bQ[q_i], n=n,
                                                bias=biasab[:, h, bc0:bc0 + n], bbuf=b_biasab,
                                                mask=((ones[0:1, :], midab[0:1, 0:n], b_mid) if mask else None),
                                                V=Vv[:, rr, kblk - lo, :], Vbuf=a.bV[k_i], oc0=qc0)
                                parts.append(mk(mb, 0, 128, 128, False))
                                if mb > 0:
                                    parts.append(mk(mb - 1, 0, 64, 0, mb == midb))
                                if mb < Mb - 1:
                                    parts.append(mk(mb + 1, 64, 64, 320, mb == midb - 1))
                                c0 = r + d * 128 * mbl
                                if d == 1:
                                    cn = accn[:, c0:c0 + 128]
                                    cd = accd[:, c0:c0 + 128]
                                else:
                                    cn = accn[:, c0:c0 + d * 127 + 1:d]
                                    cd = accd[:, c0:c0 + d * 127 + 1:d]

                                def evac(nd, bnd, i, g=g, cn=cn, cd=cd):
                                    if g == 0:
                                        P.op("dve", lambda e: e.tensor_copy(out=cn, in_=nd[:, 0:128]), reads=[bnd], writes=[b_acc])
                                        P.op("dve", lambda e: e.tensor_copy(out=cd, in_=nd[:, 128:256]), reads=[bnd], writes=[b_acc])
                                    else:
                                        P.op("dve", lambda e: e.tensor_tensor(out=cn, in0=cn, in1=nd[:, 0:128], op=ALU.add), reads=[bnd], writes=[b_acc])
                                        P.op("dve", lambda e: e.tensor_tensor(out=cd, in0=cd, in1=nd[:, 128:256], op=ALU.add), reads=[bnd], writes=[b_acc])
                                qblock(a, parts, evac)
                flush(a)
                P.op("dve", lambda e: e.reciprocal(out=accd, in_=accd), writes=[b_acc])
                P.op("dve", lambda e, s=s: e.tensor_tensor(out=mixedT[:, s, :], in0=accn, in1=accd, op=ALU.mult), reads=[b_acc], writes=[b_mx])
            lo = max(16 * u - 1, 0)
            hi = min(16 * u + 17, NB)
            nk = hi - lo
            for kvh in range(2):
                k_i = kvi[0] % 2
                kvi[0] += 1
                Kf = a.Kb[k_i]
                Vf = a.Vb[k_i]
                P.dma("sp", lambda e, Kf=Kf, kvh=kvh, lo=lo, hi=hi, nk=nk: e.dma_start(out=Kf[:, 0:nk * 128], in_=QKs[32 + kvh][:, lo * 128:hi * 128]),
                      writes=[a.bK[k_i]])
                P.dma("sp", lambda e, Vf=Vf, kvh=kvh, lo=lo, hi=hi, nk=nk: e.dma_start(
                    out=Vf[:, 0:nk, :], in_=Vs[lo * 128:hi * 128, 1536 + kvh * 128:1536 + kvh * 128 + 128].rearrange("(b p) c -> p b c", p=128)),
                    writes=[a.bV[k_i]])
                for hq in range(4):
                    h = kvh * 4 + hq
                    q_i = qi[0] % 2
                    qi[0] += 1
                    Qf = a.Qb[q_i]
                    P.dma("sp", lambda e, Qf=Qf, h=h, u=u: e.dma_start(out=Qf, in_=QKs[24 + h][:, u * 2048:(u + 1) * 2048]), writes=[a.bQ[q_i]])
                    for mbl in range(16):
                        mb = 16 * u + mbl

                        def mk(kblk, bc0, mask):
                            return dict(K=Kf[:, (kblk - lo) * 128:(kblk - lo + 1) * 128], Kb=a.bK[k_i],
                                        Q=Qf[:, mbl * 128:(mbl + 1) * 128], Qb=a.bQ[q_i], n=128,
                                        bias=biasab[:, 12 + h, bc0:bc0 + 128], bbuf=b_biasab,
                                        mask=((ones[0:1, :], midab[0:1, :], b_mid) if mask else None),
                                        V=Vf[:, kblk - lo, :], Vbuf=a.bV[k_i], oc0=0)
                        parts = [mk(mb, 128, False)]
                        if mb > 0:
                            parts.append(mk(mb - 1, 0, mb == HALF))
                        if mb < NB - 1:
                            parts.append(mk(mb + 1, 256, mb == HALF - 1))

                        def evac(nd, bnd, i, h=h, mbl=mbl):
                            dt_, bdt = a.dtmp[i], a.bdt[i]
                            P.op("dve", lambda e: e.tensor_scalar(out=dt_, in0=nd[:, 128:256], scalar1=esb[:, 0, h:h + 1], scalar2=None, op0=ALU.add),
                                 reads=[bnd, b_esb], writes=[bdt])
                            P.op("dve", lambda e: e.reciprocal(out=dt_, in_=dt_), writes=[bdt])
                            P.op("dve", lambda e: e.tensor_tensor(out=mixedT[:, 4 + h, mbl * 128:(mbl + 1) * 128], in0=nd[:, 0:128], in1=dt_, op=ALU.mult),
                                 reads=[bnd, bdt], writes=[b_mx])
                        qblock(a, parts, evac)
            flush(a)
            out_proj(a, "w_out_ab", j, 12, mixedT, b_mx, xsrc, u * 2048, 2048)
        P.barrier()

    def phase_attn_c(l, xsrc):
        ar.off = PERSIST
        j = l // 2
        a = alloc_attn(16)
        mixedT = ar.get(128, [16, 1024], BF16)
        b_mx = Buf()
        bc = [ar.get(128, [9, 128], BF16) for _ in range(2)]
        b_bc = [Buf(), Buf()]
        bsrc = biasc_in[j].rearrange("p (h t c) -> p h t c", h=16, t=9)
        it = 0
        for u in range(T // 1024):
            lo = max(8 * u - 3, 0)
            hi = min(8 * u + 11, NB)
            nk = hi - lo
            for h in range(16):
                i2 = it % 2
                it += 1
                Qf, Kf, Vf = a.Qb[i2], a.Kb[i2], a.Vb[i2]
                P.dma("poolq", lambda e, i2=i2, h=h: e.dma_start(out=bc[i2], in_=bsrc[:, h, :, :]), writes=[b_bc[i2]])
                P.dma("sp", lambda e, Qf=Qf, h=h, u=u: e.dma_start(out=Qf[:, 0:1024], in_=QKs[h][:, u * 1024:(u + 1) * 1024]), writes=[a.bQ[i2]])
                P.dma("sp", lambda e, Kf=Kf, h=h, lo=lo, hi=hi, nk=nk: e.dma_start(out=Kf[:, 0:nk * 128], in_=QKs[16 + h][:, lo * 128:hi * 128]), writes=[a.bK[i2]])
                P.dma("sp", lambda e, Vf=Vf, h=h, lo=lo, hi=hi, nk=nk: e.dma_start(
                    out=Vf[:, 0:nk, :], in_=Vs[lo * 128:hi * 128, h * 128:(h + 1) * 128].rearrange("(b p) c -> p b c", p=128)), writes=[a.bV[i2]])
                for bl in range(8):
                    b = 8 * u + bl
                    if HALF - 2 <= b <= HALF + 1:
                        dl = [(dd, dd + 3, (b - (HALF - 2)) * 7 + dd + 3) for dd in range(-3, 4)]
                    elif b == 0:
                        dl = [(dd, dd + 3, None) for dd in range(0, 4)]
                    elif b == 1:
                        dl = [(dd, dd + 3, None) for dd in range(-1, 3)]
                    elif b == NB - 2:
                        dl = [(dd, dd + 3, None) for dd in range(-2, 2)]
                    elif b == NB - 1:
                        dl = [(dd, dd + 3, None) for dd in range(-3, 1)]
                    else:
                        dl = [(-2, 7, None), (-1, 2, None), (0, 3, None), (1, 4, None), (2, 8, None)]
                    parts = []
                    for dd, tile, mi in dl:
                        kb_ = b + dd
                        if kb_ < 0 or kb_ >= NB:
                            continue
                        parts.append(dict(K=Kf[:, (kb_ - lo) * 128:(kb_ - lo + 1) * 128], Kb=a.bK[i2],
                                          Q=Qf[:, bl * 128:(bl + 1) * 128], Qb=a.bQ[i2], n=128,
                                          bias=bc[i2][:, tile, :], bbuf=b_bc[i2],
                                          mask=((kyind[0:2, :], midc[0:2, mi * 128:(mi + 1) * 128], b_mid) if mi is not None else None),
                                          V=Vf[:, kb_ - lo, :], Vbuf=a.bV[i2], oc0=0))

                    def evac(nd, bnd, i, h=h, bl=bl):
                        dt_, bdt = a.dtmp[i], a.bdt[i]
                        P.op("dve", lambda e: e.reciprocal(out=dt_, in_=nd[:, 128:256]), reads=[bnd], writes=[bdt])
                        P.op("dve", lambda e: e.tensor_tensor(out=mixedT[:, h, bl * 128:(bl + 1) * 128], in0=nd[:, 0:128], in1=dt_, op=ALU.mult),
                             reads=[bnd, bdt], writes=[b_mx])
                    qblock(a, parts, evac)
            flush(a)
            out_proj(a, "w_out_c", j, 16, mixedT, b_mx, xsrc, u * 1024, 1024)
        P.barrier()

    def phase_ffn(l, xdst, final):
        ar.off = PERSIST
        nb, gbc, b_g = alloc_norm(1)
        xq = ar.get(128, [8, D], F32)
        xqb = [Buf() for _ in range(8)]
        hT = ar.get(128, [16, 1024], BF16)
        hTb = [Buf() for _ in range(8)]
        wg = [ar.get(128, [16, 256], BF16) for _ in range(2)]
        wu = [ar.get(128, [16, 256], BF16) for _ in range(2)]
        wd = [ar.get(128, [2, D], BF16) for _ in range(3)]
        b_wg, b_wu, b_wd = [Buf(), Buf()], [Buf(), Buf()], [Buf(), Buf(), Buf()]
        gT = [ar.get(128, [2, 1024], BF16) for _ in range(2)]
        b_gT = [Buf(), Buf()]
        sg = [ar.get(128, [512], F32) for _ in range(2)]
        b_sg = [Buf(), Buf()]
        Wg = Wb["w_gate"][l].rearrange("(k p) n -> p k n", p=128)
        Wu = Wb["w_up"][l].rearrange("(k p) n -> p k n", p=128)
        Wd = Wb["w_down"][l].rearrange("(c p) n -> p c n", p=128)
        dg, du, dd_ = wc[("w_gate", l)], wc[("w_up", l)], wc[("w_down", l)]
        sgi = [0]
        NFG = DFF // 256

        def load_f(fg):
            k = fg % 2
            P.dma("sp", lambda e: e.dma_start(out=wg[k], in_=Wg[:, :, fg * 256:(fg + 1) * 256]), reads=dg, writes=[b_wg[k]])
            P.dma("sp", lambda e: e.dma_start(out=wu[k], in_=Wu[:, :, fg * 256:(fg + 1) * 256]), reads=du, writes=[b_wu[k]])
            k3 = fg % 3
            P.dma("sp", lambda e: e.dma_start(out=wd[k3], in_=Wd[:, fg * 2:fg * 2 + 2, :]), reads=dd_, writes=[b_wd[k3]])

        gu = [0]
        dj = [0]

        def down_job(fg, sub, n):
            k, k3 = fg % 2, fg % 3
            bank, bb = banks[4 + dj[0] % 2]
            dj[0] += 1
            for c in range(2):
                P.op("pe", lambda e, c=c: e.matmul(
                    bank, lhsT=gT[k][:, c, sub * 128:(sub + 1) * 128], rhs=wd[k3][:, c, n * 512:(n + 1) * 512], start=(c == 0), stop=(c == 1)),
                    reads=[b_gT[k], b_wd[k3]], writes=[bb])
            P.op("dve", lambda e: e.tensor_tensor(
                out=xq[:, sub, n * 512:(n + 1) * 512], in0=xq[:, sub, n * 512:(n + 1) * 512], in1=bank, op=ALU.add),
                reads=[bb], writes=[xqb[sub]])

        def down(fg):
            for sub in range(8):
                for n in range(4):
                    down_job(fg, sub, n)

        for tt in range(T // 1024):
            for s in range(8):
                r0 = tt * 1024 + s * 128
                P.dma("sp", lambda e, s=s, r0=r0: e.dma_start(out=xq[:, s, :], in_=XB[r0:r0 + 128, :]), writes=[xqb[s]])
            P.dma("sp", lambda e: e.dma_start(out=gbc, in_=norm_ffn[l:l + 1, :].partition_broadcast(128)), writes=[b_g])
            load_f(0)
            for s in range(8):
                norm_tile(xq[:, s, :], xqb[s], gbc[:, 0, :], b_g, nb, hT, hTb[s], s * 128)
            if final:
                P.dma("sp", lambda e: e.dma_start(out=gbc, in_=norm_final[0:1, :].partition_broadcast(128)), writes=[b_g])
            for fg in range(NFG):
                if fg + 1 < NFG:
                    load_f(fg + 1)
                k = fg % 2
                jobs = [(fg - 1, sub, n) for sub in range(8) for n in range(4)] if fg >= 1 else []
                for c in range(2):
                    for tq in range(2):
                        (G, bG), (U, bU) = (banks[0], banks[1]) if gu[0] % 2 == 0 else (banks[2], banks[3])
                        gu[0] += 1
                        for kc in range(16):
                            P.op("pe", lambda e, kc=kc, G=G, c=c, tq=tq, k=k: e.matmul(
                                G, lhsT=wg[k][:, kc, c * 128:(c + 1) * 128], rhs=hT[:, kc, tq * 512:(tq + 1) * 512], start=(kc == 0), stop=(kc == 15)),
                                reads=[b_wg[k]] + hTb[tq * 4:tq * 4 + 4], writes=[bG])
                            if kc % 8 == 7:
                                for _ in range(2):
                                    if jobs:
                                        down_job(*jobs.pop(0))
                        for kc in range(16):
                            P.op("pe", lambda e, kc=kc, U=U, c=c, tq=tq, k=k: e.matmul(
                                U, lhsT=wu[k][:, kc, c * 128:(c + 1) * 128], rhs=hT[:, kc, tq * 512:(tq + 1) * 512], start=(kc == 0), stop=(kc == 15)),
                                reads=[b_wu[k]] + hTb[tq * 4:tq * 4 + 4], writes=[bU])
                            if kc % 8 == 7:
                                for _ in range(2):
                                    if jobs:
                                        down_job(*jobs.pop(0))
                        si = sgi[0] % 2
                        sgi[0] += 1
                        P.op("act", lambda e, G=G, si=si: e.activation(out=sg[si], in_=G, func=AF.Silu), reads=[bG], writes=[b_sg[si]])
                        P.op("dve", lambda e, U=U, si=si, c=c, tq=tq, k=k: e.tensor_tensor(
                            out=gT[k][:, c, tq * 512:(tq + 1) * 512], in0=sg[si], in1=U, op=ALU.mult),
                            reads=[b_sg[si], bU], writes=[b_gT[k]])
                assert not jobs
            down(NFG - 1)
            for s in range(8):
                r0 = tt * 1024 + s * 128
                if final:
                    norm_tile(xq[:, s, :], xqb[s], gbc[:, 0, :], b_g, nb, None, None, 0, out32=(nb["sq"], nb["b_sq"]))
                    P.dma("sp", lambda e, r0=r0: e.dma_start(out=y_out[r0:r0 + 128, :], in_=nb["sq"]), reads=[nb["b_sq"]])
                else:
                    P.dma("sp", lambda e, s=s, r0=r0: e.dma_start(out=xdst[r0:r0 + 128, :], in_=xq[:, s, :]), reads=[xqb[s]])
        P.barrier()

    P.barrier(queues=())
    xcur = x_in
    def copy_out(src):
        for r0 in range(0, T, 1024):
            P.dma("sp", lambda e, r0=r0: e.dma_start(out=y_out[r0:r0 + 1024, :], in_=src[r0:r0 + 1024, :]))

    for l in range(depth):
        if stop == ("pre", l):
            copy_out(x_in)
            break
        phase_proj(l, xcur)
        if stop == ("proj", l):
            copy_out(x_in)
            break
        if l % 2 == 0:
            phase_attn_ab(l, xcur)
        else:
            phase_attn_c(l, xcur)
        if stop == ("mix", l):
            copy_out(XB)
            break
        phase_ffn(l, XA, l == depth - 1 and stop is None)
        xcur = XA
        if stop == ("ffn", l):
            copy_out(XA)
            break
    P.barrier()
    P.emit(nc)
    es.close()
    return nc


def _t5_bucket(rel):
    nb = 16
    max_exact = 8
    ret = (rel > 0).astype(np.int32) * nb
    n = np.abs(rel)
    large = max_exact + (np.log(np.maximum(n, 1) / max_exact) / np.log(1024 / max_exact) * (nb - max_exact)).astype(np.int32)
    large = np.minimum(large, nb - 1)
    return (ret + np.where(n < max_exact, n, large)).astype(np.int32)


def _host_tables(t5_table, rpb_c):
    t5 = np.asarray(t5_table, np.float32)
    p = np.arange(128)[:, None, None]
    oi = np.arange(3)[None, :, None]
    qq = np.arange(128)[None, None, :]
    rel = 128 * (oi - 1) + (127 - p) - qq
    biasab = np.empty((128, 20, 3, 128), np.float32)
    for h in range(20):
        d = DIL[h // 4] if h < 12 else 1
        hw = 64 if h < 12 else 128
        vals = t5[_t5_bucket(rel * d), h]
        biasab[:, h] = np.where(np.abs(rel) <= hw, vals, np.float32(NEG))
    rpb = np.asarray(rpb_c, np.float32)
    kk = 127 - np.arange(128)
    ky, kx = kk // 64, kk % 64
    c = np.arange(128)
    qy, qx = c // 64, c % 64
    cs = np.clip(qx - 8, 0, 48)
    colv = (kx[:, None] >= cs[None, :]) & (kx[:, None] < cs[None, :] + 16)
    dc = np.clip(kx[:, None] - qx[None, :] + 15, 0, 30)
    biasc = np.empty((2, 128, 16, 9, 128), np.float32)
    for ti in range(9):
        dd = ti - 3 if ti < 7 else (-2 if ti == 7 else 2)
        dr = 2 * dd + ky[:, None] - qy[None, :]
        valid = colv.copy()
        if ti >= 7:
            valid &= (dr >= -4) & (dr <= 3)
        dri = np.clip(dr + 7, 0, 14)
        vals = rpb[:, :, dri, dc]
        biasc[:, :, :, ti, :] = np.where(valid[None, None], vals, np.float32(NEG)).transpose(0, 2, 1, 3)
    return biasab.reshape(128, 20 * 384), biasc.reshape(2, 128, 16 * 9 * 128)


def _mode_masks(T, two_seq):
    NB = T // 128
    HALF = NB // 2
    rows = T // 64
    midab = np.full((1, 128), NEG if two_seq else 0.0, np.float32)

    def key_rows(r):
        if two_seq:
            hr = rows // 2
            base = 0 if r < hr else hr
            rs = base + min(max((r - base) - 4, 0), hr - 8)
        else:
            rs = min(max(r - 4, 0), rows - 8)
        return rs, rs + 8
    midc = np.zeros((2, 4, 7, 128), np.float32)
    for qi in range(4):
        b = HALF - 2 + qi
        for di in range(7):
            dd = di - 3
            for kyp in range(2):
                yk = 2 * (b + dd) + kyp
                for qy in range(2):
                    r = 2 * b + qy
                    lo, hi = key_rows(r)
                    if not (lo <= yk < hi):
                        midc[kyp, qi, di, qy * 64:(qy + 1) * 64] = NEG
    return midab, midc.reshape(2, 28 * 128)


def _consts():
    c = np.zeros((128, 3, 128), np.float32)
    c[:, 0] = np.eye(128)
    c[:, 1] = np.eye(128)[::-1]
    c[:, 2] = 1.0
    ky = np.zeros((2, 128), np.float32)
    ky[0, :64] = 1.0
    ky[1, 64:] = 1.0
    return c.reshape(128, 384), ky


def make_in_maps(xs, modes, T, weights):
    biasab, biasc = _host_tables(weights["t5_table"], weights["rpb_c"])
    consts, kyind = _consts()
    shared = {k: np.ascontiguousarray(np.asarray(weights[k], np.float32)) for k in
              ("w_in_ab", "w_out_ab", "w_in_c", "w_out_c", "w_gate", "w_up", "w_down", "norm_mix", "norm_ffn", "sink_b")}
    shared["norm_final"] = np.ascontiguousarray(np.asarray(weights["norm_final"], np.float32).reshape(1, D))
    shared.update(biasab=biasab, biasc=biasc, consts=consts, kyind=kyind)
    mm = {m: _mode_masks(T, m) for m in set(modes)}
    maps = []
    for x, m in zip(xs, modes):
        d = dict(shared)
        d["x"] = np.ascontiguousarray(x, dtype=np.float32)
        d["midab"], d["midc"] = mm[m]
        maps.append(d)
    return maps


def kernel(x_prompt, x_sample, w_in_ab, w_out_ab, sink_b, w_in_c, w_out_c, rpb_c, t5_table,
           norm_mix, norm_ffn, w_gate, w_up, w_down, norm_final):
    T = 8192
    xp = np.asarray(x_prompt, np.float32)
    xsm = np.asarray(x_sample, np.float32)
    weights = dict(w_in_ab=w_in_ab, w_out_ab=w_out_ab, sink_b=sink_b, w_in_c=w_in_c, w_out_c=w_out_c,
                   rpb_c=rpb_c, t5_table=t5_table, norm_mix=norm_mix, norm_ffn=norm_ffn,
                   w_gate=w_gate, w_up=w_up, w_down=w_down, norm_final=norm_final)
    zero = np.zeros((T, D), np.float32)
    xs = [xsm[0], xsm[1], xsm[2], xsm[3],
          np.concatenate([xp[0], xp[1]], 0), np.concatenate([xp[2], xp[3]], 0), zero, zero]
    modes = [False, False, False, False, True, True, False, False]
    nc = build(T, 4)
    in_maps = make_in_maps(xs, modes, T, weights)
    res = run_bass_kernel_spmd(nc, in_maps, core_ids=list(range(8)))
    ys = [np.asarray(r["y"]) for r in res.results]
    y_sample = np.stack(ys[0:4], 0)
    y_prompt = np.stack([ys[4][:4096], ys[4][4096:], ys[5][:4096], ys[5][4096:]], 0)
    return (y_prompt, y_sample)
```

```python
import contextlib
import numpy as np
import concourse.bass as bass
import concourse.mybir as mybir
from concourse.bass_utils import run_bass_kernel_spmd

F32 = mybir.dt.float32
BF16 = mybir.dt.bfloat16
AF = mybir.ActivationFunctionType
ALU = mybir.AluOpType
AX = mybir.AxisListType

D = 2048
KC = 16
DFF = 5632
SCALE = 128 ** -0.5
NEG = -1e30
DIL = (1, 4, 16)

COMPUTE = ("pe", "act", "dve", "pool")
QUEUES = {"sp": 8, "poolq": 8}
QUEUE_ENGINE = {"sp": "sp", "poolq": "pool"}


class Buf:
    __slots__ = ("name", "w", "rs")

    def __init__(self, name=""):
        self.name = name
        self.w = None
        self.rs = []


class Prog:
    def __init__(self):
        self.streams = {e: [] for e in ("pe", "act", "dve", "pool", "sp")}
        self.cnt = {e: 0 for e in COMPUTE}
        self.dcnt = {q: 0 for q in QUEUES}
        self.seen = {}

    def _waits_for(self, stream, toks):
        waits = {}
        for t in toks:
            if t is None:
                continue
            if t[0] == "c":
                _, eng, seq = t
                if eng == "pe" and stream == "pe":
                    continue
                key = ("c", eng)
                val = seq
            else:
                _, q, n = t
                k = QUEUES[q]
                key = ("d", q, n % k)
                val = 16 * (n // k + 1)
            if self.seen.get((stream, key), 0) >= val:
                continue
            if waits.get(key, 0) < val:
                waits[key] = val
        for key, val in waits.items():
            self.seen[(stream, key)] = val
        return list(waits.items())

    @staticmethod
    def _deps(reads, writes):
        toks = []
        for b in reads:
            toks.append(b.w)
        for b in writes:
            toks.append(b.w)
            toks.extend(b.rs)
        return toks

    @staticmethod
    def _mark(tok, reads, writes):
        for b in reads:
            b.rs.append(tok)
            if len(b.rs) > 64:
                last = {}
                for t in b.rs:
                    key = (t[0], t[1]) if t[0] == "c" else (t[0], t[1], t[2] % QUEUES[t[1]])
                    if key not in last or last[key][2] < t[2]:
                        last[key] = t
                b.rs = list(last.values())
        for b in writes:
            b.w = tok
            b.rs = []

    def op(self, eng, fn, reads=(), writes=()):
        waits = self._waits_for(eng, self._deps(reads, writes))
        self.cnt[eng] += 1
        tok = ("c", eng, self.cnt[eng])
        self.streams[eng].append((fn, waits, ("c", eng)))
        self._mark(tok, reads, writes)
        return tok

    def dma(self, q, fn, reads=(), writes=()):
        stream = QUEUE_ENGINE[q]
        n = self.dcnt[q]
        k = QUEUES[q]
        toks = self._deps(reads, writes)
        if n >= k:
            toks.append(("d", q, n - k))
        waits = self._waits_for(stream, toks)
        self.dcnt[q] += 1
        tok = ("d", q, n)
        self.streams[stream].append((fn, waits, ("d", q, n % k)))
        self._mark(tok, reads, writes)
        return tok

    def barrier(self, queues=("sp",)):
        toks = [("c", e, self.cnt[e]) for e in COMPUTE if self.cnt[e] > 0]
        for q in queues:
            n = self.dcnt[q]
            for i in range(max(0, n - QUEUES[q]), n):
                toks.append(("d", q, i))
        for stream in self.streams:
            waits = self._waits_for(stream, toks)
            if waits:
                self.streams[stream].append((None, waits, None))

    def emit(self, nc):
        with contextlib.ExitStack() as es:
            sems = {}
            for e in COMPUTE:
                sems[("c", e)] = es.enter_context(nc.semaphore("s_" + e))
            for q, k in QUEUES.items():
                for i in range(k):
                    sems[("d", q, i)] = es.enter_context(nc.semaphore("s_%s%d" % (q, i)))
            block = es.enter_context(nc.Block())
            streams = self.streams

            def run(engine, ops):
                for fn, waits, inc in ops:
                    for key, val in waits:
                        engine.wait_ge(sems[key], val)
                    if fn is None:
                        continue
                    ins = fn(engine)
                    ins.then_inc(sems[inc], 1 if inc[0] == "c" else 16)

            @block.tensor
            def _(e):
                run(e, streams["pe"])

            @block.scalar
            def _(e):
                run(e, streams["act"])

            @block.vector
            def _(e):
                run(e, streams["dve"])

            @block.gpsimd
            def _(e):
                run(e, streams["pool"])

            @block.sync
            def _(e):
                run(e, streams["sp"])


ARENA_WORDS = 53200


class Arena:
    def __init__(self, t):
        self.t = t
        self.off = 0

    def get(self, parts, free, dt):
        n = int(np.prod(free))
        words = n if dt == F32 else (n + 1) // 2
        ap = self.t[0:parts, self.off:self.off + words]
        self.off += words
        assert self.off <= ARENA_WORDS, ("arena overflow", self.off)
        if dt == BF16:
            ap = ap.bitcast(BF16)[:, 0:n]
        if len(free) == 2:
            ap = ap.rearrange("p (a b) -> p a b", a=free[0])
        elif len(free) == 3:
            ap = ap.rearrange("p (a b c) -> p a b c", a=free[0], b=free[1])
        return ap


def build(T, depth, stop=None):
    NB = T // 128
    HALF = NB // 2
    nc = bass.Bass("TRN2", target_bir_lowering=False)

    def din(name, shape, dt=F32):
        return nc.dram_tensor(name, list(shape), dt, kind="ExternalInput").ap()

    def dscr(name, shape, dt):
        return nc.dram_tensor(name, list(shape), dt, kind="Internal").ap()

    x_in = din("x", [T, D])
    W = {
        "w_in_ab": din("w_in_ab", [2, D, 6144]), "w_out_ab": din("w_out_ab", [2, 1536, D]),
        "w_in_c": din("w_in_c", [2, D, 6144]), "w_out_c": din("w_out_c", [2, D, D]),
        "w_gate": din("w_gate", [4, D, DFF]), "w_up": din("w_up", [4, D, DFF]),
        "w_down": din("w_down", [4, DFF, D]),
    }
    norm_mix = din("norm_mix", [4, D])
    norm_ffn = din("norm_ffn", [4, D])
    norm_final = din("norm_final", [1, D])
    sink_b = din("sink_b", [2, 8])
    biasab_in = din("biasab", [128, 20 * 384])
    biasc_in = din("biasc", [2, 128, 16 * 9 * 128])
    midab_in = din("midab", [1, 128])
    midc_in = din("midc", [2, 28 * 128])
    consts_in = din("consts", [128, 3 * 128])
    kyind_in = din("kyind", [2, 128])
    y_out = nc.dram_tensor("y", [T, D], F32, kind="ExternalOutput").ap()

    Wb = {k: dscr("b_" + k, v.shape, BF16) for k, v in W.items()}
    QKs = dscr("qks", [34, 128, T], BF16)
    Vs = dscr("vs", [T, 2048], BF16)
    XA = dscr("xa", [T, D], F32)
    XB = dscr("xb", [T, D], F32)

    P = Prog()
    es = contextlib.ExitStack()
    arena_t = es.enter_context(nc.sbuf_tensor("arena", [128, ARENA_WORDS], F32))
    ar = Arena(arena_t)
    A0 = es.enter_context(nc.psum_tensor("pa0", [128, 1024], F32))
    A1 = es.enter_context(nc.psum_tensor("pa1", [128, 1024], F32))
    B0 = es.enter_context(nc.psum_tensor("pb0", [128, 512], F32))
    B1 = es.enter_context(nc.psum_tensor("pb1", [128, 512], F32))
    TPa = es.enter_context(nc.psum_tensor("ptpa", [128, 1024], BF16))
    TPb = es.enter_context(nc.psum_tensor("ptpb", [128, 1024], BF16))
    bA0l, bA0h, bA1l, bA1h, bB0, bB1, bTP0, bTP1 = [Buf() for _ in range(8)]
    banks = [(A0[:, 0:512], bA0l), (A0[:, 512:1024], bA0h), (A1[:, 0:512], bA1l),
             (A1[:, 512:1024], bA1h), (B0[:, :], bB0), (B1[:, :], bB1)]
    bank_ctr = [0]

    def next_bank():
        b = banks[bank_ctr[0] % 6]
        bank_ctr[0] += 1
        return b

    TPv = [(TPa[:, :], bTP0), (TPb[:, :], bTP1)]

    cst = ar.get(128, [3, 128], BF16)
    ident, jrev, ones = cst[:, 0, :], cst[:, 1, :], cst[:, 2, :]
    b_cst = Buf()
    biasab = ar.get(128, [20, 384], BF16)
    b_biasab = Buf()
    midab = ar.get(1, [128], BF16)
    midc = ar.get(2, [28 * 128], BF16)
    kyind = ar.get(2, [128], BF16)
    b_mid = Buf()
    epsT = ar.get(128, [1], F32)
    b_eps = Buf()
    esb = ar.get(128, [1, 8], F32)
    b_esb = Buf()
    small = ar.get(128, [16], F32)
    b_small = [Buf() for _ in range(6)]
    PERSIST = ar.off

    P.dma("poolq", lambda e: e.dma_start(out=cst, in_=consts_in.rearrange("p (a b) -> p a b", a=3)), writes=[b_cst])
    P.dma("poolq", lambda e: e.dma_start(out=biasab, in_=biasab_in.rearrange("p (a b) -> p a b", a=20)), writes=[b_biasab])
    P.dma("poolq", lambda e: e.dma_start(out=midab, in_=midab_in), writes=[b_mid])
    P.dma("poolq", lambda e: e.dma_start(out=midc, in_=midc_in), writes=[b_mid])
    P.dma("poolq", lambda e: e.dma_start(out=kyind, in_=kyind_in), writes=[b_mid])
    P.op("dve", lambda e: e.memset(epsT, 1e-6), writes=[b_eps])

    wc = {}

    def precast(name, idx):
        src = W[name][idx]
        dst = Wb[name][idx]
        rows = src.shape[0]
        bl = []
        if name in ("w_in_ab", "w_in_c"):
            for g in range(12):
                b = Buf()
                P.dma("poolq", lambda e, g=g: e.dma_start(out=dst[:, g * 512:(g + 1) * 512], in_=src[:, g * 512:(g + 1) * 512]), writes=[b])
                bl.append(b)
            wc[(name, idx)] = bl
            return
        for r0 in range(0, rows, 256):
            r1 = min(rows, r0 + 256)
            b = Buf()
            P.dma("poolq", lambda e, r0=r0, r1=r1: e.dma_start(out=dst[r0:r1, :], in_=src[r0:r1, :]), writes=[b])
            bl.append(b)
        wc[(name, idx)] = bl

    for l in range(depth):
        j = l // 2
        if l % 2 == 0:
            precast("w_in_ab", j)
            precast("w_out_ab", j)
        else:
            precast("w_in_c", j)
            precast("w_out_c", j)
        precast("w_gate", l)
        precast("w_up", l)
        precast("w_down", l)

    norm_ctr = [0]

    def norm_tile(xin, xin_buf, gbc, b_g, nb, dstT, dst_buf, col0, out32=None):
        k = norm_ctr[0] % 2
        norm_ctr[0] += 1
        sq, b_sq, hb, b_hb = nb["sq"], nb["b_sq"], nb["hb"][k], nb["b_hb"][k]
        ss, sd, rs = small[:, k:k + 1], small[:, 2 + k:3 + k], small[:, 4 + k:5 + k]
        bss, bsd, brs = b_small[k], b_small[2 + k], b_small[4 + k]
        P.op("act", lambda e: e.activation(out=sq, in_=xin, func=AF.Square), reads=[xin_buf], writes=[b_sq])
        P.op("dve", lambda e: e.tensor_reduce(out=ss, in_=sq, axis=AX.X, op=ALU.add), reads=[b_sq], writes=[bss])
        P.op("act", lambda e: e.activation(out=sd, in_=ss, func=AF.Sqrt, bias=epsT, scale=1.0 / D),
             reads=[bss, b_eps], writes=[bsd])
        P.op("dve", lambda e: e.reciprocal(out=rs, in_=sd), reads=[bsd], writes=[brs])
        if out32 is not None:
            oap, ob = out32
            P.op("dve", lambda e: e.scalar_tensor_tensor(out=oap, in0=xin, scalar=rs, in1=gbc, op0=ALU.mult, op1=ALU.mult),
                 reads=[xin_buf, brs, b_g], writes=[ob])
            return
        P.op("dve", lambda e: e.scalar_tensor_tensor(out=hb, in0=xin, scalar=rs, in1=gbc, op0=ALU.mult, op1=ALU.mult),
             reads=[xin_buf, brs, b_g], writes=[b_hb])
        for q4 in range(2):
            tpv, btp = TPv[q4 % 2]
            for jj in range(8):
                kc = q4 * 8 + jj
                P.op("pe", lambda e, jj=jj, kc=kc, tpv=tpv: e.transpose(out=tpv[:, jj * 128:(jj + 1) * 128],
                                                                      in_=hb[:, kc * 128:(kc + 1) * 128], identity=ident),
                     reads=[b_hb, b_cst], writes=[btp])
            P.op("act", lambda e, q4=q4, tpv=tpv: e.activation(out=dstT[:, q4 * 8:q4 * 8 + 8, col0:col0 + 128],
                                                               in_=tpv.rearrange("p (k c) -> p k c", c=128), func=AF.Copy),
                 reads=[btp], writes=[dst_buf])

    def alloc_norm(nhb=2):
        hbs = [ar.get(128, [D], BF16) for _ in range(nhb)]
        bhb = [Buf() for _ in range(nhb)]
        nb = {"sq": ar.get(128, [D], F32), "b_sq": Buf(),
              "hb": [hbs[i % nhb] for i in range(2)], "b_hb": [bhb[i % nhb] for i in range(2)]}
        gbc = ar.get(128, [1, D], F32)
        return nb, gbc, Buf()

    def phase_proj(l, xsrc):
        ar.off = PERSIST
        j = l // 2
        ab = (l % 2 == 0)
        wname = "w_in_ab" if ab else "w_in_c"
        Wsrc = Wb[wname][j].rearrange("(k p) n -> p k n", p=128)
        wdeps = wc[(wname, j)]
        nb, gbc, b_g = alloc_norm()
        hT = ar.get(128, [16, 2048], BF16)
        hTb = [Buf() for _ in range(16)]
        wbuf = [ar.get(128, [16, 512], BF16) for _ in range(2)]
        b_w = [Buf(), Buf()]
        stg = [ar.get(128, [2048], BF16) for _ in range(2)]
        b_stg = [Buf(), Buf()]
        vst = [ar.get(128, [512], BF16) for _ in range(2)]
        b_vst = [Buf(), Buf()]
        xt = [ar.get(128, [D], F32) for _ in range(2)]
        b_xt = [Buf(), Buf()]
        P.dma("sp", lambda e: e.dma_start(out=gbc, in_=norm_mix[l:l + 1, :].partition_broadcast(128)), writes=[b_g])
        groups = []
        if ab:
            for g in range(3):
                groups.append([("fm", DIL[g], g * 8 + c, True, c * 128) for c in range(4)])
                groups.append([("fm", DIL[g], g * 8 + 4 + c, False, c * 128) for c in range(4)])
                groups.append([("tm", DIL[g], 512 * g, 512, 0)])
            groups.append([("fm", 1, 24 + c, True, c * 128) for c in range(4)])
            groups.append([("fm", 1, 28 + c, True, c * 128) for c in range(4)])
            groups.append([("fm", 1, 32, False, 0), ("fm", 1, 33, False, 128), ("tm", 1, 1536, 256, 256)])
        else:
            for g in range(4):
                groups.append([("fm", 1, g * 4 + c, True, c * 128) for c in range(4)])
            for g in range(4):
                groups.append([("fm", 1, 16 + g * 4 + c, False, c * 128) for c in range(4)])
            for g in range(4):
                groups.append([("tm", 1, 512 * g, 512, 0)])
        ctr = {"stg": 0, "vst": 0}

        def load_w(gi):
            P.dma("sp", lambda e: e.dma_start(out=wbuf[gi % 2], in_=Wsrc[:, :, gi * 512:(gi + 1) * 512]),
                  reads=[wdeps[gi]], writes=[b_w[gi % 2]])

        for tt in range(T // 2048):
            for s in range(16):
                k = s % 2
                r0 = tt * 2048 + s * 128
                P.dma("sp", lambda e, k=k, r0=r0: e.dma_start(out=xt[k], in_=xsrc[r0:r0 + 128, :]), writes=[b_xt[k]])
                norm_tile(xt[k], b_xt[k], gbc[:, 0, :], b_g, nb, hT, hTb[s], s * 128)
            load_w(0)
            for gi in range(12):
                if gi + 1 < 12:
                    load_w(gi + 1)
                wb_, bw_ = wbuf[gi % 2], b_w[gi % 2]
                for spec in groups[gi]:
                    if spec[0] == "fm":
                        _, d, row, scaled, c0 = spec
                        si = ctr["stg"] % 2
                        ctr["stg"] += 1
                        eng = "act" if si == 0 else "dve"
                        st_ = stg[si]
                        for tq in range(4):
                            bank, bb = next_bank()
                            for kc in range(16):
                                P.op("pe", lambda e, kc=kc, bank=bank, tq=tq, c0=c0, wb_=wb_: e.matmul(
                                    bank, lhsT=wb_[:, kc, c0:c0 + 128], rhs=hT[:, kc, tq * 512:(tq + 1) * 512],
                                    start=(kc == 0), stop=(kc == 15)),
                                    reads=[bw_] + hTb[tq * 4:tq * 4 + 4], writes=[bb])
                            m0, m1 = tq * 512 // d, (tq + 1) * 512 // d
                            if d == 1:
                                oap = st_[:, m0:m1]
                                iap = bank
                            else:
                                oap = st_.rearrange("p (r m) -> p r m", r=d)[:, :, m0:m1]
                                iap = bank.rearrange("p (m r) -> p r m", r=d)
                            sc = SCALE if scaled else 1.0
                            if eng == "act":
                                P.op("act", lambda e, oap=oap, iap=iap, sc=sc: e.activation(out=oap, in_=iap, func=AF.Copy, scale=sc),
                                     reads=[bb], writes=[b_stg[si]])
                            else:
                                P.op("dve", lambda e, oap=oap, iap=iap, sc=sc: e.tensor_scalar(out=oap, in0=iap, scalar1=sc, scalar2=None, op0=ALU.mult),
                                     reads=[bb], writes=[b_stg[si]])
                        M = T // d
                        ml = 2048 // d
                        if d == 1:
                            dst = QKs[row][:, tt * 2048:(tt + 1) * 2048]
                            src = st_
                        else:
                            dst = QKs[row].rearrange("p (r m) -> p r m", r=d)[:, :, tt * ml:(tt + 1) * ml]
                            src = st_.rearrange("p (r m) -> p r m", r=d)
                        P.dma("sp", lambda e, dst=dst, src=src: e.dma_start(out=dst, in_=src), reads=[b_stg[si]])
                    else:
                        _, d, vcol, N, c0 = spec
                        nbu = 16 // d
                        for r in range(d):
                            for mb in range(nbu):
                                vi = ctr["vst"] % 2
                                ctr["vst"] += 1
                                bank, bb = next_bank()
                                t0 = r + d * 128 * mb
                                for kc in range(16):
                                    if d == 1:
                                        lap = hT[:, kc, t0:t0 + 128]
                                    else:
                                        lap = hT[:, kc, t0:t0 + d * 127 + 1:d]
                                    P.op("pe", lambda e, kc=kc, bank=bank, lap=lap, c0=c0, N=N, wb_=wb_: e.matmul(
                                        bank[:, 0:N], lhsT=lap, rhs=wb_[:, kc, c0:c0 + N], start=(kc == 0), stop=(kc == 15)),
                                        reads=[bw_] + hTb[mb * d:mb * d + d], writes=[bb])
                                if vi == 0:
                                    P.op("act", lambda e, bank=bank, N=N, vi=vi: e.activation(out=vst[vi][:, 0:N], in_=bank[:, 0:N], func=AF.Copy),
                                         reads=[bb], writes=[b_vst[vi]])
                                else:
                                    P.op("dve", lambda e, bank=bank, N=N, vi=vi: e.tensor_copy(out=vst[vi][:, 0:N], in_=bank[:, 0:N]),
                                         reads=[bb], writes=[b_vst[vi]])
                                p0 = r * (T // d) + tt * (2048 // d) + mb * 128
                                P.dma("sp", lambda e, p0=p0, vcol=vcol, N=N, vi=vi: e.dma_start(
                                    out=Vs[p0:p0 + 128, vcol:vcol + N], in_=vst[vi][:, 0:N]), reads=[b_vst[vi]])
        P.barrier()

    class AttnBufs:
        pass

    def alloc_attn(nkc):
        a = AttnBufs()
        a.Qb = [ar.get(128, [2048], BF16) for _ in range(2)]
        a.Kb = [ar.get(128, [3072], BF16) for _ in range(2)]
        a.Vb = [ar.get(128, [24, 128], BF16) for _ in range(2)]
        a.bQ = [Buf(), Buf()]
        a.bK = [Buf(), Buf()]
        a.bV = [Buf(), Buf()]
        a.PT = [ar.get(128, [1024], BF16) for _ in range(2)]
        a.bPT = [Buf(), Buf()]
        a.dtmp = [ar.get(128, [128], F32) for _ in range(2)]
        a.bdt = [Buf(), Buf()]
        a.wo = [ar.get(128, [nkc, 512], BF16) for _ in range(2)]
        a.bwo = [Buf(), Buf()]
        a.xc = [ar.get(128, [512], F32) for _ in range(3)]
        a.bxc = [Buf() for _ in range(3)]
        a.xo = [ar.get(128, [512], F32) for _ in range(2)]
        a.bxo = [Buf(), Buf()]
        a.S = [(A0, [bA0l, bA0h]), (A1, [bA1l, bA1h])]
        a.ND = [(B0, bB0), (B1, bB1)]
        a.qb = 0
        a.pending = None
        return a

    def qblock(a, parts, evac):
        i = a.qb % 2
        a.qb += 1
        S, bS = a.S[i]
        PT, bPT = a.PT[i], a.bPT[i]
        sc = 0
        for p in parts:
            n = p["n"]
            p["sc"] = sc
            so = S[:, sc:sc + n]
            P.op("pe", lambda e, so=so, p=p: e.matmul(so, lhsT=p["K"], rhs=p["Q"], start=True, stop=False),
                 reads=[p["Kb"], p["Qb"]], writes=bS)
            P.op("pe", lambda e, so=so, p=p: e.matmul(so, lhsT=jrev, rhs=p["bias"], start=False, stop=(p["mask"] is None)),
                 reads=[b_cst, p["bbuf"]], writes=bS)
            if p["mask"] is not None:
                ml, mr, mbuf = p["mask"]
                P.op("pe", lambda e, so=so, ml=ml, mr=mr: e.matmul(so, lhsT=ml, rhs=mr, start=False, stop=True),
                     reads=[b_cst, mbuf], writes=bS)
            sc += n
        tot = sc
        P.op("act", lambda e, S=S, PT=PT, tot=tot: e.activation(out=PT[:, 0:tot], in_=S[:, 0:tot], func=AF.Exp),
             reads=bS, writes=[bPT])
        cur = (i, parts, evac)
        prev = a.pending
        a.pending = cur
        if prev is not None:
            finish(a, prev)

    def finish(a, item):
        i, parts, evac = item
        PT, bPT = a.PT[i], a.bPT[i]
        nd, bnd = a.ND[i]
        last = len(parts) - 1
        for pi, p in enumerate(parts):
            n, sc, oc0 = p["n"], p["sc"], p["oc0"]
            P.op("pe", lambda e, p=p, n=n, sc=sc, oc0=oc0, pi=pi, nd=nd, PT=PT: e.matmul(
                nd[:, oc0:oc0 + n], lhsT=p["V"], rhs=PT[:, sc:sc + n], start=(pi == 0), stop=(pi == last)),
                reads=[p["Vbuf"], bPT], writes=[bnd])
        for pi, p in enumerate(parts):
            n, sc, oc0 = p["n"], p["sc"], p["oc0"]
            P.op("pe", lambda e, n=n, sc=sc, oc0=oc0, pi=pi, nd=nd, PT=PT: e.matmul(
                nd[:, 128 + oc0:128 + oc0 + n], lhsT=ones, rhs=PT[:, sc:sc + n], start=(pi == 0), stop=(pi == last)),
                reads=[b_cst, bPT], writes=[bnd])
        evac(nd, bnd, i)

    def flush(a):
        if a.pending is not None:
            finish(a, a.pending)
            a.pending = None

    def out_proj(a, wname, j, nkc, mixedT, b_mx, xsrc, tok0, ntok):
        Wsrc = Wb[wname][j].rearrange("(k p) n -> p k n", p=128)
        wdeps = wc[(wname, j)]
        ci = 0

        def load_wo(n):
            P.dma("sp", lambda e, n=n: e.dma_start(out=a.wo[n % 2], in_=Wsrc[:, :, n * 512:(n + 1) * 512]),
                  reads=wdeps, writes=[a.bwo[n % 2]])
        load_wo(0)
        for n in range(4):
            if n + 1 < 4:
                load_wo(n + 1)
            wo, bwo = a.wo[n % 2], a.bwo[n % 2]
            for sub in range(ntok // 128):
                r0 = tok0 + sub * 128
                xi, oi = ci % 3, ci % 2
                ci += 1
                P.dma("sp", lambda e, r0=r0, n=n, xi=xi: e.dma_start(out=a.xc[xi], in_=xsrc[r0:r0 + 128, n * 512:(n + 1) * 512]),
                      writes=[a.bxc[xi]])
                bank, bb = next_bank()
                for kc in range(nkc):
                    P.op("pe", lambda e, kc=kc, bank=bank, sub=sub, wo=wo: e.matmul(
                        bank, lhsT=mixedT[:, kc, sub * 128:(sub + 1) * 128], rhs=wo[:, kc, :], start=(kc == 0), stop=(kc == nkc - 1)),
                        reads=[b_mx, bwo], writes=[bb])
                P.op("dve", lambda e, bank=bank, xi=xi, oi=oi: e.tensor_tensor(out=a.xo[oi], in0=bank, in1=a.xc[xi], op=ALU.add),
                     reads=[bb, a.bxc[xi]], writes=[a.bxo[oi]])
                P.dma("sp", lambda e, r0=r0, n=n, oi=oi: e.dma_start(out=XB[r0:r0 + 128, n * 512:(n + 1) * 512], in_=a.xo[oi]),
                      reads=[a.bxo[oi]])

    def phase_attn_ab(l, xsrc):
        ar.off = PERSIST
        j = l // 2
        a = alloc_attn(12)
        mixedT = ar.get(128, [12, 2048], BF16)
        b_mx = Buf()
        accn = ar.get(128, [2048], F32)
        accd = ar.get(128, [2048], F32)
        b_acc = Buf()
        P.dma("sp", lambda e: e.dma_start(out=esb, in_=sink_b[j:j + 1, :].partition_broadcast(128)), writes=[b_esb])
        P.op("act", lambda e: e.activation(out=esb, in_=esb, func=AF.Exp), reads=[b_esb], writes=[b_esb])
        qi = [0]
        kvi = [0]
        for u in range(T // 2048):
            for s in range(4):
                for g in range(3):
                    d = DIL[g]
                    nbu = 16 // d
                    Mb = NB // d
                    midb = Mb // 2
                    RB = min(d, 4)
                    lo = max(u * nbu - 1, 0)
                    hi = min((u + 1) * nbu + 1, Mb)
                    nk = hi - lo
                    h = g * 4 + s
                    for rb in range(d // RB):
                        q_i = qi[0] % 2
                        qi[0] += 1
                        k_i = kvi[0] % 2
                        kvi[0] += 1
                        Qv = a.Qb[q_i][:, 0:RB * nbu * 128].rearrange("p (r m) -> p r m", r=RB)
                        Kv = a.Kb[k_i][:, 0:RB * nk * 128].rearrange("p (r m) -> p r m", r=RB)
                        Vv = a.Vb[k_i][:, 0:RB * nk, :].rearrange("p (r b) c -> p r b c", r=RB)
                        qsrc = QKs[g * 8 + s].rearrange("p (r m) -> p r m", r=d)[:, rb * RB:(rb + 1) * RB, u * nbu * 128:(u + 1) * nbu * 128]
                        ksrc = QKs[g * 8 + 4 + s].rearrange("p (r m) -> p r m", r=d)[:, rb * RB:(rb + 1) * RB, lo * 128:hi * 128]
                        P.dma("sp", lambda e, Qv=Qv, qsrc=qsrc: e.dma_start(out=Qv, in_=qsrc), writes=[a.bQ[q_i]])
                        P.dma("sp", lambda e, Kv=Kv, ksrc=ksrc: e.dma_start(out=Kv, in_=ksrc), writes=[a.bK[k_i]])
                        for rr in range(RB):
                            r = rb * RB + rr
                            vsrc = Vs[(r * Mb + lo) * 128:(r * Mb + hi) * 128, 512 * g + s * 128:512 * g + s * 128 + 128].rearrange("(b p) c -> p b c", p=128)
                            P.dma("sp", lambda e, rr=rr, Vv=Vv, vsrc=vsrc: e.dma_start(out=Vv[:, rr, :, :], in_=vsrc), writes=[a.bV[k_i]])
                        for rr in range(RB):
                            r = rb * RB + rr
                            for mbl in range(nbu):
                                mb = u * nbu + mbl
                                parts = []

                                def mk(kblk, qc0, n, bc0, mask):
                                    return dict(K=Kv[:, rr, (kblk - lo) * 128:(kblk - lo + 1) * 128], Kb=a.bK[k_i],
                                                Q=Qv[:, rr, mbl * 128 + qc0:mbl * 128 + qc0 + n], Qb=a.bQ[q_i], n=n,
                                                bias=biasab[:, h, bc0:bc0 + n], bbuf=b_biasab,
                                                mask=((ones[0:1, :], midab[0:1, 0:n], b_mid) if mask else None),
                                                V=Vv[:, rr, kblk - lo, :], Vbuf=a.bV[k_i], oc0=qc0)
                                parts.append(mk(mb, 0, 128, 128, False))
                                if mb > 0:
                                    parts.append(mk(mb - 1, 0, 64, 0, mb == midb))
                                if mb < Mb - 1:
                                    parts.append(mk(mb + 1, 64, 64, 320, mb == midb - 1))
                                c0 = r + d * 128 * mbl
                                if d == 1:
                                    cn = accn[:, c0:c0 + 128]
                                    cd = accd[:, c0:c0 + 128]
                                else:
                                    cn = accn[:, c0:c0 + d * 127 + 1:d]
                                    cd = accd[:, c0:c0 + d * 127 + 1:d]

                                def evac(nd, bnd, i, g=g, cn=cn, cd=cd):
                                    if g == 0:
                                        P.op("dve", lambda e: e.tensor_copy(out=cn, in_=nd[:, 0:128]), reads=[bnd], writes=[b_acc])
                                        P.op("dve", lambda e: e.tensor_copy(out=cd, in_=nd[:, 128:256]), reads=[bnd], writes=[b_acc])
                                    else:
                                        P.op("dve", lambda e: e.tensor_tensor(out=cn, in0=cn, in1=nd[:, 0:128], op=ALU.add), reads=[bnd], writes=[b_acc])
                                        P.op("dve", lambda e: e.tensor_tensor(out=cd, in0=cd, in1=nd[:, 128:256], op=ALU.add), reads=[bnd], writes=[b_acc])
                                qblock(a, parts, evac)
                flush(a)
                P.op("act", lambda e: e.activation(out=accd, in_=accd, func=AF.Ln), writes=[b_acc])
                P.op("act", lambda e: e.activation(out=accd, in_=accd, func=AF.Exp, scale=-1.0), writes=[b_acc])
                P.op("dve", lambda e, s=s: e.tensor_tensor(out=mixedT[:, s, :], in0=accn, in1=accd, op=ALU.mult), reads=[b_acc], writes=[b_mx])
            lo = max(16 * u - 1, 0)
            hi = min(16 * u + 17, NB)
            nk = hi - lo
            for kvh in range(2):
                k_i = kvi[0] % 2
                kvi[0] += 1
                Kf = a.Kb[k_i]
                Vf = a.Vb[k_i]
                P.dma("sp", lambda e, Kf=Kf, kvh=kvh, lo=lo, hi=hi, nk=nk: e.dma_start(out=Kf[:, 0:nk * 128], in_=QKs[32 + kvh][:, lo * 128:hi * 128]),
                      writes=[a.bK[k_i]])
                P.dma("sp", lambda e, Vf=Vf, kvh=kvh, lo=lo, hi=hi, nk=nk: e.dma_start(
                    out=Vf[:, 0:nk, :], in_=Vs[lo * 128:hi * 128, 1536 + kvh * 128:1536 + kvh * 128 + 128].rearrange("(b p) c -> p b c", p=128)),
                    writes=[a.bV[k_i]])
                for hq in range(4):
                    h = kvh * 4 + hq
                    q_i = qi[0] % 2
                    qi[0] += 1
                    Qf = a.Qb[q_i]
                    P.dma("sp", lambda e, Qf=Qf, h=h, u=u: e.dma_start(out=Qf, in_=QKs[24 + h][:, u * 2048:(u + 1) * 2048]), writes=[a.bQ[q_i]])
                    for mbl in range(16):
                        mb = 16 * u + mbl

                        def mk(kblk, bc0, mask):
                            return dict(K=Kf[:, (kblk - lo) * 128:(kblk - lo + 1) * 128], Kb=a.bK[k_i],
                                        Q=Qf[:, mbl * 128:(mbl + 1) * 128], Qb=a.bQ[q_i], n=128,
                                        bias=biasab[:, 12 + h, bc0:bc0 + 128], bbuf=b_biasab,
                                        mask=((ones[0:1, :], midab[0:1, :], b_mid) if mask else None),
                                        V=Vf[:, kblk - lo, :], Vbuf=a.bV[k_i], oc0=0)
                        parts = [mk(mb, 128, False)]
                        if mb > 0:
                            parts.append(mk(mb - 1, 0, mb == HALF))
                        if mb < NB - 1:
                            parts.append(mk(mb + 1, 256, mb == HALF - 1))

                        def evac(nd, bnd, i, h=h, mbl=mbl):
                            dt_, bdt = a.dtmp[i], a.bdt[i]
                            P.op("act", lambda e: e.activation(out=dt_, in_=nd[:, 128:256], func=AF.Ln, bias=esb[:, 0, h:h + 1]),
                                 reads=[bnd, b_esb], writes=[bdt])
                            P.op("act", lambda e: e.activation(out=dt_, in_=dt_, func=AF.Exp, scale=-1.0), writes=[bdt])
                            P.op("dve", lambda e: e.tensor_tensor(out=mixedT[:, 4 + h, mbl * 128:(mbl + 1) * 128], in0=nd[:, 0:128], in1=dt_, op=ALU.mult),
                                 reads=[bnd, bdt], writes=[b_mx])
                        qblock(a, parts, evac)
            flush(a)
            out_proj(a, "w_out_ab", j, 12, mixedT, b_mx, xsrc, u * 2048, 2048)
        P.barrier()

    def phase_attn_c(l, xsrc):
        ar.off = PERSIST
        j = l // 2
        a = alloc_attn(16)
        mixedT = ar.get(128, [16, 1024], BF16)
        b_mx = Buf()
        bc = [ar.get(128, [9, 128], BF16) for _ in range(2)]
        b_bc = [Buf(), Buf()]
        bsrc = biasc_in[j].rearrange("p (h t c) -> p h t c", h=16, t=9)
        it = 0
        for u in range(T // 1024):
            lo = max(8 * u - 3, 0)
            hi = min(8 * u + 11, NB)
            nk = hi - lo
            for h in range(16):
                i2 = it % 2
                it += 1
                Qf, Kf, Vf = a.Qb[i2], a.Kb[i2], a.Vb[i2]
                P.dma("poolq", lambda e, i2=i2, h=h: e.dma_start(out=bc[i2], in_=bsrc[:, h, :, :]), writes=[b_bc[i2]])
                P.dma("sp", lambda e, Qf=Qf, h=h, u=u: e.dma_start(out=Qf[:, 0:1024], in_=QKs[h][:, u * 1024:(u + 1) * 1024]), writes=[a.bQ[i2]])
                P.dma("sp", lambda e, Kf=Kf, h=h, lo=lo, hi=hi, nk=nk: e.dma_start(out=Kf[:, 0:nk * 128], in_=QKs[16 + h][:, lo * 128:hi * 128]), writes=[a.bK[i2]])
                P.dma("sp", lambda e, Vf=Vf, h=h, lo=lo, hi=hi, nk=nk: e.dma_start(
                    out=Vf[:, 0:nk, :], in_=Vs[lo * 128:hi * 128, h * 128:(h + 1) * 128].rearrange("(b p) c -> p b c", p=128)), writes=[a.bV[i2]])
                for bl in range(8):
                    b = 8 * u + bl
                    if HALF - 2 <= b <= HALF + 1:
                        dl = [(dd, dd + 3, (b - (HALF - 2)) * 7 + dd + 3) for dd in range(-3, 4)]
                    elif b == 0:
                        dl = [(dd, dd + 3, None) for dd in range(0, 4)]
                    elif b == 1:
                        dl = [(dd, dd + 3, None) for dd in range(-1, 3)]
                    elif b == NB - 2:
                        dl = [(dd, dd + 3, None) for dd in range(-2, 2)]
                    elif b == NB - 1:
                        dl = [(dd, dd + 3, None) for dd in range(-3, 1)]
                    else:
                        dl = [(-2, 7, None), (-1, 2, None), (0, 3, None), (1, 4, None), (2, 8, None)]
                    parts = []
                    for dd, tile, mi in dl:
                        kb_ = b + dd
                        if kb_ < 0 or kb_ >= NB:
                            continue
                        parts.append(dict(K=Kf[:, (kb_ - lo) * 128:(kb_ - lo + 1) * 128], Kb=a.bK[i2],
                                          Q=Qf[:, bl * 128:(bl + 1) * 128], Qb=a.bQ[i2], n=128,
                                          bias=bc[i2][:, tile, :], bbuf=b_bc[i2],
                                          mask=((kyind[0:2, :], midc[0:2, mi * 128:(mi + 1) * 128], b_mid) if mi is not None else None),
                                          V=Vf[:, kb_ - lo, :], Vbuf=a.bV[i2], oc0=0))

                    def evac(nd, bnd, i, h=h, bl=bl):
                        dt_, bdt = a.dtmp[i], a.bdt[i]
                        P.op("act", lambda e: e.activation(out=dt_, in_=nd[:, 128:256], func=AF.Ln), reads=[bnd], writes=[bdt])
                        P.op("act", lambda e: e.activation(out=dt_, in_=dt_, func=AF.Exp, scale=-1.0), writes=[bdt])
                        P.op("dve", lambda e: e.tensor_tensor(out=mixedT[:, h, bl * 128:(bl + 1) * 128], in0=nd[:, 0:128], in1=dt_, op=ALU.mult),
                             reads=[bnd, bdt], writes=[b_mx])
                    qblock(a, parts, evac)
            flush(a)
            out_proj(a, "w_out_c", j, 16, mixedT, b_mx, xsrc, u * 1024, 1024)
        P.barrier()

    def phase_ffn(l, xdst, final):
        ar.off = PERSIST
        nb, gbc, b_g = alloc_norm(1)
        xq = ar.get(128, [8, D], F32)
        xqb = [Buf() for _ in range(8)]
        hT = ar.get(128, [16, 1024], BF16)
        hTb = [Buf() for _ in range(8)]
        wg = [ar.get(128, [16, 256], BF16) for _ in range(2)]
        wu = [ar.get(128, [16, 256], BF16) for _ in range(2)]
        wd = [ar.get(128, [2, D], BF16) for _ in range(3)]
        b_wg, b_wu, b_wd = [Buf(), Buf()], [Buf(), Buf()], [Buf(), Buf(), Buf()]
        gT = [ar.get(128, [2, 1024], BF16) for _ in range(2)]
        b_gT = [Buf(), Buf()]
        sg = [ar.get(128, [512], F32) for _ in range(2)]
        b_sg = [Buf(), Buf()]
        Wg = Wb["w_gate"][l].rearrange("(k p) n -> p k n", p=128)
        Wu = Wb["w_up"][l].rearrange("(k p) n -> p k n", p=128)
        Wd = Wb["w_down"][l].rearrange("(c p) n -> p c n", p=128)
        dg, du, dd_ = wc[("w_gate", l)], wc[("w_up", l)], wc[("w_down", l)]
        sgi = [0]
        NFG = DFF // 256

        def load_f(fg):
            k = fg % 2
            P.dma("sp", lambda e: e.dma_start(out=wg[k], in_=Wg[:, :, fg * 256:(fg + 1) * 256]), reads=dg, writes=[b_wg[k]])
            P.dma("sp", lambda e: e.dma_start(out=wu[k], in_=Wu[:, :, fg * 256:(fg + 1) * 256]), reads=du, writes=[b_wu[k]])
            k3 = fg % 3
            P.dma("sp", lambda e: e.dma_start(out=wd[k3], in_=Wd[:, fg * 2:fg * 2 + 2, :]), reads=dd_, writes=[b_wd[k3]])

        gu = [0]
        dj = [0]

        def down_job(fg, sub, n):
            k, k3 = fg % 2, fg % 3
            bank, bb = banks[4 + dj[0] % 2]
            dj[0] += 1
            for c in range(2):
                P.op("pe", lambda e, c=c: e.matmul(
                    bank, lhsT=gT[k][:, c, sub * 128:(sub + 1) * 128], rhs=wd[k3][:, c, n * 512:(n + 1) * 512], start=(c == 0), stop=(c == 1)),
                    reads=[b_gT[k], b_wd[k3]], writes=[bb])
            P.op("dve", lambda e: e.tensor_tensor(
                out=xq[:, sub, n * 512:(n + 1) * 512], in0=xq[:, sub, n * 512:(n + 1) * 512], in1=bank, op=ALU.add),
                reads=[bb], writes=[xqb[sub]])

        def down(fg):
            for sub in range(8):
                for n in range(4):
                    down_job(fg, sub, n)

        for tt in range(T // 1024):
            for s in range(8):
                r0 = tt * 1024 + s * 128
                P.dma("sp", lambda e, s=s, r0=r0: e.dma_start(out=xq[:, s, :], in_=XB[r0:r0 + 128, :]), writes=[xqb[s]])
            P.dma("sp", lambda e: e.dma_start(out=gbc, in_=norm_ffn[l:l + 1, :].partition_broadcast(128)), writes=[b_g])
            load_f(0)
            for s in range(8):
                norm_tile(xq[:, s, :], xqb[s], gbc[:, 0, :], b_g, nb, hT, hTb[s], s * 128)
            if final:
                P.dma("sp", lambda e: e.dma_start(out=gbc, in_=norm_final[0:1, :].partition_broadcast(128)), writes=[b_g])
            for fg in range(NFG):
                if fg + 1 < NFG:
                    load_f(fg + 1)
                k = fg % 2
                jobs = [(fg - 1, sub, n) for sub in range(8) for n in range(4)] if fg >= 1 else []
                for c in range(2):
                    for tq in range(2):
                        (G, bG), (U, bU) = (banks[0], banks[1]) if gu[0] % 2 == 0 else (banks[2], banks[3])
                        gu[0] += 1
                        for kc in range(16):
                            P.op("pe", lambda e, kc=kc, G=G, c=c, tq=tq, k=k: e.matmul(
                                G, lhsT=wg[k][:, kc, c * 128:(c + 1) * 128], rhs=hT[:, kc, tq * 512:(tq + 1) * 512], start=(kc == 0), stop=(kc == 15)),
                                reads=[b_wg[k]] + hTb[tq * 4:tq * 4 + 4], writes=[bG])
                            if kc % 8 == 7:
                                for _ in range(2):
                                    if jobs:
                                        down_job(*jobs.pop(0))
                        for kc in range(16):
                            P.op("pe", lambda e, kc=kc, U=U, c=c, tq=tq, k=k: e.matmul(
                                U, lhsT=wu[k][:, kc, c * 128:(c + 1) * 128], rhs=hT[:, kc, tq * 512:(tq + 1) * 512], start=(kc == 0), stop=(kc == 15)),
                                reads=[b_wu[k]] + hTb[tq * 4:tq * 4 + 4], writes=[bU])
                            if kc % 8 == 7:
                                for _ in range(2):
                                    if jobs:
                                        down_job(*jobs.pop(0))
                        si = sgi[0] % 2
                        sgi[0] += 1
                        P.op("act", lambda e, G=G, si=si: e.activation(out=sg[si], in_=G, func=AF.Silu), reads=[bG], writes=[b_sg[si]])
                        P.op("dve", lambda e, U=U, si=si, c=c, tq=tq, k=k: e.tensor_tensor(
                            out=gT[k][:, c, tq * 512:(tq + 1) * 512], in0=sg[si], in1=U, op=ALU.mult),
                            reads=[b_sg[si], bU], writes=[b_gT[k]])
                assert not jobs
            down(NFG - 1)
            for s in range(8):
                r0 = tt * 1024 + s * 128
                if final:
                    norm_tile(xq[:, s, :], xqb[s], gbc[:, 0, :], b_g, nb, None, None, 0, out32=(nb["sq"], nb["b_sq"]))
                    P.dma("sp", lambda e, r0=r0: e.dma_start(out=y_out[r0:r0 + 128, :], in_=nb["sq"]), reads=[nb["b_sq"]])
                else:
                    P.dma("sp", lambda e, s=s, r0=r0: e.dma_start(out=xdst[r0:r0 + 128, :], in_=xq[:, s, :]), reads=[xqb[s]])
        P.barrier()

    P.barrier(queues=())
    xcur = x_in
    def copy_out(src):
        for r0 in range(0, T, 1024):
            P.dma("sp", lambda e, r0=r0: e.dma_start(out=y_out[r0:r0 + 1024, :], in_=src[r0:r0 + 1024, :]))

    for l in range(depth):
        if stop == ("pre", l):
            copy_out(x_in)
            break
        phase_proj(l, xcur)
        if stop == ("proj", l):
            copy_out(x_in)
            break
        if l % 2 == 0:
            phase_attn_ab(l, xcur)
        else:
            phase_attn_c(l, xcur)
        if stop == ("mix", l):
            copy_out(XB)
            break
        phase_ffn(l, XA, l == depth - 1 and stop is None)
        xcur = XA
        if stop == ("ffn", l):
            copy_out(XA)
            break
    P.barrier()
    P.emit(nc)
    es.close()
    return nc


def _t5_bucket(rel):
    nb = 16
    max_exact = 8
    ret = (rel > 0).astype(np.int32) * nb
    n = np.abs(rel)
    large = max_exact + (np.log(np.maximum(n, 1) / max_exact) / np.log(1024 / max_exact) * (nb - max_exact)).astype(np.int32)
    large = np.minimum(large, nb - 1)
    return (ret + np.where(n < max_exact, n, large)).astype(np.int32)


def _host_tables(t5_table, rpb_c):
    t5 = np.asarray(t5_table, np.float32)
    p = np.arange(128)[:, None, None]
    oi = np.arange(3)[None, :, None]
    qq = np.arange(128)[None, None, :]
    rel = 128 * (oi - 1) + (127 - p) - qq
    biasab = np.empty((128, 20, 3, 128), np.float32)
    for h in range(20):
        d = DIL[h // 4] if h < 12 else 1
        hw = 64 if h < 12 else 128
        vals = t5[_t5_bucket(rel * d), h]
        biasab[:, h] = np.where(np.abs(rel) <= hw, vals, np.float32(NEG))
    rpb = np.asarray(rpb_c, np.float32)
    kk = 127 - np.arange(128)
    ky, kx = kk // 64, kk % 64
    c = np.arange(128)
    qy, qx = c // 64, c % 64
    cs = np.clip(qx - 8, 0, 48)
    colv = (kx[:, None] >= cs[None, :]) & (kx[:, None] < cs[None, :] + 16)
    dc = np.clip(kx[:, None] - qx[None, :] + 15, 0, 30)
    biasc = np.empty((2, 128, 16, 9, 128), np.float32)
    for ti in range(9):
        dd = ti - 3 if ti < 7 else (-2 if ti == 7 else 2)
        dr = 2 * dd + ky[:, None] - qy[None, :]
        valid = colv.copy()
        if ti >= 7:
            valid &= (dr >= -4) & (dr <= 3)
        dri = np.clip(dr + 7, 0, 14)
        vals = rpb[:, :, dri, dc]
        biasc[:, :, :, ti, :] = np.where(valid[None, None], vals, np.float32(NEG)).transpose(0, 2, 1, 3)
    return biasab.reshape(128, 20 * 384), biasc.reshape(2, 128, 16 * 9 * 128)


def _mode_masks(T, two_seq):
    NB = T // 128
    HALF = NB // 2
    rows = T // 64
    midab = np.full((1, 128), NEG if two_seq else 0.0, np.float32)

    def key_rows(r):
        if two_seq:
            hr = rows // 2
            base = 0 if r < hr else hr
            rs = base + min(max((r - base) - 4, 0), hr - 8)
        else:
            rs = min(max(r - 4, 0), rows - 8)
        return rs, rs + 8
    midc = np.zeros((2, 4, 7, 128), np.float32)
    for qi in range(4):
        b = HALF - 2 + qi
        for di in range(7):
            dd = di - 3
            for kyp in range(2):
                yk = 2 * (b + dd) + kyp
                for qy in range(2):
                    r = 2 * b + qy
                    lo, hi = key_rows(r)
                    if not (lo <= yk < hi):
                        midc[kyp, qi, di, qy * 64:(qy + 1) * 64] = NEG
    return midab, midc.reshape(2, 28 * 128)


def _consts():
    c = np.zeros((128, 3, 128), np.float32)
    c[:, 0] = np.eye(128)
    c[:, 1] = np.eye(128)[::-1]
    c[:, 2] = 1.0
    ky = np.zeros((2, 128), np.float32)
    ky[0, :64] = 1.0
    ky[1, 64:] = 1.0
    return c.reshape(128, 384), ky


def make_in_maps(xs, modes, T, weights):
    biasab, biasc = _host_tables(weights["t5_table"], weights["rpb_c"])
    consts, kyind = _consts()
    shared = {k: np.ascontiguousarray(np.asarray(weights[k], np.float32)) for k in
              ("w_in_ab", "w_out_ab", "w_in_c", "w_out_c", "w_gate", "w_up", "w_down", "norm_mix", "norm_ffn", "sink_b")}
    shared["norm_final"] = np.ascontiguousarray(np.asarray(weights["norm_final"], np.float32).reshape(1, D))
    shared.update(biasab=biasab, biasc=biasc, consts=consts, kyind=kyind)
    mm = {m: _mode_masks(T, m) for m in set(modes)}
    maps = []
    for x, m in zip(xs, modes):
        d = dict(shared)
        d["x"] = np.ascontiguousarray(x, dtype=np.float32)
        d["midab"], d["midc"] = mm[m]
        maps.append(d)
    return maps


def kernel(x_prompt, x_sample, w_in_ab, w_out_ab, sink_b, w_in_c, w_out_c, rpb_c, t5_table,
           norm_mix, norm_ffn, w_gate, w_up, w_down, norm_final):
    T = 8192
    xp = np.asarray(x_prompt, np.float32)
    xsm = np.asarray(x_sample, np.float32)
    weights = dict(w_in_ab=w_in_ab, w_out_ab=w_out_ab, sink_b=sink_b, w_in_c=w_in_c, w_out_c=w_out_c,
                   rpb_c=rpb_c, t5_table=t5_table, norm_mix=norm_mix, norm_ffn=norm_ffn,
                   w_gate=w_gate, w_up=w_up, w_down=w_down, norm_final=norm_final)
    zero = np.zeros((T, D), np.float32)
    xs = [xsm[0], xsm[1], xsm[2], xsm[3],
          np.concatenate([xp[0], xp[1]], 0), np.concatenate([xp[2], xp[3]], 0), zero, zero]
    modes = [False, False, False, False, True, True, False, False]
    nc = build(T, 4)
    in_maps = make_in_maps(xs, modes, T, weights)
    res = run_bass_kernel_spmd(nc, in_maps, core_ids=list(range(8)))
    ys = [np.asarray(r["y"]) for r in res.results]
    y_sample = np.stack(ys[0:4], 0)
    y_prompt = np.stack([ys[4][:4096], ys[4][4096:], ys[5][:4096], ys[5][4096:]], 0)
    return (y_prompt, y_sample)
```
